# Optimizing a Trainium2 kernel written in Bass

```python
import math
import jax, jax.numpy as jnp
from jax import lax
import numpy as np

D_MODEL = 2048
BATCH = 4
SEQ = 4096
DEPTH = 2

N_HEADS = 16
HEAD_DIM = 128
N_KV = 4
GROUP = N_HEADS // N_KV
CMP_LEN = 32
CMP_STRIDE = 16
CMP_HIDDEN = 256
SEL_LEN = 64
N_SEL = 16
WINDOW = 512
Q_CHUNK = 32
ATTN_DIM = N_HEADS * HEAD_DIM
KV_DIM = N_KV * HEAD_DIM
N_NSA_BRANCH = 3
CONV_DIM = D_MODEL
CONV_WIDTH = 3
D_FF = 4 * D_MODEL
N_MERGE = 2
ROPE_THETA = 10000.0
EPS = 1e-6
SPLITS = [ATTN_DIM, 6 * KV_DIM, N_HEADS * N_NSA_BRANCH, 3 * CONV_DIM, N_MERGE * D_MODEL]
IN_COLS = sum(SPLITS)

kernel_name = "hybrid_nsa_shortconv_sqrelu"


def rmsnorm(x, g):
    xf = x.astype(jnp.float32)
    y = xf * lax.rsqrt(jnp.mean(xf * xf, axis=-1, keepdims=True) + EPS)
    return (y * g.astype(jnp.float32)).astype(x.dtype)


def rope(x, pos):
    half = x.shape[-1] // 2
    inv_freq = jnp.exp(-math.log(ROPE_THETA) * jnp.arange(half, dtype=jnp.float32) / half)
    ang = pos.astype(jnp.float32)[:, None] * inv_freq[None, :]
    cos, sin = jnp.cos(ang), jnp.sin(ang)
    xf = x.astype(jnp.float32)
    x1, x2 = xf[..., :half], xf[..., half:]
    out = jnp.concatenate([x1 * cos - x2 * sin, x2 * cos + x1 * sin], axis=-1)
    return out.astype(x.dtype)


def masked_softmax(s, mask):
    s = jnp.where(mask, s.astype(jnp.float32), -jnp.inf)
    m = jnp.max(s, axis=-1, keepdims=True)
    m = jnp.where(jnp.isfinite(m), m, 0.0)
    p = jnp.exp(s - m)
    d = jnp.sum(p, axis=-1, keepdims=True)
    return p / jnp.where(d > 0, d, 1.0)


def compress(k, pos_emb, w1, w2):
    b, g, s, dh = k.shape
    nc = (s - CMP_LEN) // CMP_STRIDE + 1
    idx = jnp.arange(nc)[:, None] * CMP_STRIDE + jnp.arange(CMP_LEN)[None, :]
    blocks = k[:, :, idx] + pos_emb
    flat = blocks.reshape(b, g, nc, CMP_LEN * dh)
    return jax.nn.silu(flat @ w1) @ w2


def cmp_to_sel_matrix(nc, ns):
    cs = np.arange(nc) * CMP_STRIDE
    ss = np.arange(ns) * SEL_LEN
    ov = np.minimum(cs[:, None] + CMP_LEN, ss[None, :] + SEL_LEN) - np.maximum(cs[:, None], ss[None, :])
    return jnp.asarray(np.clip(ov, 0, None) / CMP_LEN, dtype=jnp.float32)


def nsa_attention(q, kc, vc, ks, vs, kw, vw):
    b, g, r, s, dh = q.shape
    nc = kc.shape[2]
    ns = s // SEL_LEN
    n_sel = min(N_SEL, ns)
    nq = s // Q_CHUNK
    scale = dh ** -0.5
    cmp_end = jnp.arange(nc) * CMP_STRIDE + CMP_LEN - 1
    m_sel = cmp_to_sel_matrix(nc, ns)
    ks_blk = ks.reshape(b, g, ns, SEL_LEN, dh)
    vs_blk = vs.reshape(b, g, ns, SEL_LEN, dh)
    pad = jnp.zeros((b, g, WINDOW, dh), kw.dtype)
    kw_pad = jnp.concatenate([pad, kw], axis=2)
    vw_pad = jnp.concatenate([pad, vw], axis=2)
    bi = jnp.arange(b)[:, None, None, None]
    gi = jnp.arange(g)[None, :, None, None]
    blk_ids = jnp.arange(ns)
    tok_in_blk = jnp.arange(SEL_LEN)
    win_off = jnp.arange(WINDOW + Q_CHUNK)
    q_chunks = jnp.moveaxis(q.reshape(b, g, r, nq, Q_CHUNK, dh), 3, 0)

    def chunk(args):
        qc, c = args
        start = c * Q_CHUNK
        t = start + jnp.arange(Q_CHUNK)
        sc = jnp.einsum('bgrqd,bgnd->bgrqn', qc, kc) * scale
        p_cmp = masked_softmax(sc, cmp_end[None, :] <= t[:, None])
        o_cmp = jnp.einsum('bgrqn,bgnd->bgrqd', p_cmp.astype(vc.dtype), vc)
        imp = jnp.einsum('bgrqn,nm->bgqm', p_cmp, m_sel)
        tb = t // SEL_LEN
        valid = blk_ids[None, :] <= tb[:, None]
        forced = (blk_ids[None, :] == 0) | (blk_ids[None, :] == tb[:, None]) | (blk_ids[None, :] == tb[:, None] - 1)
        imp = jnp.where(valid, jnp.where(forced, jnp.inf, imp), -jnp.inf)
        _, idx = lax.top_k(imp, n_sel)
        kg = ks_blk[bi, gi, idx]
        vg = vs_blk[bi, gi, idx]
        ss_ = jnp.einsum('bgrqd,bgqnld->bgrqnl', qc, kg) * scale
        tok = idx[..., None] * SEL_LEN + tok_in_blk
        smask = (tok <= t[None, None, :, None, None])[:, :, None]
        shp = ss_.shape
        p_sel = masked_softmax(ss_.reshape(b, g, r, Q_CHUNK, -1),
                               smask.reshape(b, g, 1, Q_CHUNK, -1)).reshape(shp)
        o_sel = jnp.einsum('bgrqnl,bgqnld->bgrqd', p_sel.astype(vg.dtype), vg)
        kwc = lax.dynamic_slice_in_dim(kw_pad, start, WINDOW + Q_CHUNK, axis=2)
        vwc = lax.dynamic_slice_in_dim(vw_pad, start, WINDOW + Q_CHUNK, axis=2)
        kp = start - WINDOW + win_off
        wmask = (kp[None, :] <= t[:, None]) & (t[:, None] - kp[None, :] < WINDOW) & (kp[None, :] >= 0)
        sw = jnp.einsum('bgrqd,bgkd->bgrqk', qc, kwc) * scale
        p_win = masked_softmax(sw, wmask)
        o_win = jnp.einsum('bgrqk,bgkd->bgrqd', p_win.astype(vwc.dtype), vwc)
        return o_cmp, o_sel, o_win

    outs = lax.map(chunk, (q_chunks, jnp.arange(nq)))
    return [jnp.moveaxis(o, 0, 3).reshape(b, g, r, s, dh) for o in outs]


def short_conv(u, w):
    c = u.shape[-1]
    return lax.conv_general_dilated(u, w[:, None, :].astype(u.dtype), window_strides=(1,),
                                    padding=[(CONV_WIDTH - 1, 0)],
                                    dimension_numbers=('NWC', 'WIO', 'NWC'),
                                    feature_group_count=c)


def setup_inputs(seed: int = 0) -> dict:
    key = jax.random.key(seed)
    ks = jax.random.split(key, 20)
    f32 = jnp.float32
    nrm = lambda k, shape, fan: jax.random.normal(k, shape, f32) * (fan ** -0.5)
    gain = lambda k, shape: 1.0 + 0.02 * jax.random.normal(k, shape, f32)
    L = DEPTH
    return {
        "x": jax.random.normal(ks[0], (BATCH, SEQ, D_MODEL), f32),
        "norm1_g": gain(ks[1], (L, D_MODEL)),
        "w_in": nrm(ks[2], (L, D_MODEL, IN_COLS), D_MODEL),
        "cmp_pos_k": 0.1 * jax.random.normal(ks[3], (L, CMP_LEN, HEAD_DIM), f32),
        "cmp_w1_k": nrm(ks[4], (L, CMP_LEN * HEAD_DIM, CMP_HIDDEN), CMP_LEN * HEAD_DIM),
        "cmp_w2_k": nrm(ks[5], (L, CMP_HIDDEN, HEAD_DIM), CMP_HIDDEN),
        "cmp_pos_v": 0.1 * jax.random.normal(ks[6], (L, CMP_LEN, HEAD_DIM), f32),
        "cmp_w1_v": nrm(ks[7], (L, CMP_LEN * HEAD_DIM, CMP_HIDDEN), CMP_LEN * HEAD_DIM),
        "cmp_w2_v": nrm(ks[8], (L, CMP_HIDDEN, HEAD_DIM), CMP_HIDDEN),
        "conv_w": nrm(ks[9], (L, CONV_WIDTH, CONV_DIM), CONV_WIDTH),
        "w_attn_proj": nrm(ks[10], (L, ATTN_DIM, D_MODEL), ATTN_DIM),
        "w_conv_out": nrm(ks[11], (L, CONV_DIM, D_MODEL), CONV_DIM),
        "w_o": nrm(ks[12], (L, D_MODEL, D_MODEL), D_MODEL),
        "norm2_g": gain(ks[13], (L, D_MODEL)),
        "w_up": nrm(ks[14], (L, D_MODEL, D_FF), D_MODEL),
        "w_down": nrm(ks[15], (L, D_FF, D_MODEL), D_FF),
        "final_g": gain(ks[16], (D_MODEL,)),
    }


def reference(x, norm1_g, w_in, cmp_pos_k, cmp_w1_k, cmp_w2_k, cmp_pos_v, cmp_w1_v, cmp_w2_v,
              conv_w, w_attn_proj, w_conv_out, w_o, norm2_g, w_up, w_down, final_g):
    b, s, _ = x.shape
    pos = jnp.arange(s)
    nc = (s - CMP_LEN) // CMP_STRIDE + 1
    cmp_pos = jnp.arange(nc) * CMP_STRIDE + CMP_LEN - 1
    offsets = np.cumsum(SPLITS)[:-1].tolist()
    for i in range(DEPTH):
        h = rmsnorm(x, norm1_g[i])
        z = h @ w_in[i]
        q, kv, ng, cv, mg = jnp.split(z, offsets, axis=-1)
        q = rope(q.reshape(b, s, N_KV, GROUP, HEAD_DIM).transpose(0, 2, 3, 1, 4), pos)
        kv = kv.reshape(b, s, 6, N_KV, HEAD_DIM)
        k_cmp, v_cmp, k_sel, v_sel, k_win, v_win = [kv[:, :, j].transpose(0, 2, 1, 3) for j in range(6)]
        kc = rope(compress(k_cmp, cmp_pos_k[i], cmp_w1_k[i], cmp_w2_k[i]), cmp_pos)
        vc = compress(v_cmp, cmp_pos_v[i], cmp_w1_v[i], cmp_w2_v[i])
        o_cmp, o_sel, o_win = nsa_attention(q, kc, vc, rope(k_sel, pos), v_sel, rope(k_win, pos), v_win)
        to_bshd = lambda o: o.transpose(0, 3, 1, 2, 4).reshape(b, s, N_HEADS, HEAD_DIM)
        gb = jax.nn.sigmoid(ng.reshape(b, s, N_HEADS, N_NSA_BRANCH).astype(jnp.float32)).astype(x.dtype)
        o_attn = (gb[..., 0:1] * to_bshd(o_cmp) + gb[..., 1:2] * to_bshd(o_sel)
                  + gb[..., 2:3] * to_bshd(o_win)).reshape(b, s, ATTN_DIM)
        y_attn = o_attn @ w_attn_proj[i]
        x_in, gate_b, gate_c = jnp.split(cv, 3, axis=-1)
        y_conv = (gate_b * short_conv(gate_c * x_in, conv_w[i])) @ w_conv_out[i]
        gm = jax.nn.sigmoid(mg.astype(jnp.float32)).astype(x.dtype)
        g_attn, g_conv = jnp.split(gm, 2, axis=-1)
        x = x + (g_attn * y_attn + g_conv * y_conv) @ w_o[i]
        h2 = rmsnorm(x, norm2_g[i])
        x = x + jnp.square(jax.nn.relu(h2 @ w_up[i])) @ w_down[i]
    return rmsnorm(x, final_g)
```

```python
import contextlib
import math

import ml_dtypes
import numpy as np

import concourse.bass as bass
import concourse.mybir as mybir
from concourse.bass_utils import run_bass_kernel_spmd

F32 = mybir.dt.float32
BF = mybir.dt.bfloat16
ALU = mybir.AluOpType
AF = mybir.ActivationFunctionType

D = 2048
S = 4096
NT = 2048
DEPTH = 2
NH = 16
DFF = 8192
IN_COLS = 15408
NEG = -30000.0
PAIRS = [[0, 1], [2, 3], [4, 5], [6, 7]]
SCALE = 128 ** -0.5


def gk_row(r, slab):
    return (slab // 4) * 1024 + r * 512 + (slab % 4) * 128


def gv_row(r, t):
    return (t // 1024) * 2048 + r * 1024 + (t % 1024)


class Sem:
    def __init__(self, slot):
        self.slot = slot
        self.h = slot[0]
        self.base = slot[1]
        self.n = 0


SEM_POOL = []


class Phase:
    def __init__(self, nc, name):
        self.nc = nc
        self.name = name
        self.es = contextlib.ExitStack()
        self.q = {e: [] for e in ("pe", "act", "dve", "pool", "sp")}
        self.k = 0
        self.sems = []

    def sem(self, nm):
        sm = Sem(SEM_POOL[self.k])
        self.k += 1
        self.sems.append(sm)
        return sm

    def sb(self, nm, shape, dt):
        return self.es.enter_context(self.nc.sbuf_tensor(f"{self.name}_{nm}", shape, dt))

    def ps(self, nm, shape, dt=F32):
        return self.es.enter_context(self.nc.psum_tensor(f"{self.name}_{nm}", shape, dt))

    def op(self, eng, fn, waits=(), sig=None, inc=1):
        waits = [(s, v) for (s, v) in waits if v > 0]

        def thunk(e, fn=fn, waits=waits, sig=sig, inc=inc):
            for s, v in waits:
                e.wait_ge(s.h, s.base + v)
            ins = fn(e)
            if sig is not None:
                ins.then_inc(sig.h, inc)

        self.q[eng].append(thunk)
        if sig is not None:
            sig.n += inc
            return sig.n
        return 0

    def dma(self, eng, out, in_, waits=(), sig=None):
        return self.op(eng, lambda e, out=out, in_=in_: e.dma_start(out=out, in_=in_), waits, sig, 16)

    def wait(self, eng, waits):
        waits = [(s, v) for (s, v) in waits if v > 0]

        def thunk(e, waits=waits):
            for s, v in waits:
                e.wait_ge(s.h, s.base + v)

        self.q[eng].append(thunk)

    def run(self):
        q = self.q
        with self.nc.Block() as block:
            @block.tensor
            def _(e):
                for f in q["pe"]:
                    f(e)

            @block.scalar
            def _(e):
                for f in q["act"]:
                    f(e)

            @block.vector
            def _(e):
                for f in q["dve"]:
                    f(e)

            @block.gpsimd
            def _(e):
                for f in q["pool"]:
                    f(e)

            @block.sync
            def _(e):
                for f in q["sp"]:
                    f(e)
        for sm in self.sems:
            sm.slot[1] += sm.n
        self.es.close()


def phase_norm(nc, name, xsrc, gain_ap, ones_bf, dst_sb=None, dst_dram=None):
    ph = Phase(nc, name)
    xb = [ph.sb(f"xb{i}", [128, 16, 512], F32) for i in range(2)]
    sq = ph.sb("sq", [128, 16, 512], BF)
    g = ph.sb("g", [128, 16], F32)
    tmp = ph.sb("tmp", [128, 512], F32)
    tmp2 = ph.sb("tmp2", [128, 512], F32)
    rstd = ph.sb("rstd", [128, 512], F32)
    pss = ph.ps("ss", [128, 512])
    ob = ph.sb("ob", [128, 16, 512], F32) if dst_dram is not None else None
    s_ld = [ph.sem("ld"), ph.sem("ld")]
    s_g = ph.sem("g")
    s_act = ph.sem("act")
    s_pe = ph.sem("pe")
    s_dve = ph.sem("dve")
    s_st = ph.sem("st")
    ph.dma("sp", g[:], gain_ap, sig=s_g)
    dve_done = {}
    pe_done = {}
    dve_a = {}
    for tb in range(4):
        b = tb % 2
        w = [(s_dve, dve_done[tb - 2])] if tb >= 2 else []
        ld = ph.dma("sp", xb[b][:], xsrc[:, :, tb * 512:(tb + 1) * 512].rearrange("k p t -> p k t"), w, s_ld[b])
        for kc in range(16):
            w = []
            if kc == 0:
                w = [(s_ld[b], ld)]
                if tb >= 1:
                    w.append((s_pe, pe_done[tb - 1]))
            a_sq = ph.op("act", lambda e, kc=kc, b=b: e.activation(out=sq[:, kc, :], in_=xb[b][:, kc, :], func=AF.Square),
                         w, s_act if kc == 15 else None)
        for kc in range(16):
            w = []
            if kc == 0:
                w = [(s_act, a_sq)]
                if tb >= 1:
                    w.append((s_dve, dve_a[tb - 1]))
            pe_done[tb] = ph.op("pe", lambda e, kc=kc: e.matmul(pss[:], ones_bf[:], sq[:, kc, :], start=(kc == 0), stop=(kc == 15)),
                                w, s_pe if kc == 15 else None)
        dve_a[tb] = ph.op("dve", lambda e: e.tensor_scalar(out=tmp[:], in0=pss[:], scalar1=1.0 / D, scalar2=1e-6,
                                                           op0=ALU.mult, op1=ALU.add),
                          [(s_pe, pe_done[tb])], s_dve)
        a_sqrt = ph.op("act", lambda e: e.activation(out=tmp2[:], in_=tmp[:], func=AF.Sqrt), [(s_dve, dve_a[tb])], s_act)
        d_b = ph.op("dve", lambda e: e.reciprocal(out=rstd[:], in_=tmp2[:]), [(s_act, a_sqrt)], s_dve)
        for kc in range(16):
            w = []
            if kc == 0:
                w = [(s_dve, d_b), (s_g, 16)]
                if ob is not None and tb >= 1:
                    w.append((s_st, 16 * tb))
            if ob is None:
                o = dst_sb[:, kc, tb * 512:(tb + 1) * 512]
            else:
                o = ob[:, kc, :]
            dve_done[tb] = ph.op("dve", lambda e, kc=kc, b=b, o=o: e.scalar_tensor_tensor(
                out=o, in0=xb[b][:, kc, :], scalar=g[:, kc:kc + 1], in1=rstd[:], op0=ALU.mult, op1=ALU.mult),
                w, s_dve if kc == 15 else None)
        if ob is not None:
            ph.dma("sp", dst_dram[:, :, tb * 512:(tb + 1) * 512].rearrange("k p t -> p k t"), ob[:],
                   [(s_dve, dve_done[tb])], s_st)
    if ob is not None:
        ph.wait("sp", [(s_st, 64)])
    ph.run()


class Gemm:
    NPS = 4

    def __init__(self, ph, nk, wcols=512, pfx=""):
        self.ph = ph
        self.nk = nk
        self.wb = [ph.sb(f"{pfx}w{i}", [128, nk, wcols], BF) for i in range(2)]
        self.s_w = [ph.sem("w"), ph.sem("w")]
        self.s_pe = ph.sem("gpe")
        self.s_ep = ph.sem("gep")
        self.psb = [ph.ps(f"{pfx}g{i}", [128, 512]) for i in range(self.NPS)]
        self.T = 0
        self.J = 0
        self.job_pe_end = {}
        self.ep_of_tile = {}

    def load_w(self, wsrc):
        ph = self.ph
        J = self.J
        b = J % 2
        ncols = wsrc.shape[1]
        w = [(self.s_pe, self.job_pe_end[J - 2])] if J >= 2 else []
        v = ph.dma("pool", self.wb[b][:, :, 0:ncols], wsrc.rearrange("(k p) c -> p k c", p=128), w, self.s_w[b])
        self.J += 1
        return b, v

    def tile(self, mm_list, evac, first_waits=()):
        ph = self.ph
        T = self.T
        ps = self.psb[T % self.NPS]
        n = len(mm_list)
        M = mm_list[0][0].shape[-1]
        N = mm_list[0][1].shape[-1]
        pso = ps[0:M, 0:N]
        for k, (l, r) in enumerate(mm_list):
            w = []
            if k == 0:
                w = list(first_waits)
                if T >= self.NPS:
                    w.append((self.s_ep, self.ep_of_tile[T - self.NPS]))
            v = ph.op("pe", lambda e, l=l, r=r, k=k, pso=pso: e.matmul(pso, l, r, start=(k == 0), stop=(k == n - 1)),
                      w, self.s_pe if k == n - 1 else None)
        before = self.s_ep.n
        evac(pso, [(self.s_pe, v)])
        assert self.s_ep.n == before + 1
        self.ep_of_tile[T] = self.s_ep.n
        self.T += 1
        return T

    def end_job(self):
        self.job_pe_end[self.J - 1] = self.s_pe.n


class SlabOut:
    def __init__(self, ph, dt=BF, nbuf=2, name="stg"):
        self.ph = ph
        self.buf = [ph.sb(f"{name}{i}", [128, NT], dt) for i in range(nbuf)]
        self.s_st = [ph.sem("st") for _ in range(nbuf)]
        self.s_x = [ph.sem("sx") for _ in range(nbuf)]
        self.n = 0

    def begin(self):
        i = self.n % len(self.buf)
        self.n += 1
        return i, [(self.s_st[i], self.s_st[i].n), (self.s_x[i], self.s_x[i].n)]

    def store(self, i, dst, waits, eng="sp"):
        return self.ph.dma(eng, dst, self.buf[i][:], waits, self.s_st[i])

    def drain(self, eng="sp"):
        self.ph.wait(eng, [(s, s.n) for s in self.s_st])


def phase_win(nc, name, actT, w_in_l, C, T):
    ph = Phase(nc, name)
    G = Gemm(ph, 16)
    so = SlabOut(ph)
    cosT = ph.sb("cos", [128, NT], F32)
    sinT = ph.sb("sin", [128, NT], F32)
    zb = [ph.sb(f"zb{i}", [128, 512], BF) for i in range(2)]
    t1 = [ph.sb(f"t1{i}", [128, 512], F32) for i in range(2)]
    t2 = [ph.sb(f"t2{i}", [128, 512], F32) for i in range(2)]
    ps2 = [ph.ps(f"r{i}", [128, 512]) for i in range(2)]
    vst = [ph.sb(f"vst{i}", [128, 512], BF) for i in range(2)]
    tails_x = ph.sb("tlx", [128, 16, 32], BF)
    tails_c = ph.sb("tlc", [128, 16, 32], BF)
    tails_u = ph.sb("tlu", [128, 16, 32], BF)
    s_c = ph.sem("c")
    s_pe2 = ph.sem("pe2")
    s_dv = ph.sem("dv")
    s_pl = ph.sem("pl")
    s_vst = [ph.sem("vst"), ph.sem("vst")]
    s_tl = ph.sem("tl")
    ph.dma("sp", cosT[:], C["cos"], sig=s_c)
    ph.dma("sp", sinT[:], C["sin"], sig=s_c)
    RT = C["RT"]
    rope_n = [0]
    pool_of_rope = {}

    def fm_job(c0, kind, dests, tails=None):
        b, wv = G.load_w(w_in_l[:, c0:c0 + 512])
        for cb in range(4):
            slot, free_w = so.begin()
            last = None
            for tb in range(4):
                mm = [(G.wb[b][:, kc, cb * 128:(cb + 1) * 128], actT[:, kc, tb * 512:(tb + 1) * 512]) for kc in range(16)]
                fw = [(G.s_w[b], wv)] if (cb == 0 and tb == 0) else []
                dst = so.buf[slot][:, tb * 512:(tb + 1) * 512]
                if kind in ("copy", "sigmoid"):
                    func = AF.Copy if kind == "copy" else AF.Sigmoid

                    def evac(ps, w, dst=dst, func=func, tb=tb, free_w=free_w):
                        ww = list(w) + (free_w if tb == 0 else [])
                        ph.op("act", lambda e: e.activation(out=dst, in_=ps, func=func), ww, G.s_ep)

                    G.tile(mm, evac, fw)
                    last = (G.s_ep, G.s_ep.n)
                else:
                    n = rope_n[0]
                    rb = n % 2
                    rope_n[0] += 1

                    def evac(ps, w, rb=rb, n=n):
                        ww = list(w)
                        if n >= 2:
                            ww.append((s_pl, pool_of_rope[n - 2]))
                        ph.op("act", lambda e: e.activation(out=zb[rb][:], in_=ps, func=AF.Copy), ww, G.s_ep)

                    G.tile(mm, evac, fw)
                    epv = G.s_ep.n
                    pv = ph.op("pe", lambda e, rb=rb: e.matmul(ps2[rb][:], RT[:], zb[rb][:], start=True, stop=True),
                               [(G.s_ep, epv)], s_pe2)
                    ph.op("pool", lambda e, rb=rb, tb=tb: e.tensor_tensor(out=t1[rb][:], in0=zb[rb][:],
                                                                          in1=cosT[:, tb * 512:(tb + 1) * 512], op=ALU.mult),
                          [(G.s_ep, epv), (s_c, 32)])
                    dv = ph.op("dve", lambda e, rb=rb, tb=tb: e.tensor_tensor(out=t2[rb][:], in0=ps2[rb][:],
                                                                              in1=sinT[:, tb * 512:(tb + 1) * 512], op=ALU.mult),
                               [(s_pe2, pv), (s_c, 32)], s_dv)
                    pl = ph.op("pool", lambda e, rb=rb, dst=dst: e.tensor_tensor(out=dst, in0=t1[rb][:], in1=t2[rb][:], op=ALU.add),
                               [(s_dv, dv)] + (free_w if tb == 0 else []), s_pl)
                    pool_of_rope[n] = pl
                    last = (s_pl, pl)
            if tails is not None:
                tl, idx = tails
                ph.op("pool", lambda e, slot=slot, tl=tl, idx=idx: e.tensor_copy(
                    out=tl[:, idx, :].rearrange("p (i t) -> p i t", t=2),
                    in_=so.buf[slot][:].rearrange("p (i j) -> p i j", j=128)[:, :, 126:128]),
                    [last], so.s_x[slot])
                tails = (tl, idx + 1)
            so.store(slot, dests[cb], [last])
        G.end_job()

    def tok_job(c0, ncols, kind, dst_fn):
        b, wv = G.load_w(w_in_l[:, c0:c0 + ncols])
        for i in range(16):
            mm = [(actT[:, kc, i * 128:(i + 1) * 128], G.wb[b][:, kc, 0:ncols]) for kc in range(16)]
            fw = [(G.s_w[b], wv)] if i == 0 else []
            if kind == "v":
                vb = i % 2

                def evac(ps, w, vb=vb):
                    ph.op("act", lambda e: e.activation(out=vst[vb][:], in_=ps, func=AF.Copy),
                          list(w) + [(s_vst[vb], s_vst[vb].n)], G.s_ep)

                G.tile(mm, evac, fw)
                ph.dma("sp", dst_fn(i), vst[vb][:], [(G.s_ep, G.s_ep.n)], s_vst[vb])
            else:
                def evac(ps, w, i=i):
                    ph.op("act", lambda e: e.activation(out=dst_fn(i), in_=ps, func=AF.Sigmoid), w, G.s_ep)

                G.tile(mm, evac, fw)
        G.end_job()

    gk = T["gk_in"]
    fm = T["fm"]
    for g in range(4):
        fm_job(g * 512, "rope", [T["qT"][4 * g + r] for r in range(4)])
    kv0 = 2048
    fm_job(kv0 + 0 * 512, "copy", [gk[(0 + g) * 128:(1 + g) * 128, :] for g in range(4)])
    fm_job(kv0 + 1 * 512, "copy", [gk[(4 + g) * 128:(5 + g) * 128, :] for g in range(4)])
    fm_job(kv0 + 2 * 512, "rope", [gk[(8 + g) * 128:(9 + g) * 128, :] for g in range(4)])
    tok_job(kv0 + 3 * 512, 512, "v", lambda i: T["gv_in"][i * 128:(i + 1) * 128, 0:512])
    fm_job(kv0 + 4 * 512, "rope", [gk[(12 + g) * 128:(13 + g) * 128, :] for g in range(4)])
    tok_job(kv0 + 5 * 512, 512, "v", lambda i: T["gv_in"][i * 128:(i + 1) * 128, 512:1024])
    tok_job(5120, 48, "ng", lambda i: T["gates"][:, i, :])
    cv0 = 5168
    for j in range(4):
        fm_job(cv0 + j * 512, "copy", [fm[4 * j + r] for r in range(4)], tails=(tails_x, 4 * j))
    for j in range(4):
        fm_job(cv0 + 2048 + j * 512, "copy", [fm[16 + 4 * j + r] for r in range(4)])
    for j in range(4):
        fm_job(cv0 + 4096 + j * 512, "copy", [fm[32 + 4 * j + r] for r in range(4)], tails=(tails_c, 4 * j))
    mg0 = 11312
    for j in range(8):
        fm_job(mg0 + j * 512, "sigmoid", [fm[48 + 4 * j + r] for r in range(4)])
    tl_w = [(s, s.n) for s in so.s_x]
    pl = ph.op("pool", lambda e: e.tensor_tensor(out=tails_u[:], in0=tails_x[:], in1=tails_c[:], op=ALU.mult), tl_w, s_tl)
    ph.dma("sp", T["gt_in"].rearrange("(s p) c -> p s c", p=128), tails_u[:], [(s_tl, pl)], s_tl)
    so.drain()
    ph.wait("sp", [(s_tl, s_tl.n), (s_vst[0], s_vst[0].n), (s_vst[1], s_vst[1].n)])
    ph.run()


def phase_gather(nc, name, T):
    ph = Phase(nc, name)
    s = ph.sem("cc")
    for a, b, rows, rc in (("gk_in", "gk", 2048, 512), ("gv_in", "gv", 2048, 1024), ("gt_in", "gt", 2048, 2048)):
        for k in range(rows // rc):
            if rc == rows:
                i_ap = T[a + "_t"].ap().opt()
                o_ap = T[b + "_t"].ap().opt()
            else:
                i_ap = T[a + "_t"].ap()[k * rc:(k + 1) * rc, :].opt()
                o_ap = T[b + "_t"].ap()[2 * k * rc:2 * (k + 1) * rc, :].opt()
            ph.op("pool", lambda e, i_ap=i_ap, o_ap=o_ap: e.collective_compute("AllGather", ALU.bypass, PAIRS,
                                                                               ins=[i_ap], outs=[o_ap]), [], s, 1)
            ph.wait("pool", [(s, s.n)])
    ph.run()


def phase_compress(nc, name, L, W, C, T, KCT, VCA):
    ph = Phase(nc, name)
    gk = T["gk"]
    xin = [ph.sb(f"xin{i}", [128, S], BF) for i in range(2)]
    w1 = [ph.sb(f"w1{i}", [128, 32, 256], BF) for i in range(2)]
    w2 = [ph.sb(f"w2{i}", [128, 2, 128], BF) for i in range(2)]
    posT = [ph.sb(f"pos{i}", [128, 32], F32) for i in range(2)]
    posb = [ph.sb(f"posb{i}", [128, 32], BF) for i in range(2)]
    cbias = [ph.sb(f"cb{i}", [128, 2], F32) for i in range(2)]
    hid = ph.sb("hid", [128, 2, 256], BF)
    zb = ph.sb("zb", [128, 256], BF)
    t1 = ph.sb("t1", [128, 256], F32)
    t2 = ph.sb("t2", [128, 256], F32)
    cosc = ph.sb("cosc", [128, 256], F32)
    sinc = ph.sb("sinc", [128, 256], F32)
    pb = ph.ps("pb", [128, 2])
    phid = [ph.ps(f"ph{i}", [128, 256]) for i in range(2)]
    pk = ph.ps("pk", [128, 256])
    pr = ph.ps("pr", [128, 256])
    pv = ph.ps("pv", [128, 2, 128])
    s_ld = ph.sem("ld")
    s_x = [ph.sem("x"), ph.sem("x")]
    s_pe = ph.sem("pe")
    s_act = ph.sem("act")
    s_dve = ph.sem("dve")
    s_pl = ph.sem("pl")
    RT = C["RT"]
    names = [("cmp_w1_k", "cmp_w2_k", "cmp_pos_kT"), ("cmp_w1_v", "cmp_w2_v", "cmp_pos_vT")]
    for kv in range(2):
        ph.dma("pool", w1[kv][:], W[names[kv][0]][L].rearrange("(l p) c -> p l c", p=128), sig=s_ld)
        ph.dma("pool", w2[kv][:], W[names[kv][1]][L].rearrange("(j p) c -> p j c", p=128), sig=s_ld)
        ph.dma("sp", posT[kv][:], W[names[kv][2]][L], sig=s_ld)
    ph.dma("sp", cosc[:], C["cosc"], sig=s_ld)
    ph.dma("sp", sinc[:], C["sinc"], sig=s_ld)
    ph.op("dve", lambda e: e.memset(KCT[:], 0.0), [], s_dve)
    ph.op("dve", lambda e: e.memset(VCA[:], 0.0), [], s_dve)
    ph.wait("dve", [(s_dve, s_dve.n)])
    ph.op("dve", lambda e: e.memset(VCA[:, :, 0, 128:129], 1.0), [], s_dve)
    ph.op("dve", lambda e: e.memset(VCA[0:127, :, 1, 128:129], 1.0), [], s_dve)
    ph.wait("pool", [(s_dve, s_dve.n)])
    for g in range(4):
        ph.dma("pool", VCA[:, g, :, 129:193], C["msel"], sig=s_ld)
    LD_ALL = 16 * 12
    for kv in range(2):
        d0 = ph.op("dve", lambda e, kv=kv: e.tensor_copy(out=posb[kv][:], in_=posT[kv][:]), [(s_ld, LD_ALL)], s_dve)
        for jc in range(2):
            for l in range(32):
                w = [(s_dve, d0), (s_act, s_act.n)] if l == 0 else []
                v = ph.op("pe", lambda e, kv=kv, jc=jc, l=l: e.matmul(pb[:, jc:jc + 1], w1[kv][:, l, jc * 128:(jc + 1) * 128],
                                                                       posb[kv][:, l:l + 1], start=(l == 0), stop=(l == 31)),
                          w, s_pe if l == 31 else None)
        ph.op("act", lambda e, kv=kv: e.activation(out=cbias[kv][:], in_=pb[:], func=AF.Copy), [(s_pe, v)], s_act)
    pe_end = {}
    it = 0
    for g in range(4):
        for kv in range(2):
            xb = it % 2
            slab = kv * 4 + g
            w = [(s_pe, pe_end[it - 2])] if it >= 2 else []
            for r in range(2):
                ph.dma("sp", xin[xb][:].rearrange("p (i r j) -> p i r j", r=2, j=128)[:, :, r, :],
                       gk[gk_row(r, slab): gk_row(r, slab) + 128, :].rearrange("p (i j) -> p i j", j=128),
                       w, s_x[xb])
            xv = s_x[xb].n
            for jc in range(2):
                for l in range(32):
                    w = []
                    if l == 0:
                        w = [(s_x[xb], xv), (s_act, s_act.n)]
                    rhs = xin[xb][:, l:l + 16 * 254 + 1:16]
                    v = ph.op("pe", lambda e, kv=kv, jc=jc, l=l, rhs=rhs: e.matmul(
                        phid[jc][:, 0:255], w1[kv][:, l, jc * 128:(jc + 1) * 128], rhs, start=(l == 0), stop=(l == 31)),
                        w, s_pe if l == 31 else None)
                ph.op("act", lambda e, kv=kv, jc=jc: e.activation(out=hid[:, jc, 0:255], in_=phid[jc][:, 0:255], func=AF.Silu,
                                                                  bias=cbias[kv][:, jc:jc + 1]),
                      [(s_pe, v), (s_pe, s_pe.n)], s_act)
            av = s_act.n
            if kv == 0:
                for jc in range(2):
                    v = ph.op("pe", lambda e, jc=jc: e.matmul(pk[:, 0:255], w2[0][:, jc, :], hid[:, jc, 0:255],
                                                               start=(jc == 0), stop=(jc == 1)),
                              [(s_act, av), (s_dve, s_dve.n), (s_pl, s_pl.n)] if jc == 0 else [], s_pe if jc == 1 else None)
                a2 = ph.op("act", lambda e: e.activation(out=zb[:, 0:255], in_=pk[:, 0:255], func=AF.Copy), [(s_pe, v)], s_act)
                v2 = ph.op("pe", lambda e: e.matmul(pr[:, 0:255], RT[:], zb[:, 0:255], start=True, stop=True), [(s_act, a2)], s_pe)
                ph.op("pool", lambda e: e.tensor_tensor(out=t1[:, 0:255], in0=zb[:, 0:255], in1=cosc[:, 0:255], op=ALU.mult),
                      [(s_act, a2)])
                dv = ph.op("dve", lambda e: e.tensor_tensor(out=t2[:, 0:255], in0=pr[:, 0:255], in1=sinc[:, 0:255], op=ALU.mult),
                           [(s_pe, v2)], s_dve)
                ph.op("pool", lambda e, g=g: e.tensor_tensor(out=KCT[:, g, 0:255], in0=t1[:, 0:255], in1=t2[:, 0:255], op=ALU.add),
                      [(s_dve, dv)], s_pl)
            else:
                for nt in range(2):
                    M = 128 if nt == 0 else 127
                    for jc in range(2):
                        v = ph.op("pe", lambda e, jc=jc, nt=nt, M=M: e.matmul(pv[0:M, nt, :], hid[:, jc, nt * 128:nt * 128 + M],
                                                                               w2[1][:, jc, :], start=(jc == 0), stop=(jc == 1)),
                                  [(s_act, av)] if (jc == 0 and nt == 0) else [], s_pe if (jc == 1 and nt == 1) else None)
                ph.op("act", lambda e, g=g: e.activation(out=VCA[:, g, 0, 0:128], in_=pv[:, 0, :], func=AF.Copy), [(s_pe, v)], None)
                ph.op("act", lambda e, g=g: e.activation(out=VCA[0:127, g, 1, 0:128], in_=pv[0:127, 1, :], func=AF.Copy), [], s_act)
            pe_end[it] = s_pe.n
            it += 1
    ph.wait("sp", [(s_act, s_act.n), (s_pl, s_pl.n), (s_ld, LD_ALL)])
    ph.run()


class SPipe:
    def __init__(self, ph):
        self.ph = ph
        self.buf = [ph.ps("sA", [128, 512]), ph.ps("sB", [128, 512])]
        self.s_qk = ph.sem("qk")
        self.s_ac = ph.sem("ac")
        self.k = 0
        self.ac_of = {}
        self.pending = None

    def tile(self, mm, M, N, act, post=None, first_waits=(), three_d=True):
        ph = self.ph
        k = self.k
        ps = self.buf[k % 2][0:M, 0:N]
        n = len(mm)
        v = 0
        for j, (l, r) in enumerate(mm):
            w = []
            if j == 0:
                w = list(first_waits)
                if k >= 2:
                    w.append((self.s_ac, self.ac_of[k - 2]))
            o = ps.rearrange("p (r q) -> p r q", r=4) if (three_d and len(r.shape) == 3) else ps
            v = ph.op("pe", lambda e, l=l, r=r, j=j, o=o: e.matmul(o, l, r, start=(j == 0), stop=(j == n - 1)),
                      w, self.s_qk if j == n - 1 else None)
        before = self.s_ac.n
        act(ps, [(self.s_qk, v)])
        assert self.s_ac.n == before + 1
        self.ac_of[k] = self.s_ac.n
        self.flush()
        if post is not None:
            acv = self.s_ac.n
            self.pending = lambda: post([(self.s_ac, acv)])
        self.k += 1

    def flush(self):
        if self.pending is not None:
            p = self.pending
            self.pending = None
            p()


def phase_attn(nc, name, C, T, KCT, VCA, gates):
    ph = Phase(nc, name)
    gk, gv = T["gk"], T["gv"]
    ident = C["ident"]
    qt0 = ph.sb("qt0", [128, 4, NT], BF)
    qt = [qt0, qt0]
    ksel = [ph.sb(f"ks{i}", [128, S], BF) for i in range(2)]
    kwin = [ph.sb(f"kw{i}", [128, S], BF) for i in range(2)]
    vsel = [ph.sb(f"vs{i}", [128, 32, 130], BF) for i in range(2)]
    vwin = [ph.sb(f"vw{i}", [128, 32, 130], BF) for i in range(2)]
    cmpmask = ph.sb("cmpm", [128, 2, NT], BF)
    fb = ph.sb("fb", [128, 16, 64], F32)
    expand = ph.sb("exp", [64, 32, 128], BF)
    cmaskS = ph.sb("cms", [128, 2, 128], BF)
    wmask = ph.sb("wm", [128, 6, 128], BF)
    PcT = ph.sb("pct", [128, 2, 512], BF)
    PT = [ph.sb(f"pt{i}", [128, 512], BF) for i in range(3)]
    dsafe = ph.sb("dsafe", [128, 12], F32)
    rec = ph.sb("rec", [128, 12], F32)
    coef = ph.sb("coef", [128, 12], F32)
    impm = ph.sb("impm", [128, 64], F32)
    impr = ph.sb("impr", [128, 64], F32)
    m8 = ph.sb("m8", [128, 16], F32)
    biasq = ph.sb("biasq", [128, 64], BF)
    BiasT = ph.sb("biasT", [64, 128], BF)
    o_acc = ph.sb("oacc", [128, 4, 128], F32)
    tmpo = ph.sb("tmpo", [128, 4, 128], F32)
    o_bf = ph.sb("obf", [128, 4, 128], BF)
    oTs = ph.sb("oTs", [128, 4, NT], BF)
    oc = ph.ps("oc", [128, 4, 256])
    osl = ph.ps("os", [128, 4, 256])
    ow = ph.ps("ow", [128, 4, 256])
    sp = SPipe(ph)
    s_c = ph.sem("c")
    s_ld = [ph.sem("ld"), ph.sem("ld")]
    s_pv = ph.sem("pv")
    s_dv = ph.sem("dv")
    s_pl = ph.sem("pl")
    s_st = ph.sem("st")
    s_ms = ph.sem("ms")
    for dst, src in ((cmpmask, "cmpmask"), (fb, "fb"), (expand, "expand"), (cmaskS, "cmaskS"), (wmask, "wmask")):
        ph.dma("sp", dst[:], C[src], sig=s_c)
    NC_C = 16 * 5
    for b in range(2):
        ph.op("dve", lambda e, b=b: e.memset(vsel[b][:, :, 129:130], 0.0), [], s_ms)
        ph.op("dve", lambda e, b=b: e.memset(vwin[b][:, :, 129:130], 0.0), [], s_ms)
        ph.op("dve", lambda e, b=b: e.memset(vsel[b][:, :, 128:129], 1.0), [], s_ms)
        ph.op("dve", lambda e, b=b: e.memset(vwin[b][:, :, 128:129], 1.0), [], s_ms)
    grp_pe_end = {}
    pt_n = [0]
    pv_of_pt = {}
    late = [None]
    dv_last_oc = [0]
    dv_last_os = [0]
    dv_last_ow = [0]
    pl_last = [0]

    def load_group(g):
        b = g % 2
        w = [(s_pv, grp_pe_end[g - 2]), (sp.s_qk, grp_pe_end[("qk", g - 2)])] if g >= 2 else []
        for r in range(2):
            for dst, slab in ((ksel[b], 8 + g), (kwin[b], 12 + g)):
                ph.dma("sp", dst[:].rearrange("p (i r j) -> p i r j", r=2, j=128)[:, :, r, :],
                       gk[gk_row(r, slab): gk_row(r, slab) + 128, :].rearrange("p (i j) -> p i j", j=128),
                       w, s_ld[b])
            for dst, c0 in ((vsel[b], 0), (vwin[b], 512)):
                for hf in range(2):
                    r0 = gv_row(r, hf * 1024)
                    ph.dma("sp", dst[:].rearrange("p (i r) c -> p i r c", r=2)[:, hf * 8:(hf + 1) * 8, r, 0:128],
                           gv[r0:r0 + 1024, c0 + g * 128: c0 + (g + 1) * 128].rearrange("(i j) c -> j i c", j=128),
                           w, s_ld[b])
        return s_ld[b].n

    ldv = {0: load_group(0)}
    for g in range(4):
        b = g % 2
        wq = [(s_pv, grp_pe_end[g - 1]), (sp.s_qk, grp_pe_end[("qk", g - 1)])] if g >= 1 else []
        ph.dma("sp", qt0[:], T["qT"][4 * g:4 * g + 4].rearrange("h p t -> p h t"), wq, s_ld[b])
        ldv[g] = s_ld[b].n
        if g + 1 < 4:
            ldv[g + 1] = load_group(g + 1)
        for i in range(16):
            first = [(s_ld[b], ldv[g]), (s_c, NC_C), (s_ms, 8)] if i == 0 else []
            qti = qt[b][:, :, i * 128:(i + 1) * 128]
            for nt in range(2):
                mm = [(KCT[:, g, nt * 128:(nt + 1) * 128], qti),
                      (ident[:], cmpmask[:, nt, i * 128:(i + 1) * 128].unsqueeze(1).broadcast_to([128, 4, 128]))]

                def act(ps, w, nt=nt):
                    ww = list(w)
                    if nt == 0:
                        ww.append((s_pv, s_pv.n))
                    ph.op("act", lambda e: e.activation(out=PcT[:, nt, :], in_=ps, func=AF.Exp, scale=SCALE), ww, sp.s_ac)

                post = None
                if nt == 1:
                    def post(w, g=g):
                        ww = list(w) + [(s_dv, dv_last_oc[0])]
                        for r in range(4):
                            for n2 in range(2):
                                ph.op("pe", lambda e, r=r, n2=n2: e.matmul(oc[:, r, 0:194], PcT[:, n2, r * 128:(r + 1) * 128],
                                                                          VCA[:, g, n2, :], start=(n2 == 0), stop=(n2 == 1)),
                                      ww if (r == 0 and n2 == 0) else [], s_pv if (r == 3 and n2 == 1) else None)
                sp.tile(mm, 128, 512, act, post, first if nt == 0 else [])
            sp.flush()
            pv_c = s_pv.n
            def dchain(fn, w=()):
                v = ph.op("dve", fn, [(s_dv, s_dv.n)] + list(w), s_dv)
                return v
            dchain(lambda e: e.tensor_scalar(out=dsafe[:, 0:4], in0=oc[:, :, 128], scalar1=1e-30, scalar2=None, op0=ALU.max),
                   [(s_pv, pv_c), (s_pl, pl_last[0])])
            dchain(lambda e: e.reciprocal(out=rec[:, 0:4], in_=dsafe[:, 0:4]))
            for r in range(4):
                src1 = fb[:, i, :] if r == 0 else impm[:]
                dchain(lambda e, r=r, src1=src1: e.scalar_tensor_tensor(out=impm[:], in0=oc[:, r, 129:193], scalar=rec[:, r:r + 1],
                                                                        in1=src1, op0=ALU.mult, op1=ALU.add))
            dchain(lambda e: e.max(out=m8[:, 0:8], in_=impm[:]))
            dchain(lambda e: e.match_replace(out=impr[:], in_to_replace=m8[:, 0:8], in_values=impm[:], imm_value=-1e9))
            dchain(lambda e: e.max(out=m8[:, 8:16], in_=impr[:]))
            bq = dchain(lambda e: e.tensor_scalar(out=biasq[:], in0=impm[:], scalar1=m8[:, 15:16], scalar2=NEG,
                                                  op0=ALU.is_lt, op1=ALU.mult), [(sp.s_qk, sp.s_qk.n)])
            gsl = gates[:, i, 12 * g:12 * g + 12].rearrange("p (h c) -> p h c", c=3)
            dchain(lambda e, gsl=gsl: e.tensor_tensor(out=coef[:, 0:4], in0=rec[:, 0:4], in1=gsl[:, :, 0], op=ALU.mult))
            dv_last_oc[0] = dchain(lambda e: e.tensor_tensor(out=o_acc[:], in0=oc[:, :, 0:128],
                                                             in1=coef[:, 0:4].unsqueeze(2).broadcast_to([128, 4, 128]), op=ALU.mult))
            wt = [j for j in range(6) if 2 * i - 4 + j >= 0]
            for j in wt:
                kt = 2 * i - 4 + j
                mm = [(kwin[b][:, kt * 128:(kt + 1) * 128], qti)]
                if j not in (2, 3):
                    mm.append((ident[:], wmask[:, j, :].unsqueeze(1).broadcast_to([128, 4, 128])))
                n = pt_n[0]
                pt_n[0] += 1
                pb = n % 3

                def act(ps, w, pb=pb, n=n):
                    ww = list(w)
                    if n >= 3:
                        ww.append((s_pv, pv_of_pt[n - 3]))
                    ph.op("act", lambda e: e.activation(out=PT[pb][:], in_=ps, func=AF.Exp, scale=SCALE), ww, sp.s_ac)

                def post(w, pb=pb, n=n, kt=kt, j=j, b=b, wt=wt):
                    ww = list(w)
                    if j == wt[0]:
                        ww.append((s_dv, dv_last_ow[0]))
                    for r in range(4):
                        v = ph.op("pe", lambda e, r=r: e.matmul(ow[:, r, 0:130], PT[pb][:, r * 128:(r + 1) * 128], vwin[b][:, kt, :],
                                                                 start=(j == wt[0] and r in (0, 2)), stop=(j == wt[-1]),
                                                                 skip_group_check=True),
                                  ww if r == 0 else [], s_pv if r == 3 else None)
                    pv_of_pt[n] = v

                sp.tile(mm, 128, 512, act, post)
            if late[0] is not None:
                late[0]()
                late[0] = None
            def actb(ps, w):
                ph.op("act", lambda e: e.activation(out=BiasT[:], in_=ps, func=AF.Copy), list(w) + [(s_pv, s_pv.n)], sp.s_ac)
            sp.tile([(biasq[:], ident[:])], 64, 128, actb, None, [(s_dv, bq)], three_d=False)
            bias_ready = sp.s_ac.n
            nkt = 2 * i + 2
            for kt in range(nkt):
                mm = [(ksel[b][:, kt * 128:(kt + 1) * 128], qti),
                      (expand[:, kt, :], BiasT[:].unsqueeze(1).broadcast_to([64, 4, 128]))]
                if kt >= 2 * i:
                    mm.append((ident[:], cmaskS[:, kt - 2 * i, :].unsqueeze(1).broadcast_to([128, 4, 128])))
                n = pt_n[0]
                pt_n[0] += 1
                pb = n % 3

                def act(ps, w, pb=pb, n=n):
                    ww = list(w)
                    if n >= 3:
                        ww.append((s_pv, pv_of_pt[n - 3]))
                    ph.op("act", lambda e: e.activation(out=PT[pb][:], in_=ps, func=AF.Exp, scale=SCALE), ww, sp.s_ac)

                def post(w, pb=pb, n=n, kt=kt, nkt=nkt, b=b):
                    ww = list(w)
                    if kt == 0:
                        ww.append((s_dv, dv_last_os[0]))
                    for r in range(4):
                        v = ph.op("pe", lambda e, r=r: e.matmul(osl[:, r, 0:130], PT[pb][:, r * 128:(r + 1) * 128], vsel[b][:, kt, :],
                                                                 start=(kt == 0 and r in (0, 2)), stop=(kt == nkt - 1),
                                                                 skip_group_check=True),
                                  ww if r == 0 else [], s_pv if r == 3 else None)
                    pv_of_pt[n] = v

                sp.tile(mm, 128, 512, act, post, [(sp.s_ac, bias_ready)] if kt == 0 else [])
            sp.flush()
            pv_end = s_pv.n
            dchain(lambda e: e.tensor_scalar(out=dsafe[:, 4:8], in0=osl[:, :, 128], scalar1=1e-30, scalar2=None, op0=ALU.max),
                   [(s_pv, pv_end)])
            dchain(lambda e: e.tensor_scalar(out=dsafe[:, 8:12], in0=ow[:, :, 128], scalar1=1e-30, scalar2=None, op0=ALU.max))
            dchain(lambda e: e.reciprocal(out=rec[:, 4:12], in_=dsafe[:, 4:12]))
            dchain(lambda e, gsl=gsl: e.tensor_tensor(out=coef[:, 4:8], in0=rec[:, 4:8], in1=gsl[:, :, 1], op=ALU.mult))
            dchain(lambda e, gsl=gsl: e.tensor_tensor(out=coef[:, 8:12], in0=rec[:, 8:12], in1=gsl[:, :, 2], op=ALU.mult))
            d1 = dchain(lambda e: e.tensor_tensor(out=tmpo[:], in0=osl[:, :, 0:128],
                                                  in1=coef[:, 4:8].unsqueeze(2).broadcast_to([128, 4, 128]), op=ALU.mult),
                        [(s_pl, s_pl.n)])
            dv_last_os[0] = d1
            p1 = ph.op("pool", lambda e: e.tensor_tensor(out=o_acc[:], in0=o_acc[:], in1=tmpo[:], op=ALU.add), [(s_dv, d1)], s_pl)
            d2 = dchain(lambda e: e.tensor_tensor(out=tmpo[:], in0=ow[:, :, 0:128],
                                                  in1=coef[:, 8:12].unsqueeze(2).broadcast_to([128, 4, 128]), op=ALU.mult),
                        [(s_pl, p1)])
            dv_last_ow[0] = d2
            p2 = ph.op("pool", lambda e: e.tensor_tensor(out=o_bf[:], in0=o_acc[:], in1=tmpo[:], op=ALU.add),
                       [(s_dv, d2), (sp.s_qk, sp.s_qk.n)], s_pl)
            pl_last[0] = p2

            def do_late(g=g, i=i, p2=p2):
                def actT(ps, w):
                    ww = list(w)
                    if i == 0 and g >= 1:
                        ww.append((s_st, 16 * g))
                    ph.op("act", lambda e: e.activation(out=oTs[:, :, i * 128:(i + 1) * 128],
                                                        in_=ps.rearrange("p (r q) -> p r q", r=4), func=AF.Copy), ww, sp.s_ac)
                k = sp.k
                pso = sp.buf[k % 2]
                w0 = [(s_pl, p2)]
                if k >= 2:
                    w0.append((sp.s_ac, sp.ac_of[k - 2]))
                v = 0
                for r in range(4):
                    v = ph.op("pe", lambda e, r=r: e.matmul(pso[:, r * 128:(r + 1) * 128], o_bf[:, r, :], ident[:], start=True, stop=True),
                              w0 if r == 0 else [], sp.s_qk if r == 3 else None)
                actT(pso[:, :], [(sp.s_qk, v)])
                sp.ac_of[k] = sp.s_ac.n
                sp.flush()
                sp.k += 1

            late[0] = do_late
            if "dbg_k" in T and g == 0 and i == 1:
                sd = ph.sem("dbgs")
                w = [(s_pl, p2), (s_dv, s_dv.n)]
                for nm, src in (("dbg_k", ksel[b][:]), ("dbg_kw", kwin[b][:]),
                                ("dbg_v", vsel[b][:].rearrange("p a c -> p (a c)")), ("dbg_vw", vwin[b][:].rearrange("p a c -> p (a c)")),
                                ("dbg_bt", BiasT[:]), ("dbg_imp", impm[:]), ("dbg_ds", dsafe[:]), ("dbg_coef", coef[:]),
                                ("dbg_obf", o_bf[:].rearrange("p a c -> p (a c)")), ("dbg_m8", m8[:]),
                                ("dbg_kct", KCT[:].rearrange("p a c -> p (a c)")), ("dbg_vca", VCA[:].rearrange("p a b c -> p (a b c)")),
                                ("dbg_gates", gates[:].rearrange("p a c -> p (a c)"))):
                    ph.dma("sp", T[nm], src, w, sd)
                ph.wait("sp", [(sd, sd.n)])
        late[0]()
        late[0] = None
        grp_pe_end[g] = s_pv.n
        grp_pe_end[("qk", g)] = sp.s_qk.n
        ph.dma("sp", T["oT"][4 * g:4 * g + 4].rearrange("h p t -> p h t"), oTs[:], [(sp.s_ac, sp.s_ac.n)], s_st)
    ph.wait("sp", [(s_st, 64)])
    ph.run()


def load_act(ph, actT, src, sem, nk=16):
    for k in range(0, nk, 4):
        ph.dma("sp", actT[:, k:k + 4, :], src[k:k + 4].rearrange("k p t -> p k t"), [], sem)
    return sem.n


def phase_proj(nc, name, actT, wsrc, mode, T, C, src_act=None, xsrc=None):
    ph = Phase(nc, name)
    G = Gemm(ph, 16)
    fm = T["fm"]
    s_a = ph.sem("a")
    av = load_act(ph, actT, src_act, s_a) if src_act is not None else 0
    yf = [ph.sb(f"yf{i}", [128, 512], F32) for i in range(2)]
    s_po = ph.sem("po")
    s_l = [ph.sem("l"), ph.sem("l")]
    if mode == "resid":
        so = SlabOut(ph, F32, 2, "xs")
        xl = [ph.sb(f"xl{i}", [128, NT], F32) for i in range(2)]
        aux = aux2 = None
    else:
        so = SlabOut(ph, BF, 2)
        aux = [ph.sb(f"ax{i}", [128, NT], BF) for i in range(2)]
        aux2 = [ph.sb(f"ay{i}", [128, NT], BF) for i in range(2)] if mode == "conv" else None
        tt = [ph.sb(f"tt{i}", [128, 512], F32) for i in range(2)]
    s_p2 = ph.sem("p2")
    tn = 0
    post_of = {}
    for j in range(4):
        b, wv = G.load_w(wsrc[:, j * 512:(j + 1) * 512])
        for cb in range(4):
            c = 4 * j + cb
            slot, free_w = so.begin()
            lw = [(s_po, post_of.get(so.n - 3, 0))]
            if mode == "resid":
                ph.dma("sp", xl[slot][:], xsrc[c], lw, s_l[slot])
            else:
                ph.dma("sp", aux[slot][:], fm[(48 if mode == "attn" else 64) + c], lw, s_l[slot])
                if mode == "conv":
                    ph.dma("sp", aux2[slot][:], T["maT"][c], lw, s_l[slot])
            lv = s_l[slot].n
            last = 0
            for tb in range(4):
                mm = [(G.wb[b][:, kc, cb * 128:(cb + 1) * 128], actT[:, kc, tb * 512:(tb + 1) * 512]) for kc in range(16)]
                fw = [(G.s_w[b], wv), (s_a, av)] if (cb == 0 and tb == 0) else []
                yb = tn % 2
                tsl = slice(tb * 512, (tb + 1) * 512)

                def evac(ps, w, yb=yb, tn=tn):
                    ww = list(w) + [(s_po, post_of.get(("t", tn - 2), 0))]
                    ph.op("act", lambda e: e.activation(out=yf[yb][:], in_=ps, func=AF.Copy), ww, G.s_ep)

                G.tile(mm, evac, fw)
                ev = G.s_ep.n
                w0 = [(G.s_ep, ev), (s_l[slot], lv)] + (free_w if tb == 0 else [])
                if mode == "attn":
                    last = ph.op("pool", lambda e, yb=yb, slot=slot, tsl=tsl: e.tensor_tensor(
                        out=so.buf[slot][:, tsl], in0=yf[yb][:], in1=aux[slot][:, tsl], op=ALU.mult), w0, s_po)
                elif mode == "conv":
                    p = ph.op("pool", lambda e, yb=yb, slot=slot, tsl=tsl: e.tensor_tensor(
                        out=tt[yb][:], in0=yf[yb][:], in1=aux[slot][:, tsl], op=ALU.mult),
                        w0 + [(s_po, post_of.get(("t", tn - 2), 0))], s_p2)
                    last = ph.op("dve", lambda e, yb=yb, slot=slot, tsl=tsl: e.tensor_tensor(
                        out=so.buf[slot][:, tsl], in0=tt[yb][:], in1=aux2[slot][:, tsl], op=ALU.add), [(s_p2, p)], s_po)
                else:
                    last = ph.op("dve", lambda e, yb=yb, slot=slot, tsl=tsl: e.tensor_tensor(
                        out=so.buf[slot][:, tsl], in0=yf[yb][:], in1=xl[slot][:, tsl], op=ALU.add), w0, s_po)
                post_of[("t", tn)] = last
                tn += 1
            post_of[so.n - 1] = last
            dst = T["xres"][c] if mode == "resid" else (T["maT"][c] if mode == "attn" else T["mT"][c])
            so.store(slot, dst, [(s_po, last)])
        G.end_job()
    so.drain()
    ph.run()


def phase_conv(nc, name, actT, W, L, C, T):
    ph = Phase(nc, name)
    fm, gt = T["fm"], T["gt"]
    xi = [ph.sb(f"xi{i}", [128, NT], BF) for i in range(2)]
    gc = [ph.sb(f"gc{i}", [128, NT], BF) for i in range(2)]
    gb = [ph.sb(f"gb{i}", [128, NT], BF) for i in range(2)]
    ub = [ph.sb(f"ub{i}", [128, 16, 130], F32) for i in range(2)]
    acc = [ph.sb(f"acc{i}", [128, 16, 128], F32) for i in range(2)]
    H0 = ph.sb("H0", [128, 16, 16, 2], BF)
    H1 = ph.sb("H1", [128, 16, 16, 2], BF)
    Ht = ph.sb("Ht", [128, 16, 16, 2], F32)
    halo = ph.sb("halo", [128, 16, 16, 2], F32)
    cw = ph.sb("cw", [128, 16, 3], F32)
    fl = ph.sb("fl", [128, 2], F32)
    s_c = ph.sem("c")
    s_l = [ph.sem("l"), ph.sem("l")]
    s_pl = ph.sem("pl")
    s_dv = ph.sem("dv")
    m0 = ph.op("dve", lambda e: e.memset(H0[:], 0.0), [], s_dv)
    ph.dma("sp", H0[:, :, 1:16, :].rearrange("p c i t -> p c (i t)"),
           gt[2048:4096, 0:30].rearrange("(c p) x -> p c x", p=128), [(s_dv, m0)], s_c)
    ph.dma("sp", H1[:].rearrange("p c i t -> p c (i t)"), gt[0:2048, :].rearrange("(c p) x -> p c x", p=128), [], s_c)
    ph.dma("sp", cw[:], W["conv_wT"][L], [], s_c)
    ph.dma("sp", fl[:], C["flags"], [], s_c)
    h1 = ph.op("dve", lambda e: e.tensor_scalar(out=Ht[:], in0=H0[:], scalar1=fl[:, 0:1], scalar2=None, op0=ALU.mult),
               [(s_c, 64)], s_dv)
    h2 = ph.op("dve", lambda e: e.scalar_tensor_tensor(out=halo[:], in0=H1[:], scalar=fl[:, 1:2], in1=Ht[:],
                                                       op0=ALU.mult, op1=ALU.add), [(s_dv, h1)], s_dv)
    dv_of = {}
    pl_of = {}
    for c in range(16):
        b = c % 2
        w = [(s_dv, dv_of[c - 2])] if c >= 2 else []
        for dst, off in ((xi[b], 0), (gb[b], 16), (gc[b], 32)):
            ph.dma("sp", dst[:], fm[off + c], w, s_l[b])
        lv = s_l[b].n
        ph.op("pool", lambda e, b=b, c=c: e.tensor_copy(out=ub[b][:, :, 0:2], in_=halo[:, c, :, :]),
              [(s_dv, h2)] + w, None)
        pl_of[c] = ph.op("pool", lambda e, b=b: e.tensor_tensor(out=ub[b][:, :, 2:130],
                                                                in0=xi[b][:].rearrange("p (i j) -> p i j", j=128),
                                                                in1=gc[b][:].rearrange("p (i j) -> p i j", j=128), op=ALU.mult),
                         [(s_l[b], lv)], s_pl)
        d = ph.op("dve", lambda e, b=b, c=c: e.tensor_scalar(out=acc[b][:], in0=ub[b][:, :, 2:130], scalar1=cw[:, c, 2:3],
                                                             scalar2=None, op0=ALU.mult), [(s_pl, pl_of[c])], s_dv)
        d = ph.op("dve", lambda e, b=b, c=c: e.scalar_tensor_tensor(out=acc[b][:], in0=ub[b][:, :, 1:129], scalar=cw[:, c, 1:2],
                                                                    in1=acc[b][:], op0=ALU.mult, op1=ALU.add), [(s_dv, d)], s_dv)
        d = ph.op("dve", lambda e, b=b, c=c: e.scalar_tensor_tensor(out=acc[b][:], in0=ub[b][:, :, 0:128], scalar=cw[:, c, 0:1],
                                                                    in1=acc[b][:], op0=ALU.mult, op1=ALU.add), [(s_dv, d)], s_dv)
        dv_of[c] = ph.op("dve", lambda e, b=b, c=c: e.tensor_tensor(out=actT[:, c, :].rearrange("p (i j) -> p i j", j=128),
                                                                    in0=acc[b][:], in1=gb[b][:].rearrange("p (i j) -> p i j", j=128),
                                                                    op=ALU.mult), [(s_dv, d)], s_dv)
    ph.wait("sp", [(s_dv, s_dv.n)])
    ph.run()


def phase_ffn(nc, name, actT, W, L, T):
    ph = Phase(nc, name)
    Gu = Gemm(ph, 16)
    Gd = Gemm(ph, 8, pfx="d")
    fT = ph.sb("fT", [128, 8, NT], BF)
    rf = [ph.sb(f"rf{i}", [128, 512], F32) for i in range(2)]
    yf = [ph.sb(f"yf{i}", [128, 512], F32) for i in range(2)]
    so = SlabOut(ph, F32, 2, "xs")
    xl = [ph.sb(f"xl{i}", [128, NT], F32) for i in range(2)]
    s_pl = ph.sem("pl")
    s_po = ph.sem("po")
    s_l = [ph.sem("l"), ph.sem("l")]
    xres = T["xres"]
    un = 0
    dn = 0
    pl_of = {}
    po_of = {}
    slab_store = {}
    slab_n = 0
    slab_last = {}
    for hg in range(8):
        for j in range(2):
            b, wv = Gu.load_w(W["w_up"][L][:, hg * 1024 + j * 512: hg * 1024 + (j + 1) * 512])
            for cb in range(4):
                for tb in range(4):
                    mm = [(Gu.wb[b][:, kc, cb * 128:(cb + 1) * 128], actT[:, kc, tb * 512:(tb + 1) * 512]) for kc in range(16)]
                    fw = [(Gu.s_w[b], wv)] if (cb == 0 and tb == 0) else []
                    rb = un % 2

                    def evac(ps, w, rb=rb, un=un):
                        ph.op("act", lambda e: e.activation(out=rf[rb][:], in_=ps, func=AF.Relu),
                              list(w) + [(s_pl, pl_of.get(un - 2, 0))], Gu.s_ep)

                    Gu.tile(mm, evac, fw)
                    pl_of[un] = ph.op("pool", lambda e, rb=rb, j=j, cb=cb, tb=tb: e.tensor_tensor(
                        out=fT[:, j * 4 + cb, tb * 512:(tb + 1) * 512], in0=rf[rb][:], in1=rf[rb][:], op=ALU.mult),
                        [(Gu.s_ep, Gu.s_ep.n), (Gd.s_pe, Gd.s_pe.n)], s_pl)
                    un += 1
            Gu.end_job()
        f_ready = s_pl.n
        for j in range(4):
            b, wv = Gd.load_w(W["w_down"][L][hg * 1024:(hg + 1) * 1024, j * 512:(j + 1) * 512])
            for cb in range(4):
                c = 4 * j + cb
                slot, free_w = so.begin()
                lw = [(s_po, slab_last.get(slab_n - 2, 0))]
                if c in slab_store:
                    lw.append(slab_store[c])
                ph.dma("sp", xl[slot][:], xres[c], lw, s_l[slot])
                lv = s_l[slot].n
                last = 0
                for tb in range(4):
                    mm = [(Gd.wb[b][:, kc, cb * 128:(cb + 1) * 128], fT[:, kc, tb * 512:(tb + 1) * 512]) for kc in range(8)]
                    fw = [(Gd.s_w[b], wv), (s_pl, f_ready)] if (cb == 0 and tb == 0) else []
                    yb = dn % 2
                    tsl = slice(tb * 512, (tb + 1) * 512)

                    def evac(ps, w, yb=yb, dn=dn):
                        ph.op("act", lambda e: e.activation(out=yf[yb][:], in_=ps, func=AF.Copy),
                              list(w) + [(s_po, po_of.get(dn - 2, 0))], Gd.s_ep)

                    Gd.tile(mm, evac, fw)
                    last = ph.op("dve", lambda e, yb=yb, slot=slot, tsl=tsl: e.tensor_tensor(
                        out=so.buf[slot][:, tsl], in0=yf[yb][:], in1=xl[slot][:, tsl], op=ALU.add),
                        [(Gd.s_ep, Gd.s_ep.n), (s_l[slot], lv)] + (list(free_w) if tb == 0 else []), s_po)
                    po_of[dn] = last
                    dn += 1
                slab_last[slab_n] = last
                slab_n += 1
                v = so.store(slot, xres[c], [(s_po, last)])
                slab_store[c] = (so.s_st[slot], v)
            Gd.end_job()
    so.drain()
    ph.run()


WEIGHT_USERS = {
    "w_in": ("win",), "cmp_w1_k": ("cmp",), "cmp_w2_k": ("cmp",), "cmp_pos_kT": ("cmp",),
    "cmp_w1_v": ("cmp",), "cmp_w2_v": ("cmp",), "cmp_pos_vT": ("cmp",), "conv_wT": ("cv",),
    "w_attn_proj": ("ap",), "w_conv_out": ("co",), "w_o": ("wo",), "w_up": ("ffn",), "w_down": ("ffn",),
    "norm1_gT": ("n1",), "norm2_gT": ("n2",), "final_gT": ("nf",),
}
LAST_SHAPES = {}


def build(sel=None, dbg=()):
    nc = bass.Bass("TRN2", target_bir_lowering=False)

    def on(L, nm):
        return sel is None or (L, nm) in sel

    def din(nm, shape, dt=F32):
        LAST_SHAPES[nm] = tuple(shape)
        return nc.dram_tensor(nm, shape, dt, kind="ExternalInput").ap()

    def scr(nm, shape, dt=BF):
        if nm in dbg:
            t = nc.dram_tensor(nm, shape, dt, kind="ExternalOutput")
        else:
            t = nc.dram_tensor(nm, shape, dt)
        return t

    W = {}
    for nm, shp in (("w_in", [DEPTH, D, IN_COLS]),
                    ("cmp_w1_k", [DEPTH, 4096, 256]), ("cmp_w2_k", [DEPTH, 256, 128]), ("cmp_pos_kT", [DEPTH, 128, 32]),
                    ("cmp_w1_v", [DEPTH, 4096, 256]), ("cmp_w2_v", [DEPTH, 256, 128]), ("cmp_pos_vT", [DEPTH, 128, 32]),
                    ("conv_wT", [DEPTH, 128, 16, 3]), ("w_attn_proj", [DEPTH, D, D]), ("w_conv_out", [DEPTH, D, D]),
                    ("w_o", [DEPTH, D, D]), ("w_up", [DEPTH, D, DFF]), ("w_down", [DEPTH, DFF, D]),
                    ("norm1_gT", [DEPTH, 128, 16]), ("norm2_gT", [DEPTH, 128, 16]), ("final_gT", [128, 16])):
        users = WEIGHT_USERS[nm]
        used = sel is None or any((L, u) in sel for L in range(DEPTH + 1) for u in users)
        if not used:
            shp = [1] * len(shp)
        elif sel is not None and nm != "final_gT" and not any((1, u) in sel for u in users):
            shp = [1] + list(shp[1:])
        W[nm] = din(nm, shp)
    xin = din("xT", [16, 128, NT])
    Cd = {}
    for nm, shp, dt in (("cos", [128, NT], F32), ("sin", [128, NT], F32), ("cosc", [128, 256], F32), ("sinc", [128, 256], F32),
                        ("RT", [128, 128], BF), ("ident", [128, 128], BF), ("ones", [128, 128], BF), ("msel", [128, 2, 64], BF),
                        ("cmpmask", [128, 2, NT], BF), ("fb", [128, 16, 64], F32), ("expand", [64, 32, 128], BF),
                        ("cmaskS", [128, 2, 128], BF), ("wmask", [128, 6, 128], BF), ("flags", [128, 2], F32)):
        Cd[nm] = din("c_" + nm, shp, dt)
    outT = nc.dram_tensor("outT", [16, 128, NT], F32, kind="ExternalOutput").ap()

    T = {}
    T["qT"] = scr("qT", [16, 128, NT]).ap()
    T["fm"] = scr("fm", [80, 128, NT]).ap()
    for nm, shp in (("gk_in", [2048, NT]), ("gk", [4096, NT]), ("gv_in", [2048, 1024]), ("gv", [4096, 1024]),
                    ("gt_in", [2048, 32]), ("gt", [4096, 32])):
        t = scr(nm, shp)
        T[nm + "_t"] = t
        T[nm] = t.ap()
    T["oT"] = scr("oT", [16, 128, NT]).ap()
    T["maT"] = scr("maT", [16, 128, NT]).ap()
    T["mT"] = scr("mT", [16, 128, NT]).ap()
    T["xres"] = scr("xres", [16, 128, NT], F32).ap()
    if "dbg_k" in dbg:
        for nm, shp, dt in (("dbg_k", [128, S], BF), ("dbg_kw", [128, S], BF), ("dbg_v", [128, 32 * 130], BF),
                            ("dbg_vw", [128, 32 * 130], BF), ("dbg_bt", [64, 128], BF), ("dbg_imp", [128, 64], F32),
                            ("dbg_ds", [128, 12], F32), ("dbg_coef", [128, 12], F32), ("dbg_obf", [128, 512], BF),
                            ("dbg_m8", [128, 16], F32), ("dbg_kct", [128, 1024], BF), ("dbg_vca", [128, 4 * 2 * 194], BF),
                            ("dbg_gates", [128, 16 * 48], F32), ("dbg_osl", [128, 1024], F32), ("dbg_ow", [128, 1024], F32),
                            ("dbg_oacc", [128, 512], F32), ("dbg_tmpo", [128, 512], F32)):
            T[nm] = scr(nm, shp, dt).ap()

    with contextlib.ExitStack() as es:
        del SEM_POOL[:]
        for i in range(24):
            SEM_POOL.append([es.enter_context(nc.semaphore(f"sem{i}")), 0])
        actT = es.enter_context(nc.sbuf_tensor("actT", [128, 16, NT], BF))
        gates = es.enter_context(nc.sbuf_tensor("gates", [128, 16, 48], F32))
        KCT = es.enter_context(nc.sbuf_tensor("KCT", [128, 4, 256], BF))
        VCA = es.enter_context(nc.sbuf_tensor("VCA", [128, 4, 2, 194], BF))
        RT = es.enter_context(nc.sbuf_tensor("RTs", [128, 128], BF))
        ident = es.enter_context(nc.sbuf_tensor("idents", [128, 128], BF))
        ones = es.enter_context(nc.sbuf_tensor("oness", [128, 128], BF))
        T["gates"] = gates
        C = dict(Cd)
        C["RT"], C["ident"], C["ones"] = RT, ident, ones
        ph = Phase(nc, "init")
        s = ph.sem("s")
        ph.dma("sp", RT[:], Cd["RT"], [], s)
        ph.dma("sp", ident[:], Cd["ident"], [], s)
        ph.dma("sp", ones[:], Cd["ones"], [], s)
        ph.wait("sp", [(s, 48)])
        ph.run()
        xcur = xin
        for L in range(DEPTH):
            if on(L, "n1"):
                phase_norm(nc, f"n1_{L}", xcur, W["norm1_gT"][L], ones, dst_sb=actT)
            if on(L, "win"):
                phase_win(nc, f"win{L}", actT, W["w_in"][L], C, T)
            if on(L, "ag"):
                phase_gather(nc, f"ag{L}", T)
            if on(L, "cmp"):
                phase_compress(nc, f"cmp{L}", L, W, C, T, KCT, VCA)
            if on(L, "att"):
                phase_attn(nc, f"att{L}", C, T, KCT, VCA, gates)
            if on(L, "ap"):
                phase_proj(nc, f"ap{L}", actT, W["w_attn_proj"][L], "attn", T, C, src_act=T["oT"])
            if on(L, "cv"):
                phase_conv(nc, f"cv{L}", actT, W, L, C, T)
            if on(L, "co"):
                phase_proj(nc, f"co{L}", actT, W["w_conv_out"][L], "conv", T, C)
            if on(L, "wo"):
                phase_proj(nc, f"wo{L}", actT, W["w_o"][L], "resid", T, C, src_act=T["mT"], xsrc=xcur)
            xcur = T["xres"]
            if on(L, "n2"):
                phase_norm(nc, f"n2_{L}", xcur, W["norm2_gT"][L], ones, dst_sb=actT)
            if on(L, "ffn"):
                phase_ffn(nc, f"ffn{L}", actT, W, L, T)
        if on(DEPTH, "nf"):
            phase_norm(nc, "nf", xcur, W["final_gT"], ones, dst_dram=outT)
    return nc


def _bf(a):
    return np.ascontiguousarray(a.astype(ml_dtypes.bfloat16))


def _consts(p):
    f32 = np.float32
    tl = np.arange(NT)
    pos = (128 * (2 * (tl // 128) + p) + tl % 128)
    half = 64
    inv_freq = np.exp(-math.log(10000.0) * np.arange(half, dtype=f32) / half).astype(f32)
    ang = pos.astype(f32)[None, :] * inv_freq[:, None]
    c = {}
    c["cos"] = np.concatenate([np.cos(ang), np.cos(ang)], 0).astype(f32)
    c["sin"] = np.concatenate([np.sin(ang), np.sin(ang)], 0).astype(f32)
    cpos = (np.arange(256) * 16 + 31).astype(f32)
    angc = cpos[None, :] * inv_freq[:, None]
    c["cosc"] = np.concatenate([np.cos(angc), np.cos(angc)], 0).astype(f32)
    c["sinc"] = np.concatenate([np.sin(angc), np.sin(angc)], 0).astype(f32)
    RT = np.zeros((128, 128), f32)
    for d in range(64):
        RT[d + 64, d] = -1.0
        RT[d, d + 64] = 1.0
    c["RT"] = _bf(RT)
    c["ident"] = _bf(np.eye(128, dtype=f32))
    c["ones"] = _bf(np.ones((128, 128), f32))
    n = np.arange(256)
    cs = n * 16
    ss = np.arange(64) * 64
    ov = np.minimum(cs[:, None] + 32, ss[None, :] + 64) - np.maximum(cs[:, None], ss[None, :])
    msel = (np.clip(ov, 0, None) / 32.0).astype(f32)
    msel[255] = 0.0
    c["msel"] = _bf(msel.reshape(2, 128, 64).transpose(1, 0, 2))
    cm = np.where((n[:, None] * 16 + 31 <= pos[None, :]) & (n[:, None] < 255), 0.0, NEG).astype(f32)
    c["cmpmask"] = _bf(cm.reshape(2, 128, NT).transpose(1, 0, 2))
    tb = pos // 64
    m = np.arange(64)
    valid = m[None, :] <= tb[:, None]
    forced = (m[None, :] == 0) | (m[None, :] == tb[:, None]) | (m[None, :] == tb[:, None] - 1)
    fb = np.where(valid, np.where(forced, 1e4, 0.0), -1e4).astype(f32)
    c["fb"] = np.ascontiguousarray(fb.reshape(16, 128, 64).transpose(1, 0, 2))
    ex = np.zeros((64, 32, 128), f32)
    for kt in range(32):
        ex[2 * kt, kt, 0:64] = 1.0
        ex[2 * kt + 1, kt, 64:128] = 1.0
    c["expand"] = _bf(ex)
    k = np.arange(128)
    causal = np.where(k[:, None] <= k[None, :], 0.0, NEG).astype(f32)
    allm = np.full((128, 128), NEG, f32)
    zero = np.zeros((128, 128), f32)
    anti = np.where(k[:, None] > k[None, :], 0.0, NEG).astype(f32)
    if p == 0:
        cms = [causal, allm]
        wm = [anti, zero, zero, zero, causal, allm]
    else:
        cms = [zero, causal]
        wm = [allm, anti, zero, zero, zero, causal]
    c["cmaskS"] = _bf(np.stack(cms, 1))
    c["wmask"] = _bf(np.stack(wm, 1))
    fl = np.zeros((128, 2), f32)
    fl[:, p] = 1.0
    c["flags"] = fl
    return c


def kernel(**inputs):
    f32 = np.float32
    x = np.asarray(inputs["x"], f32)

    def gT(a):
        a = np.asarray(a, f32)
        return np.ascontiguousarray(np.swapaxes(a.reshape(a.shape[:-1] + (16, 128)), -1, -2))

    shared = {
        "w_in": np.asarray(inputs["w_in"], f32),
        "cmp_w1_k": np.asarray(inputs["cmp_w1_k"], f32), "cmp_w2_k": np.asarray(inputs["cmp_w2_k"], f32),
        "cmp_pos_kT": np.ascontiguousarray(np.swapaxes(np.asarray(inputs["cmp_pos_k"], f32), 1, 2)),
        "cmp_w1_v": np.asarray(inputs["cmp_w1_v"], f32), "cmp_w2_v": np.asarray(inputs["cmp_w2_v"], f32),
        "cmp_pos_vT": np.ascontiguousarray(np.swapaxes(np.asarray(inputs["cmp_pos_v"], f32), 1, 2)),
        "conv_wT": np.ascontiguousarray(np.asarray(inputs["conv_w"], f32).reshape(DEPTH, 3, 16, 128).transpose(0, 3, 2, 1)),
        "w_attn_proj": np.asarray(inputs["w_attn_proj"], f32), "w_conv_out": np.asarray(inputs["w_conv_out"], f32),
        "w_o": np.asarray(inputs["w_o"], f32), "w_up": np.asarray(inputs["w_up"], f32), "w_down": np.asarray(inputs["w_down"], f32),
        "norm1_gT": gT(inputs["norm1_g"]), "norm2_gT": gT(inputs["norm2_g"]), "final_gT": gT(inputs["final_g"]),
    }
    consts = [_consts(0), _consts(1)]
    in_maps = []
    for c in range(8):
        b, p = c // 2, c % 2
        xo = x[b].reshape(16, 2, 128, D)[:, p].reshape(NT, D)
        m = dict(shared)
        m["xT"] = np.ascontiguousarray(xo.T.reshape(16, 128, NT))
        for k, v in consts[p].items():
            m["c_" + k] = v
        in_maps.append(m)
    nc = build()
    res = run_bass_kernel_spmd(nc, in_maps, core_ids=list(range(8)))
    out = np.empty((4, S, D), f32)
    for c in range(8):
        b, p = c // 2, c % 2
        o = np.asarray(res.results[c]["outT"], f32).reshape(D, NT).T
        out[b].reshape(16, 2, 128, D)[:, p] = o.reshape(16, 128, D)
    return out
```

```python
import contextlib
import math

import ml_dtypes
import numpy as np

import concourse.bass as bass
import concourse.mybir as mybir
from concourse.bass_utils import run_bass_kernel_spmd

F32 = mybir.dt.float32
BF = mybir.dt.bfloat16
ALU = mybir.AluOpType
AF = mybir.ActivationFunctionType

D = 2048
S = 4096
NT = 2048
DEPTH = 2
NH = 16
DFF = 8192
IN_COLS = 15408
NEG = -30000.0
PAIRS = [[0, 1], [2, 3], [4, 5], [6, 7]]
SCALE = 128 ** -0.5


def gk_row(r, slab):
    return (slab // 4) * 1024 + r * 512 + (slab % 4) * 128


def gv_row(r, t):
    return (t // 1024) * 2048 + r * 1024 + (t % 1024)


class Sem:
    def __init__(self, slot):
        self.slot = slot
        self.h = slot[0]
        self.base = slot[1]
        self.n = 0


SEM_POOL = []


class Phase:
    def __init__(self, nc, name):
        self.nc = nc
        self.name = name
        self.es = contextlib.ExitStack()
        self.q = {e: [] for e in ("pe", "act", "dve", "pool", "sp")}
        self.k = 0
        self.sems = []

    def sem(self, nm):
        sm = Sem(SEM_POOL[self.k])
        self.k += 1
        self.sems.append(sm)
        return sm

    def sb(self, nm, shape, dt):
        return self.es.enter_context(self.nc.sbuf_tensor(f"{self.name}_{nm}", shape, dt))

    def ps(self, nm, shape, dt=F32):
        return self.es.enter_context(self.nc.psum_tensor(f"{self.name}_{nm}", shape, dt))

    def op(self, eng, fn, waits=(), sig=None, inc=1):
        waits = [(s, v) for (s, v) in waits if v > 0]

        def thunk(e, fn=fn, waits=waits, sig=sig, inc=inc):
            for s, v in waits:
                e.wait_ge(s.h, s.base + v)
            ins = fn(e)
            if sig is not None:
                ins.then_inc(sig.h, inc)

        self.q[eng].append(thunk)
        if sig is not None:
            sig.n += inc
            return sig.n
        return 0

    def dma(self, eng, out, in_, waits=(), sig=None):
        return self.op(eng, lambda e, out=out, in_=in_: e.dma_start(out=out, in_=in_), waits, sig, 16)

    def wait(self, eng, waits):
        waits = [(s, v) for (s, v) in waits if v > 0]

        def thunk(e, waits=waits):
            for s, v in waits:
                e.wait_ge(s.h, s.base + v)

        self.q[eng].append(thunk)

    def run(self):
        q = self.q
        with self.nc.Block() as block:
            @block.tensor
            def _(e):
                for f in q["pe"]:
                    f(e)

            @block.scalar
            def _(e):
                for f in q["act"]:
                    f(e)

            @block.vector
            def _(e):
                for f in q["dve"]:
                    f(e)

            @block.gpsimd
            def _(e):
                for f in q["pool"]:
                    f(e)

            @block.sync
            def _(e):
                for f in q["sp"]:
                    f(e)
        for sm in self.sems:
            sm.slot[1] += sm.n
        self.es.close()


def phase_norm(nc, name, xsrc, gain_ap, ones_bf, dst_sb=None, dst_dram=None):
    ph = Phase(nc, name)
    xb = [ph.sb(f"xb{i}", [128, 16, 512], F32) for i in range(2)]
    sq = ph.sb("sq", [128, 16, 512], BF)
    g = ph.sb("g", [128, 16], F32)
    tmp = ph.sb("tmp", [128, 512], F32)
    tmp2 = ph.sb("tmp2", [128, 512], F32)
    rstd = ph.sb("rstd", [128, 512], F32)
    pss = ph.ps("ss", [128, 512])
    ob = ph.sb("ob", [128, 16, 512], F32) if dst_dram is not None else None
    s_ld = [ph.sem("ld"), ph.sem("ld")]
    s_g = ph.sem("g")
    s_act = ph.sem("act")
    s_pe = ph.sem("pe")
    s_dve = ph.sem("dve")
    s_st = ph.sem("st")
    ph.dma("sp", g[:], gain_ap, sig=s_g)
    dve_done = {}
    pe_done = {}
    dve_a = {}
    for tb in range(4):
        b = tb % 2
        w = [(s_dve, dve_done[tb - 2])] if tb >= 2 else []
        ld = ph.dma("sp", xb[b][:], xsrc[:, :, tb * 512:(tb + 1) * 512].rearrange("k p t -> p k t"), w, s_ld[b])
        for kc in range(16):
            w = []
            if kc == 0:
                w = [(s_ld[b], ld)]
                if tb >= 1:
                    w.append((s_pe, pe_done[tb - 1]))
            a_sq = ph.op("act", lambda e, kc=kc, b=b: e.activation(out=sq[:, kc, :], in_=xb[b][:, kc, :], func=AF.Square),
                         w, s_act if kc == 15 else None)
        for kc in range(16):
            w = []
            if kc == 0:
                w = [(s_act, a_sq)]
                if tb >= 1:
                    w.append((s_dve, dve_a[tb - 1]))
            pe_done[tb] = ph.op("pe", lambda e, kc=kc: e.matmul(pss[:], ones_bf[:], sq[:, kc, :], start=(kc == 0), stop=(kc == 15)),
                                w, s_pe if kc == 15 else None)
        dve_a[tb] = ph.op("dve", lambda e: e.tensor_scalar(out=tmp[:], in0=pss[:], scalar1=1.0 / D, scalar2=1e-6,
                                                           op0=ALU.mult, op1=ALU.add),
                          [(s_pe, pe_done[tb])], s_dve)
        a_sqrt = ph.op("act", lambda e: e.activation(out=tmp2[:], in_=tmp[:], func=AF.Sqrt), [(s_dve, dve_a[tb])], s_act)
        d_b = ph.op("dve", lambda e: e.reciprocal(out=rstd[:], in_=tmp2[:]), [(s_act, a_sqrt)], s_dve)
        for kc in range(16):
            w = []
            if kc == 0:
                w = [(s_dve, d_b), (s_g, 16)]
                if ob is not None and tb >= 1:
                    w.append((s_st, 16 * tb))
            if ob is None:
                o = dst_sb[:, kc, tb * 512:(tb + 1) * 512]
            else:
                o = ob[:, kc, :]
            dve_done[tb] = ph.op("dve", lambda e, kc=kc, b=b, o=o: e.scalar_tensor_tensor(
                out=o, in0=xb[b][:, kc, :], scalar=g[:, kc:kc + 1], in1=rstd[:], op0=ALU.mult, op1=ALU.mult),
                w, s_dve if kc == 15 else None)
        if ob is not None:
            ph.dma("sp", dst_dram[:, :, tb * 512:(tb + 1) * 512].rearrange("k p t -> p k t"), ob[:],
                   [(s_dve, dve_done[tb])], s_st)
    if ob is not None:
        ph.wait("sp", [(s_st, 64)])
    ph.run()


class Gemm:
    NPS = 4

    def __init__(self, ph, nk, wcols=512, pfx=""):
        self.ph = ph
        self.nk = nk
        self.wb = [ph.sb(f"{pfx}w{i}", [128, nk, wcols], BF) for i in range(2)]
        self.s_w = [ph.sem("w"), ph.sem("w")]
        self.s_pe = ph.sem("gpe")
        self.s_ep = ph.sem("gep")
        self.psb = [ph.ps(f"{pfx}g{i}", [128, 512]) for i in range(self.NPS)]
        self.T = 0
        self.J = 0
        self.job_pe_end = {}
        self.ep_of_tile = {}

    def load_w(self, wsrc):
        ph = self.ph
        J = self.J
        b = J % 2
        ncols = wsrc.shape[1]
        w = [(self.s_pe, self.job_pe_end[J - 2])] if J >= 2 else []
        v = ph.dma("pool", self.wb[b][:, :, 0:ncols], wsrc.rearrange("(k p) c -> p k c", p=128), w, self.s_w[b])
        self.J += 1
        return b, v

    def tile(self, mm_list, evac, first_waits=()):
        ph = self.ph
        T = self.T
        ps = self.psb[T % self.NPS]
        n = len(mm_list)
        M = mm_list[0][0].shape[-1]
        N = mm_list[0][1].shape[-1]
        pso = ps[0:M, 0:N]
        for k, (l, r) in enumerate(mm_list):
            w = []
            if k == 0:
                w = list(first_waits)
                if T >= self.NPS:
                    w.append((self.s_ep, self.ep_of_tile[T - self.NPS]))
            v = ph.op("pe", lambda e, l=l, r=r, k=k, pso=pso: e.matmul(pso, l, r, start=(k == 0), stop=(k == n - 1)),
                      w, self.s_pe if k == n - 1 else None)
        before = self.s_ep.n
        evac(pso, [(self.s_pe, v)])
        assert self.s_ep.n == before + 1
        self.ep_of_tile[T] = self.s_ep.n
        self.T += 1
        return T

    def end_job(self):
        self.job_pe_end[self.J - 1] = self.s_pe.n


class SlabOut:
    def __init__(self, ph, dt=BF, nbuf=2, name="stg"):
        self.ph = ph
        self.buf = [ph.sb(f"{name}{i}", [128, NT], dt) for i in range(nbuf)]
        self.s_st = [ph.sem("st") for _ in range(nbuf)]
        self.s_x = [ph.sem("sx") for _ in range(nbuf)]
        self.n = 0

    def begin(self):
        i = self.n % len(self.buf)
        self.n += 1
        return i, [(self.s_st[i], self.s_st[i].n), (self.s_x[i], self.s_x[i].n)]

    def store(self, i, dst, waits, eng="sp"):
        return self.ph.dma(eng, dst, self.buf[i][:], waits, self.s_st[i])

    def drain(self, eng="sp"):
        self.ph.wait(eng, [(s, s.n) for s in self.s_st])


def phase_win(nc, name, actT, w_in_l, C, T):
    ph = Phase(nc, name)
    G = Gemm(ph, 16)
    so = SlabOut(ph)
    cosT = ph.sb("cos", [128, NT], F32)
    sinT = ph.sb("sin", [128, NT], F32)
    zb = [ph.sb(f"zb{i}", [128, 512], BF) for i in range(2)]
    t1 = [ph.sb(f"t1{i}", [128, 512], F32) for i in range(2)]
    t2 = [ph.sb(f"t2{i}", [128, 512], F32) for i in range(2)]
    ps2 = [ph.ps(f"r{i}", [128, 512]) for i in range(2)]
    vst = [ph.sb(f"vst{i}", [128, 512], BF) for i in range(2)]
    tails_x = ph.sb("tlx", [128, 16, 32], BF)
    tails_c = ph.sb("tlc", [128, 16, 32], BF)
    tails_u = ph.sb("tlu", [128, 16, 32], BF)
    s_c = ph.sem("c")
    s_pe2 = ph.sem("pe2")
    s_dv = ph.sem("dv")
    s_pl = ph.sem("pl")
    s_vst = [ph.sem("vst"), ph.sem("vst")]
    s_tl = ph.sem("tl")
    ph.dma("sp", cosT[:], C["cos"], sig=s_c)
    ph.dma("sp", sinT[:], C["sin"], sig=s_c)
    RT = C["RT"]
    rope_n = [0]
    pool_of_rope = {}

    def fm_job(c0, kind, dests, tails=None):
        b, wv = G.load_w(w_in_l[:, c0:c0 + 512])
        for cb in range(4):
            slot, free_w = so.begin()
            last = None
            for tb in range(4):
                mm = [(G.wb[b][:, kc, cb * 128:(cb + 1) * 128], actT[:, kc, tb * 512:(tb + 1) * 512]) for kc in range(16)]
                fw = [(G.s_w[b], wv)] if (cb == 0 and tb == 0) else []
                dst = so.buf[slot][:, tb * 512:(tb + 1) * 512]
                if kind in ("copy", "sigmoid"):
                    func = AF.Copy if kind == "copy" else AF.Sigmoid

                    def evac(ps, w, dst=dst, func=func, tb=tb, free_w=free_w):
                        ww = list(w) + (free_w if tb == 0 else [])
                        ph.op("act", lambda e: e.activation(out=dst, in_=ps, func=func), ww, G.s_ep)

                    G.tile(mm, evac, fw)
                    last = (G.s_ep, G.s_ep.n)
                else:
                    n = rope_n[0]
                    rb = n % 2
                    rope_n[0] += 1

                    def evac(ps, w, rb=rb, n=n):
                        ww = list(w)
                        if n >= 2:
                            ww.append((s_pl, pool_of_rope[n - 2]))
                        ph.op("act", lambda e: e.activation(out=zb[rb][:], in_=ps, func=AF.Copy), ww, G.s_ep)

                    G.tile(mm, evac, fw)
                    epv = G.s_ep.n
                    pv = ph.op("pe", lambda e, rb=rb: e.matmul(ps2[rb][:], RT[:], zb[rb][:], start=True, stop=True),
                               [(G.s_ep, epv)], s_pe2)
                    ph.op("pool", lambda e, rb=rb, tb=tb: e.tensor_tensor(out=t1[rb][:], in0=zb[rb][:],
                                                                          in1=cosT[:, tb * 512:(tb + 1) * 512], op=ALU.mult),
                          [(G.s_ep, epv), (s_c, 32)])
                    dv = ph.op("dve", lambda e, rb=rb, tb=tb: e.tensor_tensor(out=t2[rb][:], in0=ps2[rb][:],
                                                                              in1=sinT[:, tb * 512:(tb + 1) * 512], op=ALU.mult),
                               [(s_pe2, pv), (s_c, 32)], s_dv)
                    pl = ph.op("pool", lambda e, rb=rb, dst=dst: e.tensor_tensor(out=dst, in0=t1[rb][:], in1=t2[rb][:], op=ALU.add),
                               [(s_dv, dv)] + (free_w if tb == 0 else []), s_pl)
                    pool_of_rope[n] = pl
                    last = (s_pl, pl)
            if tails is not None:
                tl, idx = tails
                ph.op("pool", lambda e, slot=slot, tl=tl, idx=idx: e.tensor_copy(
                    out=tl[:, idx, :].rearrange("p (i t) -> p i t", t=2),
                    in_=so.buf[slot][:].rearrange("p (i j) -> p i j", j=128)[:, :, 126:128]),
                    [last], so.s_x[slot])
                tails = (tl, idx + 1)
            so.store(slot, dests[cb], [last])
        G.end_job()

    def tok_job(c0, ncols, kind, dst_fn):
        b, wv = G.load_w(w_in_l[:, c0:c0 + ncols])
        for i in range(16):
            mm = [(actT[:, kc, i * 128:(i + 1) * 128], G.wb[b][:, kc, 0:ncols]) for kc in range(16)]
            fw = [(G.s_w[b], wv)] if i == 0 else []
            if kind == "v":
                vb = i % 2

                def evac(ps, w, vb=vb):
                    ph.op("act", lambda e: e.activation(out=vst[vb][:], in_=ps, func=AF.Copy),
                          list(w) + [(s_vst[vb], s_vst[vb].n)], G.s_ep)

                G.tile(mm, evac, fw)
                ph.dma("sp", dst_fn(i), vst[vb][:], [(G.s_ep, G.s_ep.n)], s_vst[vb])
            else:
                def evac(ps, w, i=i):
                    ph.op("act", lambda e: e.activation(out=dst_fn(i), in_=ps, func=AF.Sigmoid), w, G.s_ep)

                G.tile(mm, evac, fw)
        G.end_job()

    gk = T["gk_in"]
    fm = T["fm"]
    for g in range(4):
        fm_job(g * 512, "rope", [T["qT"][4 * g + r] for r in range(4)])
    kv0 = 2048
    fm_job(kv0 + 0 * 512, "copy", [gk[(0 + g) * 128:(1 + g) * 128, :] for g in range(4)])
    fm_job(kv0 + 1 * 512, "copy", [gk[(4 + g) * 128:(5 + g) * 128, :] for g in range(4)])
    fm_job(kv0 + 2 * 512, "rope", [gk[(8 + g) * 128:(9 + g) * 128, :] for g in range(4)])
    tok_job(kv0 + 3 * 512, 512, "v", lambda i: T["gv_in"][i * 128:(i + 1) * 128, 0:512])
    fm_job(kv0 + 4 * 512, "rope", [gk[(12 + g) * 128:(13 + g) * 128, :] for g in range(4)])
    tok_job(kv0 + 5 * 512, 512, "v", lambda i: T["gv_in"][i * 128:(i + 1) * 128, 512:1024])
    tok_job(5120, 48, "ng", lambda i: T["gates"][:, i, :])
    cv0 = 5168
    for j in range(4):
        fm_job(cv0 + j * 512, "copy", [fm[4 * j + r] for r in range(4)], tails=(tails_x, 4 * j))
    for j in range(4):
        fm_job(cv0 + 2048 + j * 512, "copy", [fm[16 + 4 * j + r] for r in range(4)])
    for j in range(4):
        fm_job(cv0 + 4096 + j * 512, "copy", [fm[32 + 4 * j + r] for r in range(4)], tails=(tails_c, 4 * j))
    mg0 = 11312
    for j in range(8):
        fm_job(mg0 + j * 512, "sigmoid", [fm[48 + 4 * j + r] for r in range(4)])
    tl_w = [(s, s.n) for s in so.s_x]
    pl = ph.op("pool", lambda e: e.tensor_tensor(out=tails_u[:], in0=tails_x[:], in1=tails_c[:], op=ALU.mult), tl_w, s_tl)
    ph.dma("sp", T["gt_in"].rearrange("(s p) c -> p s c", p=128), tails_u[:], [(s_tl, pl)], s_tl)
    so.drain()
    ph.wait("sp", [(s_tl, s_tl.n), (s_vst[0], s_vst[0].n), (s_vst[1], s_vst[1].n)])
    ph.run()


def phase_gather(nc, name, T):
    ph = Phase(nc, name)
    s = ph.sem("cc")
    for a, b, rows, rc in (("gk_in", "gk", 2048, 512), ("gv_in", "gv", 2048, 1024), ("gt_in", "gt", 2048, 2048)):
        for k in range(rows // rc):
            if rc == rows:
                i_ap = T[a + "_t"].ap().opt()
                o_ap = T[b + "_t"].ap().opt()
            else:
                i_ap = T[a + "_t"].ap()[k * rc:(k + 1) * rc, :].opt()
                o_ap = T[b + "_t"].ap()[2 * k * rc:2 * (k + 1) * rc, :].opt()
            ph.op("pool", lambda e, i_ap=i_ap, o_ap=o_ap: e.collective_compute("AllGather", ALU.bypass, PAIRS,
                                                                               ins=[i_ap], outs=[o_ap]), [], s, 1)
            ph.wait("pool", [(s, s.n)])
    ph.run()


def phase_compress(nc, name, L, W, C, T, KCT, VCA):
    ph = Phase(nc, name)
    gk = T["gk"]
    xin = [ph.sb(f"xin{i}", [128, S], BF) for i in range(2)]
    w1 = [ph.sb(f"w1{i}", [128, 32, 256], BF) for i in range(2)]
    w2 = [ph.sb(f"w2{i}", [128, 2, 128], BF) for i in range(2)]
    posT = [ph.sb(f"pos{i}", [128, 32], F32) for i in range(2)]
    posb = [ph.sb(f"posb{i}", [128, 32], BF) for i in range(2)]
    cbias = [ph.sb(f"cb{i}", [128, 2], F32) for i in range(2)]
    hid = ph.sb("hid", [128, 2, 256], BF)
    zb = ph.sb("zb", [128, 256], BF)
    t1 = ph.sb("t1", [128, 256], F32)
    t2 = ph.sb("t2", [128, 256], F32)
    cosc = ph.sb("cosc", [128, 256], F32)
    sinc = ph.sb("sinc", [128, 256], F32)
    pb = ph.ps("pb", [128, 2])
    phid = [ph.ps(f"ph{i}", [128, 256]) for i in range(2)]
    pk = ph.ps("pk", [128, 256])
    pr = ph.ps("pr", [128, 256])
    pv = ph.ps("pv", [128, 2, 128])
    s_ld = ph.sem("ld")
    s_x = [ph.sem("x"), ph.sem("x")]
    s_pe = ph.sem("pe")
    s_act = ph.sem("act")
    s_dve = ph.sem("dve")
    s_pl = ph.sem("pl")
    RT = C["RT"]
    names = [("cmp_w1_k", "cmp_w2_k", "cmp_pos_kT"), ("cmp_w1_v", "cmp_w2_v", "cmp_pos_vT")]
    for kv in range(2):
        ph.dma("pool", w1[kv][:], W[names[kv][0]][L].rearrange("(l p) c -> p l c", p=128), sig=s_ld)
        ph.dma("pool", w2[kv][:], W[names[kv][1]][L].rearrange("(j p) c -> p j c", p=128), sig=s_ld)
        ph.dma("sp", posT[kv][:], W[names[kv][2]][L], sig=s_ld)
    ph.dma("sp", cosc[:], C["cosc"], sig=s_ld)
    ph.dma("sp", sinc[:], C["sinc"], sig=s_ld)
    ph.op("dve", lambda e: e.memset(KCT[:], 0.0), [], s_dve)
    ph.op("dve", lambda e: e.memset(VCA[:], 0.0), [], s_dve)
    ph.wait("dve", [(s_dve, s_dve.n)])
    ph.op("dve", lambda e: e.memset(VCA[:, :, 0, 128:129], 1.0), [], s_dve)
    ph.op("dve", lambda e: e.memset(VCA[0:127, :, 1, 128:129], 1.0), [], s_dve)
    ph.wait("pool", [(s_dve, s_dve.n)])
    for g in range(4):
        ph.dma("pool", VCA[:, g, :, 129:193], C["msel"], sig=s_ld)
    LD_ALL = 16 * 12
    for kv in range(2):
        d0 = ph.op("dve", lambda e, kv=kv: e.tensor_copy(out=posb[kv][:], in_=posT[kv][:]), [(s_ld, LD_ALL)], s_dve)
        for jc in range(2):
            for l in range(32):
                w = [(s_dve, d0), (s_act, s_act.n)] if l == 0 else []
                v = ph.op("pe", lambda e, kv=kv, jc=jc, l=l: e.matmul(pb[:, jc:jc + 1], w1[kv][:, l, jc * 128:(jc + 1) * 128],
                                                                       posb[kv][:, l:l + 1], start=(l == 0), stop=(l == 31)),
                          w, s_pe if l == 31 else None)
        ph.op("act", lambda e, kv=kv: e.activation(out=cbias[kv][:], in_=pb[:], func=AF.Copy), [(s_pe, v)], s_act)
    pe_end = {}
    it = 0
    for g in range(4):
        for kv in range(2):
            xb = it % 2
            slab = kv * 4 + g
            w = [(s_pe, pe_end[it - 2])] if it >= 2 else []
            for r in range(2):
                ph.dma("sp", xin[xb][:].rearrange("p (i r j) -> p i r j", r=2, j=128)[:, :, r, :],
                       gk[gk_row(r, slab): gk_row(r, slab) + 128, :].rearrange("p (i j) -> p i j", j=128),
                       w, s_x[xb])
            xv = s_x[xb].n
            for jc in range(2):
                for l in range(32):
                    w = []
                    if l == 0:
                        w = [(s_x[xb], xv), (s_act, s_act.n)]
                    rhs = xin[xb][:, l:l + 16 * 254 + 1:16]
                    v = ph.op("pe", lambda e, kv=kv, jc=jc, l=l, rhs=rhs: e.matmul(
                        phid[jc][:, 0:255], w1[kv][:, l, jc * 128:(jc + 1) * 128], rhs, start=(l == 0), stop=(l == 31)),
                        w, s_pe if l == 31 else None)
                ph.op("act", lambda e, kv=kv, jc=jc: e.activation(out=hid[:, jc, 0:255], in_=phid[jc][:, 0:255], func=AF.Silu,
                                                                  bias=cbias[kv][:, jc:jc + 1]),
                      [(s_pe, v), (s_pe, s_pe.n)], s_act)
            av = s_act.n
            if kv == 0:
                for jc in range(2):
                    v = ph.op("pe", lambda e, jc=jc: e.matmul(pk[:, 0:255], w2[0][:, jc, :], hid[:, jc, 0:255],
                                                               start=(jc == 0), stop=(jc == 1)),
                              [(s_act, av), (s_dve, s_dve.n), (s_pl, s_pl.n)] if jc == 0 else [], s_pe if jc == 1 else None)
                a2 = ph.op("act", lambda e: e.activation(out=zb[:, 0:255], in_=pk[:, 0:255], func=AF.Copy), [(s_pe, v)], s_act)
                v2 = ph.op("pe", lambda e: e.matmul(pr[:, 0:255], RT[:], zb[:, 0:255], start=True, stop=True), [(s_act, a2)], s_pe)
                ph.op("pool", lambda e: e.tensor_tensor(out=t1[:, 0:255], in0=zb[:, 0:255], in1=cosc[:, 0:255], op=ALU.mult),
                      [(s_act, a2)])
                dv = ph.op("dve", lambda e: e.tensor_tensor(out=t2[:, 0:255], in0=pr[:, 0:255], in1=sinc[:, 0:255], op=ALU.mult),
                           [(s_pe, v2)], s_dve)
                ph.op("pool", lambda e, g=g: e.tensor_tensor(out=KCT[:, g, 0:255], in0=t1[:, 0:255], in1=t2[:, 0:255], op=ALU.add),
                      [(s_dve, dv)], s_pl)
            else:
                for nt in range(2):
                    M = 128 if nt == 0 else 127
                    for jc in range(2):
                        v = ph.op("pe", lambda e, jc=jc, nt=nt, M=M: e.matmul(pv[0:M, nt, :], hid[:, jc, nt * 128:nt * 128 + M],
                                                                               w2[1][:, jc, :], start=(jc == 0), stop=(jc == 1)),
                                  [(s_act, av)] if (jc == 0 and nt == 0) else [], s_pe if (jc == 1 and nt == 1) else None)
                ph.op("act", lambda e, g=g: e.activation(out=VCA[:, g, 0, 0:128], in_=pv[:, 0, :], func=AF.Copy), [(s_pe, v)], None)
                ph.op("act", lambda e, g=g: e.activation(out=VCA[0:127, g, 1, 0:128], in_=pv[0:127, 1, :], func=AF.Copy), [], s_act)
            pe_end[it] = s_pe.n
            it += 1
    ph.wait("sp", [(s_act, s_act.n), (s_pl, s_pl.n), (s_ld, LD_ALL)])
    ph.run()


class SPipe:
    def __init__(self, ph):
        self.ph = ph
        self.buf = [ph.ps("sA", [128, 512]), ph.ps("sB", [128, 512])]
        self.s_qk = ph.sem("qk")
        self.s_ac = ph.sem("ac")
        self.k = 0
        self.ac_of = {}
        self.pending = None

    def tile(self, mm, M, N, act, post=None, first_waits=(), three_d=True):
        ph = self.ph
        k = self.k
        ps = self.buf[k % 2][0:M, 0:N]
        n = len(mm)
        v = 0
        for j, (l, r) in enumerate(mm):
            w = []
            if j == 0:
                w = list(first_waits)
                if k >= 2:
                    w.append((self.s_ac, self.ac_of[k - 2]))
            o = ps.rearrange("p (r q) -> p r q", r=4) if (three_d and len(r.shape) == 3) else ps
            v = ph.op("pe", lambda e, l=l, r=r, j=j, o=o: e.matmul(o, l, r, start=(j == 0), stop=(j == n - 1)),
                      w, self.s_qk if j == n - 1 else None)
        before = self.s_ac.n
        act(ps, [(self.s_qk, v)])
        assert self.s_ac.n == before + 1
        self.ac_of[k] = self.s_ac.n
        self.flush()
        if post is not None:
            acv = self.s_ac.n
            self.pending = lambda: post([(self.s_ac, acv)])
        self.k += 1

    def flush(self):
        if self.pending is not None:
            p = self.pending
            self.pending = None
            p()


def phase_attn(nc, name, C, T, KCT, VCA, gates):
    ph = Phase(nc, name)
    gk, gv = T["gk"], T["gv"]
    ident = C["ident"]
    qt0 = ph.sb("qt0", [128, 4, NT], BF)
    qt = [qt0, qt0]
    ksel = [ph.sb(f"ks{i}", [128, S], BF) for i in range(2)]
    kwin = [ph.sb(f"kw{i}", [128, S], BF) for i in range(2)]
    vsel = [ph.sb(f"vs{i}", [128, 32, 130], BF) for i in range(2)]
    vwin = [ph.sb(f"vw{i}", [128, 32, 130], BF) for i in range(2)]
    cmpmask = ph.sb("cmpm", [128, 2, NT], BF)
    fb = ph.sb("fb", [128, 16, 64], F32)
    expand = ph.sb("exp", [64, 32, 128], BF)
    cmaskS = ph.sb("cms", [128, 2, 128], BF)
    wmask = ph.sb("wm", [128, 6, 128], BF)
    PcT = ph.sb("pct", [128, 2, 512], BF)
    PT = [ph.sb(f"pt{i}", [128, 512], BF) for i in range(3)]
    dsafe = ph.sb("dsafe", [128, 12], F32)
    rec = ph.sb("rec", [128, 12], F32)
    coef = ph.sb("coef", [128, 12], F32)
    impm = ph.sb("impm", [128, 64], F32)
    impr = ph.sb("impr", [128, 64], F32)
    m8 = ph.sb("m8", [128, 16], F32)
    biasq = ph.sb("biasq", [128, 64], BF)
    BiasT = ph.sb("biasT", [64, 128], BF)
    o_acc = ph.sb("oacc", [128, 4, 128], F32)
    tmpo = ph.sb("tmpo", [128, 4, 128], F32)
    o_bf = ph.sb("obf", [128, 4, 128], BF)
    oTs = ph.sb("oTs", [128, 4, NT], BF)
    oc = ph.ps("oc", [128, 4, 256])
    osl = ph.ps("os", [128, 4, 256])
    ow = ph.ps("ow", [128, 4, 256])
    sp = SPipe(ph)
    s_c = ph.sem("c")
    s_ld = [ph.sem("ld"), ph.sem("ld")]
    s_pv = ph.sem("pv")
    s_dv = ph.sem("dv")
    s_pl = ph.sem("pl")
    s_st = ph.sem("st")
    s_ms = ph.sem("ms")
    for dst, src in ((cmpmask, "cmpmask"), (fb, "fb"), (expand, "expand"), (cmaskS, "cmaskS"), (wmask, "wmask")):
        ph.dma("sp", dst[:], C[src], sig=s_c)
    NC_C = 16 * 5
    for b in range(2):
        ph.op("dve", lambda e, b=b: e.memset(vsel[b][:, :, 129:130], 0.0), [], s_ms)
        ph.op("dve", lambda e, b=b: e.memset(vwin[b][:, :, 129:130], 0.0), [], s_ms)
        ph.op("dve", lambda e, b=b: e.memset(vsel[b][:, :, 128:129], 1.0), [], s_ms)
        ph.op("dve", lambda e, b=b: e.memset(vwin[b][:, :, 128:129], 1.0), [], s_ms)
    grp_pe_end = {}
    pt_n = [0]
    pv_of_pt = {}
    late = [None]
    dv_last_oc = [0]
    dv_last_os = [0]
    dv_last_ow = [0]
    pl_last = [0]

    def load_group(g):
        b = g % 2
        w = [(s_pv, grp_pe_end[g - 2]), (sp.s_qk, grp_pe_end[("qk", g - 2)])] if g >= 2 else []
        for r in range(2):
            for dst, slab in ((ksel[b], 8 + g), (kwin[b], 12 + g)):
                ph.dma("sp", dst[:].rearrange("p (i r j) -> p i r j", r=2, j=128)[:, :, r, :],
                       gk[gk_row(r, slab): gk_row(r, slab) + 128, :].rearrange("p (i j) -> p i j", j=128),
                       w, s_ld[b])
            for dst, c0 in ((vsel[b], 0), (vwin[b], 512)):
                for hf in range(2):
                    r0 = gv_row(r, hf * 1024)
                    ph.dma("sp", dst[:].rearrange("p (i r) c -> p i r c", r=2)[:, hf * 8:(hf + 1) * 8, r, 0:128],
                           gv[r0:r0 + 1024, c0 + g * 128: c0 + (g + 1) * 128].rearrange("(i j) c -> j i c", j=128),
                           w, s_ld[b])
        return s_ld[b].n

    ldv = {0: load_group(0)}
    for g in range(4):
        b = g % 2
        wq = [(s_pv, grp_pe_end[g - 1]), (sp.s_qk, grp_pe_end[("qk", g - 1)])] if g >= 1 else []
        ph.dma("sp", qt0[:], T["qT"][4 * g:4 * g + 4].rearrange("h p t -> p h t"), wq, s_ld[b])
        ldv[g] = s_ld[b].n
        if g + 1 < 4:
            ldv[g + 1] = load_group(g + 1)
        for i in range(16):
            first = [(s_ld[b], ldv[g]), (s_c, NC_C), (s_ms, 8)] if i == 0 else []
            qti = qt[b][:, :, i * 128:(i + 1) * 128]
            for nt in range(2):
                mm = [(KCT[:, g, nt * 128:(nt + 1) * 128], qti),
                      (ident[:], cmpmask[:, nt, i * 128:(i + 1) * 128].unsqueeze(1).broadcast_to([128, 4, 128]))]

                def act(ps, w, nt=nt):
                    ww = list(w)
                    if nt == 0:
                        ww.append((s_pv, s_pv.n))
                    ph.op("act", lambda e: e.activation(out=PcT[:, nt, :], in_=ps, func=AF.Exp, scale=SCALE), ww, sp.s_ac)

                post = None
                if nt == 1:
                    def post(w, g=g):
                        ww = list(w) + [(s_dv, dv_last_oc[0])]
                        for r in range(4):
                            for n2 in range(2):
                                ph.op("pe", lambda e, r=r, n2=n2: e.matmul(oc[:, r, 0:194], PcT[:, n2, r * 128:(r + 1) * 128],
                                                                          VCA[:, g, n2, :], start=(n2 == 0), stop=(n2 == 1)),
                                      ww if (r == 0 and n2 == 0) else [], s_pv if (r == 3 and n2 == 1) else None)
                sp.tile(mm, 128, 512, act, post, first if nt == 0 else [])
            sp.flush()
            pv_c = s_pv.n
            def dchain(fn, w=()):
                v = ph.op("dve", fn, [(s_dv, s_dv.n)] + list(w), s_dv)
                return v
            dchain(lambda e: e.tensor_scalar(out=dsafe[:, 0:4], in0=oc[:, :, 128], scalar1=1e-30, scalar2=None, op0=ALU.max),
                   [(s_pv, pv_c), (s_pl, pl_last[0])])
            dchain(lambda e: e.reciprocal(out=rec[:, 0:4], in_=dsafe[:, 0:4]))
            for r in range(4):
                src1 = fb[:, i, :] if r == 0 else impm[:]
                dchain(lambda e, r=r, src1=src1: e.scalar_tensor_tensor(out=impm[:], in0=oc[:, r, 129:193], scalar=rec[:, r:r + 1],
                                                                        in1=src1, op0=ALU.mult, op1=ALU.add))
            dchain(lambda e: e.max(out=m8[:, 0:8], in_=impm[:]))
            dchain(lambda e: e.match_replace(out=impr[:], in_to_replace=m8[:, 0:8], in_values=impm[:], imm_value=-1e9))
            dchain(lambda e: e.max(out=m8[:, 8:16], in_=impr[:]))
            bq = dchain(lambda e: e.tensor_scalar(out=biasq[:], in0=impm[:], scalar1=m8[:, 15:16], scalar2=NEG,
                                                  op0=ALU.is_lt, op1=ALU.mult), [(sp.s_qk, sp.s_qk.n)])
            gsl = gates[:, i, 12 * g:12 * g + 12].rearrange("p (h c) -> p h c", c=3)
            dchain(lambda e, gsl=gsl: e.tensor_tensor(out=coef[:, 0:4], in0=rec[:, 0:4], in1=gsl[:, :, 0], op=ALU.mult))
            dv_last_oc[0] = dchain(lambda e: e.tensor_tensor(out=o_acc[:], in0=oc[:, :, 0:128],
                                                             in1=coef[:, 0:4].unsqueeze(2).broadcast_to([128, 4, 128]), op=ALU.mult))
            wt = [j for j in range(6) if 2 * i - 4 + j >= 0]
            for j in wt:
                kt = 2 * i - 4 + j
                mm = [(kwin[b][:, kt * 128:(kt + 1) * 128], qti)]
                if j not in (2, 3):
                    mm.append((ident[:], wmask[:, j, :].unsqueeze(1).broadcast_to([128, 4, 128])))
                n = pt_n[0]
                pt_n[0] += 1
                pb = n % 3

                def act(ps, w, pb=pb, n=n):
                    ww = list(w)
                    if n >= 3:
                        ww.append((s_pv, pv_of_pt[n - 3]))
                    ph.op("act", lambda e: e.activation(out=PT[pb][:], in_=ps, func=AF.Exp, scale=SCALE), ww, sp.s_ac)

                def post(w, pb=pb, n=n, kt=kt, j=j, b=b, wt=wt):
                    ww = list(w)
                    if j == wt[0]:
                        ww.append((s_dv, dv_last_ow[0]))
                    for r in range(4):
                        v = ph.op("pe", lambda e, r=r: e.matmul(ow[:, r, 0:130], PT[pb][:, r * 128:(r + 1) * 128], vwin[b][:, kt, :],
                                                                 start=(j == wt[0] and r in (0, 2)), stop=(j == wt[-1]),
                                                                 skip_group_check=True),
                                  ww if r == 0 else [], s_pv if r == 3 else None)
                    pv_of_pt[n] = v

                sp.tile(mm, 128, 512, act, post)
            if late[0] is not None:
                late[0]()
                late[0] = None
            def actb(ps, w):
                ph.op("act", lambda e: e.activation(out=BiasT[:], in_=ps, func=AF.Copy), list(w) + [(s_pv, s_pv.n)], sp.s_ac)
            sp.tile([(biasq[:], ident[:])], 64, 128, actb, None, [(s_dv, bq)], three_d=False)
            bias_ready = sp.s_ac.n
            nkt = 2 * i + 2
            for kt in range(nkt):
                mm = [(ksel[b][:, kt * 128:(kt + 1) * 128], qti),
                      (expand[:, kt, :], BiasT[:].unsqueeze(1).broadcast_to([64, 4, 128]))]
                if kt >= 2 * i:
                    mm.append((ident[:], cmaskS[:, kt - 2 * i, :].unsqueeze(1).broadcast_to([128, 4, 128])))
                n = pt_n[0]
                pt_n[0] += 1
                pb = n % 3

                def act(ps, w, pb=pb, n=n):
                    ww = list(w)
                    if n >= 3:
                        ww.append((s_pv, pv_of_pt[n - 3]))
                    ph.op("act", lambda e: e.activation(out=PT[pb][:], in_=ps, func=AF.Exp, scale=SCALE), ww, sp.s_ac)

                def post(w, pb=pb, n=n, kt=kt, nkt=nkt, b=b):
                    ww = list(w)
                    if kt == 0:
                        ww.append((s_dv, dv_last_os[0]))
                    for r in range(4):
                        v = ph.op("pe", lambda e, r=r: e.matmul(osl[:, r, 0:130], PT[pb][:, r * 128:(r + 1) * 128], vsel[b][:, kt, :],
                                                                 start=(kt == 0 and r in (0, 2)), stop=(kt == nkt - 1),
                                                                 skip_group_check=True),
                                  ww if r == 0 else [], s_pv if r == 3 else None)
                    pv_of_pt[n] = v

                sp.tile(mm, 128, 512, act, post, [(sp.s_ac, bias_ready)] if kt == 0 else [])
            sp.flush()
            pv_end = s_pv.n
            dchain(lambda e: e.tensor_scalar(out=dsafe[:, 4:8], in0=osl[:, :, 128], scalar1=1e-30, scalar2=None, op0=ALU.max),
                   [(s_pv, pv_end)])
            dchain(lambda e: e.tensor_scalar(out=dsafe[:, 8:12], in0=ow[:, :, 128], scalar1=1e-30, scalar2=None, op0=ALU.max))
            dchain(lambda e: e.reciprocal(out=rec[:, 4:12], in_=dsafe[:, 4:12]))
            dchain(lambda e, gsl=gsl: e.tensor_tensor(out=coef[:, 4:8], in0=rec[:, 4:8], in1=gsl[:, :, 1], op=ALU.mult))
            dchain(lambda e, gsl=gsl: e.tensor_tensor(out=coef[:, 8:12], in0=rec[:, 8:12], in1=gsl[:, :, 2], op=ALU.mult))
            d1 = dchain(lambda e: e.tensor_tensor(out=tmpo[:], in0=osl[:, :, 0:128],
                                                  in1=coef[:, 4:8].unsqueeze(2).broadcast_to([128, 4, 128]), op=ALU.mult),
                        [(s_pl, s_pl.n)])
            dv_last_os[0] = d1
            p1 = ph.op("pool", lambda e: e.tensor_tensor(out=o_acc[:], in0=o_acc[:], in1=tmpo[:], op=ALU.add), [(s_dv, d1)], s_pl)
            d2 = dchain(lambda e: e.tensor_tensor(out=tmpo[:], in0=ow[:, :, 0:128],
                                                  in1=coef[:, 8:12].unsqueeze(2).broadcast_to([128, 4, 128]), op=ALU.mult),
                        [(s_pl, p1)])
            dv_last_ow[0] = d2
            p2 = ph.op("pool", lambda e: e.tensor_tensor(out=o_bf[:], in0=o_acc[:], in1=tmpo[:], op=ALU.add),
                       [(s_dv, d2), (sp.s_qk, sp.s_qk.n)], s_pl)
            pl_last[0] = p2

            def do_late(g=g, i=i, p2=p2):
                def actT(ps, w):
                    ww = list(w)
                    if i == 0 and g >= 1:
                        ww.append((s_st, 16 * g))
                    ph.op("act", lambda e: e.activation(out=oTs[:, :, i * 128:(i + 1) * 128],
                                                        in_=ps.rearrange("p (r q) -> p r q", r=4), func=AF.Copy), ww, sp.s_ac)
                k = sp.k
                pso = sp.buf[k % 2]
                w0 = [(s_pl, p2)]
                if k >= 2:
                    w0.append((sp.s_ac, sp.ac_of[k - 2]))
                v = 0
                for r in range(4):
                    v = ph.op("pe", lambda e, r=r: e.matmul(pso[:, r * 128:(r + 1) * 128], o_bf[:, r, :], ident[:], start=True, stop=True),
                              w0 if r == 0 else [], sp.s_qk if r == 3 else None)
                actT(pso[:, :], [(sp.s_qk, v)])
                sp.ac_of[k] = sp.s_ac.n
                sp.flush()
                sp.k += 1

            late[0] = do_late
            if "dbg_k" in T and g == 0 and i == 1:
                sd = ph.sem("dbgs")
                w = [(s_pl, p2), (s_dv, s_dv.n)]
                for nm, src in (("dbg_k", ksel[b][:]), ("dbg_kw", kwin[b][:]),
                                ("dbg_v", vsel[b][:].rearrange("p a c -> p (a c)")), ("dbg_vw", vwin[b][:].rearrange("p a c -> p (a c)")),
                                ("dbg_bt", BiasT[:]), ("dbg_imp", impm[:]), ("dbg_ds", dsafe[:]), ("dbg_coef", coef[:]),
                                ("dbg_obf", o_bf[:].rearrange("p a c -> p (a c)")), ("dbg_m8", m8[:]),
                                ("dbg_kct", KCT[:].rearrange("p a c -> p (a c)")), ("dbg_vca", VCA[:].rearrange("p a b c -> p (a b c)")),
                                ("dbg_gates", gates[:].rearrange("p a c -> p (a c)"))):
                    ph.dma("sp", T[nm], src, w, sd)
                ph.wait("sp", [(sd, sd.n)])
        late[0]()
        late[0] = None
        grp_pe_end[g] = s_pv.n
        grp_pe_end[("qk", g)] = sp.s_qk.n
        ph.dma("sp", T["oT"][4 * g:4 * g + 4].rearrange("h p t -> p h t"), oTs[:], [(sp.s_ac, sp.s_ac.n)], s_st)
    ph.wait("sp", [(s_st, 64)])
    ph.run()


def load_act(ph, actT, src, sem, nk=16):
    for k in range(0, nk, 4):
        ph.dma("sp", actT[:, k:k + 4, :], src[k:k + 4].rearrange("k p t -> p k t"), [], sem)
    return sem.n


def phase_proj(nc, name, actT, wsrc, mode, T, C, src_act=None, xsrc=None):
    ph = Phase(nc, name)
    G = Gemm(ph, 16)
    fm = T["fm"]
    s_a = ph.sem("a")
    av = load_act(ph, actT, src_act, s_a) if src_act is not None else 0
    yf = [ph.sb(f"yf{i}", [128, 512], F32) for i in range(2)]
    s_po = ph.sem("po")
    s_l = [ph.sem("l"), ph.sem("l")]
    if mode == "resid":
        so = SlabOut(ph, F32, 2, "xs")
        xl = [ph.sb(f"xl{i}", [128, NT], F32) for i in range(2)]
        aux = aux2 = None
    else:
        so = SlabOut(ph, BF, 2)
        aux = [ph.sb(f"ax{i}", [128, NT], BF) for i in range(2)]
        aux2 = [ph.sb(f"ay{i}", [128, NT], BF) for i in range(2)] if mode == "conv" else None
        tt = [ph.sb(f"tt{i}", [128, 512], F32) for i in range(2)]
    s_p2 = ph.sem("p2")
    tn = 0
    post_of = {}
    lv_of = {}

    def issue_load(c):
        sl = c % 2
        lw = [(s_po, post_of.get(c - 2, 0))]
        if mode == "resid":
            ph.dma("sp", xl[sl][:], xsrc[c], lw, s_l[sl])
        else:
            ph.dma("sp", aux[sl][:], fm[(48 if mode == "attn" else 64) + c], lw, s_l[sl])
            if mode == "conv":
                ph.dma("sp", aux2[sl][:], T["maT"][c], lw, s_l[sl])
        lv_of[c] = s_l[sl].n

    for j in range(4):
        b, wv = G.load_w(wsrc[:, j * 512:(j + 1) * 512])
        for cb in range(4):
            c = 4 * j + cb
            slot, free_w = so.begin()
            if c == 0:
                issue_load(0)
            if c + 1 < 16:
                issue_load(c + 1)
            lv = lv_of[c]
            last = 0
            for tb in range(4):
                mm = [(G.wb[b][:, kc, cb * 128:(cb + 1) * 128], actT[:, kc, tb * 512:(tb + 1) * 512]) for kc in range(16)]
                fw = [(G.s_w[b], wv), (s_a, av)] if (cb == 0 and tb == 0) else []
                yb = tn % 2
                tsl = slice(tb * 512, (tb + 1) * 512)

                def evac(ps, w, yb=yb, tn=tn):
                    ww = list(w) + [(s_po, post_of.get(("t", tn - 2), 0))]
                    ph.op("act", lambda e: e.activation(out=yf[yb][:], in_=ps, func=AF.Copy), ww, G.s_ep)

                G.tile(mm, evac, fw)
                ev = G.s_ep.n
                w0 = [(G.s_ep, ev), (s_l[slot], lv)] + (free_w if tb == 0 else [])
                if mode == "attn":
                    last = ph.op("pool", lambda e, yb=yb, slot=slot, tsl=tsl: e.tensor_tensor(
                        out=so.buf[slot][:, tsl], in0=yf[yb][:], in1=aux[slot][:, tsl], op=ALU.mult), w0, s_po)
                elif mode == "conv":
                    p = ph.op("pool", lambda e, yb=yb, slot=slot, tsl=tsl: e.tensor_tensor(
                        out=tt[yb][:], in0=yf[yb][:], in1=aux[slot][:, tsl], op=ALU.mult),
                        w0 + [(s_po, post_of.get(("t", tn - 2), 0))], s_p2)
                    last = ph.op("dve", lambda e, yb=yb, slot=slot, tsl=tsl: e.tensor_tensor(
                        out=so.buf[slot][:, tsl], in0=tt[yb][:], in1=aux2[slot][:, tsl], op=ALU.add), [(s_p2, p)], s_po)
                else:
                    last = ph.op("dve", lambda e, yb=yb, slot=slot, tsl=tsl: e.tensor_tensor(
                        out=so.buf[slot][:, tsl], in0=yf[yb][:], in1=xl[slot][:, tsl], op=ALU.add), w0, s_po)
                post_of[("t", tn)] = last
                tn += 1
            post_of[c] = last
            dst = T["xres"][c] if mode == "resid" else (T["maT"][c] if mode == "attn" else T["mT"][c])
            so.store(slot, dst, [(s_po, last)])
        G.end_job()
    so.drain()
    ph.run()


def phase_conv(nc, name, actT, W, L, C, T):
    ph = Phase(nc, name)
    fm, gt = T["fm"], T["gt"]
    xi = [ph.sb(f"xi{i}", [128, NT], BF) for i in range(2)]
    gc = [ph.sb(f"gc{i}", [128, NT], BF) for i in range(2)]
    gb = [ph.sb(f"gb{i}", [128, NT], BF) for i in range(2)]
    ub = [ph.sb(f"ub{i}", [128, 16, 130], F32) for i in range(2)]
    acc = [ph.sb(f"acc{i}", [128, 16, 128], F32) for i in range(2)]
    H0 = ph.sb("H0", [128, 16, 16, 2], BF)
    H1 = ph.sb("H1", [128, 16, 16, 2], BF)
    Ht = ph.sb("Ht", [128, 16, 16, 2], F32)
    halo = ph.sb("halo", [128, 16, 16, 2], F32)
    cw = ph.sb("cw", [128, 16, 3], F32)
    fl = ph.sb("fl", [128, 2], F32)
    s_c = ph.sem("c")
    s_l = [ph.sem("l"), ph.sem("l")]
    s_pl = ph.sem("pl")
    s_dv = ph.sem("dv")
    m0 = ph.op("dve", lambda e: e.memset(H0[:], 0.0), [], s_dv)
    ph.dma("sp", H0[:, :, 1:16, :].rearrange("p c i t -> p c (i t)"),
           gt[2048:4096, 0:30].rearrange("(c p) x -> p c x", p=128), [(s_dv, m0)], s_c)
    ph.dma("sp", H1[:].rearrange("p c i t -> p c (i t)"), gt[0:2048, :].rearrange("(c p) x -> p c x", p=128), [], s_c)
    ph.dma("sp", cw[:], W["conv_wT"][L], [], s_c)
    ph.dma("sp", fl[:], C["flags"], [], s_c)
    h1 = ph.op("dve", lambda e: e.tensor_scalar(out=Ht[:], in0=H0[:], scalar1=fl[:, 0:1], scalar2=None, op0=ALU.mult),
               [(s_c, 64)], s_dv)
    h2 = ph.op("dve", lambda e: e.scalar_tensor_tensor(out=halo[:], in0=H1[:], scalar=fl[:, 1:2], in1=Ht[:],
                                                       op0=ALU.mult, op1=ALU.add), [(s_dv, h1)], s_dv)
    dv_of = {}
    pl_of = {}
    for c in range(16):
        b = c % 2
        w = [(s_dv, dv_of[c - 2])] if c >= 2 else []
        for dst, off in ((xi[b], 0), (gb[b], 16), (gc[b], 32)):
            ph.dma("sp", dst[:], fm[off + c], w, s_l[b])
        lv = s_l[b].n
        ph.op("pool", lambda e, b=b, c=c: e.tensor_copy(out=ub[b][:, :, 0:2], in_=halo[:, c, :, :]),
              [(s_dv, h2)] + w, None)
        pl_of[c] = ph.op("pool", lambda e, b=b: e.tensor_tensor(out=ub[b][:, :, 2:130],
                                                                in0=xi[b][:].rearrange("p (i j) -> p i j", j=128),
                                                                in1=gc[b][:].rearrange("p (i j) -> p i j", j=128), op=ALU.mult),
                         [(s_l[b], lv)], s_pl)
        d = ph.op("dve", lambda e, b=b, c=c: e.tensor_scalar(out=acc[b][:], in0=ub[b][:, :, 2:130], scalar1=cw[:, c, 2:3],
                                                             scalar2=None, op0=ALU.mult), [(s_pl, pl_of[c])], s_dv)
        d = ph.op("dve", lambda e, b=b, c=c: e.scalar_tensor_tensor(out=acc[b][:], in0=ub[b][:, :, 1:129], scalar=cw[:, c, 1:2],
                                                                    in1=acc[b][:], op0=ALU.mult, op1=ALU.add), [(s_dv, d)], s_dv)
        d = ph.op("dve", lambda e, b=b, c=c: e.scalar_tensor_tensor(out=acc[b][:], in0=ub[b][:, :, 0:128], scalar=cw[:, c, 0:1],
                                                                    in1=acc[b][:], op0=ALU.mult, op1=ALU.add), [(s_dv, d)], s_dv)
        dv_of[c] = ph.op("dve", lambda e, b=b, c=c: e.tensor_tensor(out=actT[:, c, :].rearrange("p (i j) -> p i j", j=128),
                                                                    in0=acc[b][:], in1=gb[b][:].rearrange("p (i j) -> p i j", j=128),
                                                                    op=ALU.mult), [(s_dv, d)], s_dv)
    ph.wait("sp", [(s_dv, s_dv.n)])
    ph.run()


def phase_ffn(nc, name, actT, W, L, T):
    ph = Phase(nc, name)
    Gu = Gemm(ph, 16)
    Gd = Gemm(ph, 8, pfx="d")
    fT = ph.sb("fT", [128, 8, NT], BF)
    rf = [ph.sb(f"rf{i}", [128, 512], F32) for i in range(2)]
    yf = [ph.sb(f"yf{i}", [128, 512], F32) for i in range(2)]
    so = SlabOut(ph, F32, 2, "xs")
    xl = [ph.sb(f"xl{i}", [128, NT], F32) for i in range(2)]
    s_pl = ph.sem("pl")
    s_po = ph.sem("po")
    s_l = [ph.sem("l"), ph.sem("l")]
    xres = T["xres"]
    un = 0
    dn = 0
    pl_of = {}
    po_of = {}
    slab_store = {}
    slab_n = 0
    slab_last = {}
    lv_of = {}

    def issue_load(n):
        sl = n % 2
        c = n % 16
        lw = [(s_po, slab_last.get(n - 2, 0))]
        if c in slab_store:
            lw.append(slab_store[c])
        ph.dma("sp", xl[sl][:], xres[c], lw, s_l[sl])
        lv_of[n] = s_l[sl].n

    for hg in range(8):
        for j in range(2):
            b, wv = Gu.load_w(W["w_up"][L][:, hg * 1024 + j * 512: hg * 1024 + (j + 1) * 512])
            for cb in range(4):
                for tb in range(4):
                    mm = [(Gu.wb[b][:, kc, cb * 128:(cb + 1) * 128], actT[:, kc, tb * 512:(tb + 1) * 512]) for kc in range(16)]
                    fw = [(Gu.s_w[b], wv)] if (cb == 0 and tb == 0) else []
                    rb = un % 2

                    def evac(ps, w, rb=rb, un=un):
                        ph.op("act", lambda e: e.activation(out=rf[rb][:], in_=ps, func=AF.Relu),
                              list(w) + [(s_pl, pl_of.get(un - 2, 0))], Gu.s_ep)

                    Gu.tile(mm, evac, fw)
                    pl_of[un] = ph.op("pool", lambda e, rb=rb, j=j, cb=cb, tb=tb: e.tensor_tensor(
                        out=fT[:, j * 4 + cb, tb * 512:(tb + 1) * 512], in0=rf[rb][:], in1=rf[rb][:], op=ALU.mult),
                        [(Gu.s_ep, Gu.s_ep.n), (Gd.s_pe, Gd.s_pe.n)], s_pl)
                    un += 1
            Gu.end_job()
        f_ready = s_pl.n
        for j in range(4):
            b, wv = Gd.load_w(W["w_down"][L][hg * 1024:(hg + 1) * 1024, j * 512:(j + 1) * 512])
            for cb in range(4):
                c = 4 * j + cb
                slot, free_w = so.begin()
                if slab_n == 0:
                    issue_load(0)
                if slab_n + 1 < 128:
                    issue_load(slab_n + 1)
                lv = lv_of[slab_n]
                last = 0
                for tb in range(4):
                    mm = [(Gd.wb[b][:, kc, cb * 128:(cb + 1) * 128], fT[:, kc, tb * 512:(tb + 1) * 512]) for kc in range(8)]
                    fw = [(Gd.s_w[b], wv), (s_pl, f_ready)] if (cb == 0 and tb == 0) else []
                    yb = dn % 2
                    tsl = slice(tb * 512, (tb + 1) * 512)

                    def evac(ps, w, yb=yb, dn=dn):
                        ph.op("act", lambda e: e.activation(out=yf[yb][:], in_=ps, func=AF.Copy),
                              list(w) + [(s_po, po_of.get(dn - 2, 0))], Gd.s_ep)

                    Gd.tile(mm, evac, fw)
                    last = ph.op("dve", lambda e, yb=yb, slot=slot, tsl=tsl: e.tensor_tensor(
                        out=so.buf[slot][:, tsl], in0=yf[yb][:], in1=xl[slot][:, tsl], op=ALU.add),
                        [(Gd.s_ep, Gd.s_ep.n), (s_l[slot], lv)] + (list(free_w) if tb == 0 else []), s_po)
                    po_of[dn] = last
                    dn += 1
                slab_last[slab_n] = last
                slab_n += 1
                v = so.store(slot, xres[c], [(s_po, last)])
                slab_store[c] = (so.s_st[slot], v)
            Gd.end_job()
    so.drain()
    ph.run()


WEIGHT_USERS = {
    "w_in": ("win",), "cmp_w1_k": ("cmp",), "cmp_w2_k": ("cmp",), "cmp_pos_kT": ("cmp",),
    "cmp_w1_v": ("cmp",), "cmp_w2_v": ("cmp",), "cmp_pos_vT": ("cmp",), "conv_wT": ("cv",),
    "w_attn_proj": ("ap",), "w_conv_out": ("co",), "w_o": ("wo",), "w_up": ("ffn",), "w_down": ("ffn",),
    "norm1_gT": ("n1",), "norm2_gT": ("n2",), "final_gT": ("nf",),
}
LAST_SHAPES = {}


def build(sel=None, dbg=()):
    nc = bass.Bass("TRN2", target_bir_lowering=False)

    def on(L, nm):
        return sel is None or (L, nm) in sel

    def din(nm, shape, dt=F32):
        LAST_SHAPES[nm] = tuple(shape)
        return nc.dram_tensor(nm, shape, dt, kind="ExternalInput").ap()

    def scr(nm, shape, dt=BF):
        if nm in dbg:
            t = nc.dram_tensor(nm, shape, dt, kind="ExternalOutput")
        else:
            t = nc.dram_tensor(nm, shape, dt)
        return t

    W = {}
    for nm, shp in (("w_in", [DEPTH, D, IN_COLS]),
                    ("cmp_w1_k", [DEPTH, 4096, 256]), ("cmp_w2_k", [DEPTH, 256, 128]), ("cmp_pos_kT", [DEPTH, 128, 32]),
                    ("cmp_w1_v", [DEPTH, 4096, 256]), ("cmp_w2_v", [DEPTH, 256, 128]), ("cmp_pos_vT", [DEPTH, 128, 32]),
                    ("conv_wT", [DEPTH, 128, 16, 3]), ("w_attn_proj", [DEPTH, D, D]), ("w_conv_out", [DEPTH, D, D]),
                    ("w_o", [DEPTH, D, D]), ("w_up", [DEPTH, D, DFF]), ("w_down", [DEPTH, DFF, D]),
                    ("norm1_gT", [DEPTH, 128, 16]), ("norm2_gT", [DEPTH, 128, 16]), ("final_gT", [128, 16])):
        users = WEIGHT_USERS[nm]
        used = sel is None or any((L, u) in sel for L in range(DEPTH + 1) for u in users)
        if not used:
            shp = [1] * len(shp)
        elif sel is not None and nm != "final_gT" and not any((1, u) in sel for u in users):
            shp = [1] + list(shp[1:])
        W[nm] = din(nm, shp)
    xin = din("xT", [16, 128, NT])
    Cd = {}
    for nm, shp, dt in (("cos", [128, NT], F32), ("sin", [128, NT], F32), ("cosc", [128, 256], F32), ("sinc", [128, 256], F32),
                        ("RT", [128, 128], BF), ("ident", [128, 128], BF), ("ones", [128, 128], BF), ("msel", [128, 2, 64], BF),
                        ("cmpmask", [128, 2, NT], BF), ("fb", [128, 16, 64], F32), ("expand", [64, 32, 128], BF),
                        ("cmaskS", [128, 2, 128], BF), ("wmask", [128, 6, 128], BF), ("flags", [128, 2], F32)):
        Cd[nm] = din("c_" + nm, shp, dt)
    outT = nc.dram_tensor("outT", [16, 128, NT], F32, kind="ExternalOutput").ap()

    T = {}
    T["qT"] = scr("qT", [16, 128, NT]).ap()
    T["fm"] = scr("fm", [80, 128, NT]).ap()
    for nm, shp in (("gk_in", [2048, NT]), ("gk", [4096, NT]), ("gv_in", [2048, 1024]), ("gv", [4096, 1024]),
                    ("gt_in", [2048, 32]), ("gt", [4096, 32])):
        t = scr(nm, shp)
        T[nm + "_t"] = t
        T[nm] = t.ap()
    T["oT"] = scr("oT", [16, 128, NT]).ap()
    T["maT"] = scr("maT", [16, 128, NT]).ap()
    T["mT"] = scr("mT", [16, 128, NT]).ap()
    T["xres"] = scr("xres", [16, 128, NT], F32).ap()
    if "dbg_k" in dbg:
        for nm, shp, dt in (("dbg_k", [128, S], BF), ("dbg_kw", [128, S], BF), ("dbg_v", [128, 32 * 130], BF),
                            ("dbg_vw", [128, 32 * 130], BF), ("dbg_bt", [64, 128], BF), ("dbg_imp", [128, 64], F32),
                            ("dbg_ds", [128, 12], F32), ("dbg_coef", [128, 12], F32), ("dbg_obf", [128, 512], BF),
                            ("dbg_m8", [128, 16], F32), ("dbg_kct", [128, 1024], BF), ("dbg_vca", [128, 4 * 2 * 194], BF),
                            ("dbg_gates", [128, 16 * 48], F32), ("dbg_osl", [128, 1024], F32), ("dbg_ow", [128, 1024], F32),
                            ("dbg_oacc", [128, 512], F32), ("dbg_tmpo", [128, 512], F32)):
            T[nm] = scr(nm, shp, dt).ap()

    with contextlib.ExitStack() as es:
        del SEM_POOL[:]
        for i in range(24):
            SEM_POOL.append([es.enter_context(nc.semaphore(f"sem{i}")), 0])
        actT = es.enter_context(nc.sbuf_tensor("actT", [128, 16, NT], BF))
        gates = es.enter_context(nc.sbuf_tensor("gates", [128, 16, 48], F32))
        KCT = es.enter_context(nc.sbuf_tensor("KCT", [128, 4, 256], BF))
        VCA = es.enter_context(nc.sbuf_tensor("VCA", [128, 4, 2, 194], BF))
        RT = es.enter_context(nc.sbuf_tensor("RTs", [128, 128], BF))
        ident = es.enter_context(nc.sbuf_tensor("idents", [128, 128], BF))
        ones = es.enter_context(nc.sbuf_tensor("oness", [128, 128], BF))
        T["gates"] = gates
        C = dict(Cd)
        C["RT"], C["ident"], C["ones"] = RT, ident, ones
        ph = Phase(nc, "init")
        s = ph.sem("s")
        ph.dma("sp", RT[:], Cd["RT"], [], s)
        ph.dma("sp", ident[:], Cd["ident"], [], s)
        ph.dma("sp", ones[:], Cd["ones"], [], s)
        ph.wait("sp", [(s, 48)])
        ph.run()
        xcur = xin
        for L in range(DEPTH):
            if on(L, "n1"):
                phase_norm(nc, f"n1_{L}", xcur, W["norm1_gT"][L], ones, dst_sb=actT)
            if on(L, "win"):
                phase_win(nc, f"win{L}", actT, W["w_in"][L], C, T)
            if on(L, "ag"):
                phase_gather(nc, f"ag{L}", T)
            if on(L, "cmp"):
                phase_compress(nc, f"cmp{L}", L, W, C, T, KCT, VCA)
            if on(L, "att"):
                phase_attn(nc, f"att{L}", C, T, KCT, VCA, gates)
            if on(L, "ap"):
                phase_proj(nc, f"ap{L}", actT, W["w_attn_proj"][L], "attn", T, C, src_act=T["oT"])
            if on(L, "cv"):
                phase_conv(nc, f"cv{L}", actT, W, L, C, T)
            if on(L, "co"):
                phase_proj(nc, f"co{L}", actT, W["w_conv_out"][L], "conv", T, C)
            if on(L, "wo"):
                phase_proj(nc, f"wo{L}", actT, W["w_o"][L], "resid", T, C, src_act=T["mT"], xsrc=xcur)
            xcur = T["xres"]
            if on(L, "n2"):
                phase_norm(nc, f"n2_{L}", xcur, W["norm2_gT"][L], ones, dst_sb=actT)
            if on(L, "ffn"):
                phase_ffn(nc, f"ffn{L}", actT, W, L, T)
        if on(DEPTH, "nf"):
            phase_norm(nc, "nf", xcur, W["final_gT"], ones, dst_dram=outT)
    return nc


def _bf(a):
    return np.ascontiguousarray(a.astype(ml_dtypes.bfloat16))


def _consts(p):
    f32 = np.float32
    tl = np.arange(NT)
    pos = (128 * (2 * (tl // 128) + p) + tl % 128)
    half = 64
    inv_freq = np.exp(-math.log(10000.0) * np.arange(half, dtype=f32) / half).astype(f32)
    ang = pos.astype(f32)[None, :] * inv_freq[:, None]
    c = {}
    c["cos"] = np.concatenate([np.cos(ang), np.cos(ang)], 0).astype(f32)
    c["sin"] = np.concatenate([np.sin(ang), np.sin(ang)], 0).astype(f32)
    cpos = (np.arange(256) * 16 + 31).astype(f32)
    angc = cpos[None, :] * inv_freq[:, None]
    c["cosc"] = np.concatenate([np.cos(angc), np.cos(angc)], 0).astype(f32)
    c["sinc"] = np.concatenate([np.sin(angc), np.sin(angc)], 0).astype(f32)
    RT = np.zeros((128, 128), f32)
    for d in range(64):
        RT[d + 64, d] = -1.0
        RT[d, d + 64] = 1.0
    c["RT"] = _bf(RT)
    c["ident"] = _bf(np.eye(128, dtype=f32))
    c["ones"] = _bf(np.ones((128, 128), f32))
    n = np.arange(256)
    cs = n * 16
    ss = np.arange(64) * 64
    ov = np.minimum(cs[:, None] + 32, ss[None, :] + 64) - np.maximum(cs[:, None], ss[None, :])
    msel = (np.clip(ov, 0, None) / 32.0).astype(f32)
    msel[255] = 0.0
    c["msel"] = _bf(msel.reshape(2, 128, 64).transpose(1, 0, 2))
    cm = np.where((n[:, None] * 16 + 31 <= pos[None, :]) & (n[:, None] < 255), 0.0, NEG).astype(f32)
    c["cmpmask"] = _bf(cm.reshape(2, 128, NT).transpose(1, 0, 2))
    tb = pos // 64
    m = np.arange(64)
    valid = m[None, :] <= tb[:, None]
    forced = (m[None, :] == 0) | (m[None, :] == tb[:, None]) | (m[None, :] == tb[:, None] - 1)
    fb = np.where(valid, np.where(forced, 1e4, 0.0), -1e4).astype(f32)
    c["fb"] = np.ascontiguousarray(fb.reshape(16, 128, 64).transpose(1, 0, 2))
    ex = np.zeros((64, 32, 128), f32)
    for kt in range(32):
        ex[2 * kt, kt, 0:64] = 1.0
        ex[2 * kt + 1, kt, 64:128] = 1.0
    c["expand"] = _bf(ex)
    k = np.arange(128)
    causal = np.where(k[:, None] <= k[None, :], 0.0, NEG).astype(f32)
    allm = np.full((128, 128), NEG, f32)
    zero = np.zeros((128, 128), f32)
    anti = np.where(k[:, None] > k[None, :], 0.0, NEG).astype(f32)
    if p == 0:
        cms = [causal, allm]
        wm = [anti, zero, zero, zero, causal, allm]
    else:
        cms = [zero, causal]
        wm = [allm, anti, zero, zero, zero, causal]
    c["cmaskS"] = _bf(np.stack(cms, 1))
    c["wmask"] = _bf(np.stack(wm, 1))
    fl = np.zeros((128, 2), f32)
    fl[:, p] = 1.0
    c["flags"] = fl
    return c


def kernel(**inputs):
    f32 = np.float32
    x = np.asarray(inputs["x"], f32)

    def gT(a):
        a = np.asarray(a, f32)
        return np.ascontiguousarray(np.swapaxes(a.reshape(a.shape[:-1] + (16, 128)), -1, -2))

    shared = {
        "w_in": np.asarray(inputs["w_in"], f32),
        "cmp_w1_k": np.asarray(inputs["cmp_w1_k"], f32), "cmp_w2_k": np.asarray(inputs["cmp_w2_k"], f32),
        "cmp_pos_kT": np.ascontiguousarray(np.swapaxes(np.asarray(inputs["cmp_pos_k"], f32), 1, 2)),
        "cmp_w1_v": np.asarray(inputs["cmp_w1_v"], f32), "cmp_w2_v": np.asarray(inputs["cmp_w2_v"], f32),
        "cmp_pos_vT": np.ascontiguousarray(np.swapaxes(np.asarray(inputs["cmp_pos_v"], f32), 1, 2)),
        "conv_wT": np.ascontiguousarray(np.asarray(inputs["conv_w"], f32).reshape(DEPTH, 3, 16, 128).transpose(0, 3, 2, 1)),
        "w_attn_proj": np.asarray(inputs["w_attn_proj"], f32), "w_conv_out": np.asarray(inputs["w_conv_out"], f32),
        "w_o": np.asarray(inputs["w_o"], f32), "w_up": np.asarray(inputs["w_up"], f32), "w_down": np.asarray(inputs["w_down"], f32),
        "norm1_gT": gT(inputs["norm1_g"]), "norm2_gT": gT(inputs["norm2_g"]), "final_gT": gT(inputs["final_g"]),
    }
    consts = [_consts(0), _consts(1)]
    in_maps = []
    for c in range(8):
        b, p = c // 2, c % 2
        xo = x[b].reshape(16, 2, 128, D)[:, p].reshape(NT, D)
        m = dict(shared)
        m["xT"] = np.ascontiguousarray(xo.T.reshape(16, 128, NT))
        for k, v in consts[p].items():
            m["c_" + k] = v
        in_maps.append(m)
    nc = build()
    res = run_bass_kernel_spmd(nc, in_maps, core_ids=list(range(8)))
    out = np.empty((4, S, D), f32)
    for c in range(8):
        b, p = c // 2, c % 2
        o = np.asarray(res.results[c]["outT"], f32).reshape(D, NT).T
        out[b].reshape(16, 2, 128, D)[:, p] = o.reshape(16, 128, D)
    return out
```

```python
import contextlib
import math

import ml_dtypes
import numpy as np

import concourse.bass as bass
import concourse.mybir as mybir
from concourse.bass_utils import run_bass_kernel_spmd

F32 = mybir.dt.float32
BF = mybir.dt.bfloat16
ALU = mybir.AluOpType
AF = mybir.ActivationFunctionType

D = 2048
S = 4096
NT = 2048
DEPTH = 2
NH = 16
DFF = 8192
IN_COLS = 15408
NEG = -30000.0
PAIRS = [[0, 1], [2, 3], [4, 5], [6, 7]]
SCALE = 128 ** -0.5


def gk_row(r, slab):
    return (slab // 4) * 1024 + r * 512 + (slab % 4) * 128


def gv_row(r, t):
    return (t // 1024) * 2048 + r * 1024 + (t % 1024)


class Sem:
    def __init__(self, slot):
        self.slot = slot
        self.h = slot[0]
        self.base = slot[1]
        self.n = 0


SEM_POOL = []


class Phase:
    def __init__(self, nc, name):
        self.nc = nc
        self.name = name
        self.es = contextlib.ExitStack()
        self.q = {e: [] for e in ("pe", "act", "dve", "pool", "sp")}
        self.k = 0
        self.sems = []

    def sem(self, nm):
        sm = Sem(SEM_POOL[self.k])
        self.k += 1
        self.sems.append(sm)
        return sm

    def sb(self, nm, shape, dt):
        return self.es.enter_context(self.nc.sbuf_tensor(f"{self.name}_{nm}", shape, dt))

    def ps(self, nm, shape, dt=F32):
        return self.es.enter_context(self.nc.psum_tensor(f"{self.name}_{nm}", shape, dt))

    def op(self, eng, fn, waits=(), sig=None, inc=1):
        waits = [(s, v) for (s, v) in waits if v > 0]

        def thunk(e, fn=fn, waits=waits, sig=sig, inc=inc):
            for s, v in waits:
                e.wait_ge(s.h, s.base + v)
            ins = fn(e)
            if sig is not None:
                ins.then_inc(sig.h, inc)

        self.q[eng].append(thunk)
        if sig is not None:
            sig.n += inc
            return sig.n
        return 0

    def dma(self, eng, out, in_, waits=(), sig=None):
        return self.op(eng, lambda e, out=out, in_=in_: e.dma_start(out=out, in_=in_), waits, sig, 16)

    def wait(self, eng, waits):
        waits = [(s, v) for (s, v) in waits if v > 0]

        def thunk(e, waits=waits):
            for s, v in waits:
                e.wait_ge(s.h, s.base + v)

        self.q[eng].append(thunk)

    def run(self):
        q = self.q
        with self.nc.Block() as block:
            @block.tensor
            def _(e):
                for f in q["pe"]:
                    f(e)

            @block.scalar
            def _(e):
                for f in q["act"]:
                    f(e)

            @block.vector
            def _(e):
                for f in q["dve"]:
                    f(e)

            @block.gpsimd
            def _(e):
                for f in q["pool"]:
                    f(e)

            @block.sync
            def _(e):
                for f in q["sp"]:
                    f(e)
        for sm in self.sems:
            sm.slot[1] += sm.n
        self.es.close()


def phase_norm(nc, name, xsrc, gain_ap, ones_bf, dst_sb=None, dst_dram=None):
    ph = Phase(nc, name)
    xb = [ph.sb(f"xb{i}", [128, 16, 512], F32) for i in range(2)]
    sq = ph.sb("sq", [128, 16, 512], BF)
    g = ph.sb("g", [128, 16], F32)
    tmp = ph.sb("tmp", [128, 512], F32)
    tmp2 = ph.sb("tmp2", [128, 512], F32)
    rstd = ph.sb("rstd", [128, 512], F32)
    pss = ph.ps("ss", [128, 512])
    ob = ph.sb("ob", [128, 16, 512], F32) if dst_dram is not None else None
    s_ld = [ph.sem("ld"), ph.sem("ld")]
    s_g = ph.sem("g")
    s_act = ph.sem("act")
    s_pe = ph.sem("pe")
    s_dve = ph.sem("dve")
    s_st = ph.sem("st")
    ph.dma("sp", g[:], gain_ap, sig=s_g)
    dve_done = {}
    pe_done = {}
    dve_a = {}
    for tb in range(4):
        b = tb % 2
        w = [(s_dve, dve_done[tb - 2])] if tb >= 2 else []
        ld = ph.dma("sp", xb[b][:], xsrc[:, :, tb * 512:(tb + 1) * 512].rearrange("k p t -> p k t"), w, s_ld[b])
        for kc in range(16):
            w = []
            if kc == 0:
                w = [(s_ld[b], ld)]
                if tb >= 1:
                    w.append((s_pe, pe_done[tb - 1]))
            a_sq = ph.op("act", lambda e, kc=kc, b=b: e.activation(out=sq[:, kc, :], in_=xb[b][:, kc, :], func=AF.Square),
                         w, s_act if kc == 15 else None)
        for kc in range(16):
            w = []
            if kc == 0:
                w = [(s_act, a_sq)]
                if tb >= 1:
                    w.append((s_dve, dve_a[tb - 1]))
            pe_done[tb] = ph.op("pe", lambda e, kc=kc: e.matmul(pss[:], ones_bf[:], sq[:, kc, :], start=(kc == 0), stop=(kc == 15)),
                                w, s_pe if kc == 15 else None)
        dve_a[tb] = ph.op("dve", lambda e: e.tensor_scalar(out=tmp[:], in0=pss[:], scalar1=1.0 / D, scalar2=1e-6,
                                                           op0=ALU.mult, op1=ALU.add),
                          [(s_pe, pe_done[tb])], s_dve)
        a_sqrt = ph.op("act", lambda e: e.activation(out=tmp2[:], in_=tmp[:], func=AF.Sqrt), [(s_dve, dve_a[tb])], s_act)
        d_b = ph.op("dve", lambda e: e.reciprocal(out=rstd[:], in_=tmp2[:]), [(s_act, a_sqrt)], s_dve)
        for kc in range(16):
            w = []
            if kc == 0:
                w = [(s_dve, d_b), (s_g, 16)]
                if ob is not None and tb >= 1:
                    w.append((s_st, 16 * tb))
            if ob is None:
                o = dst_sb[:, kc, tb * 512:(tb + 1) * 512]
            else:
                o = ob[:, kc, :]
            dve_done[tb] = ph.op("dve", lambda e, kc=kc, b=b, o=o: e.scalar_tensor_tensor(
                out=o, in0=xb[b][:, kc, :], scalar=g[:, kc:kc + 1], in1=rstd[:], op0=ALU.mult, op1=ALU.mult),
                w, s_dve if kc == 15 else None)
        if ob is not None:
            ph.dma("sp", dst_dram[:, :, tb * 512:(tb + 1) * 512].rearrange("k p t -> p k t"), ob[:],
                   [(s_dve, dve_done[tb])], s_st)
    if ob is not None:
        ph.wait("sp", [(s_st, 64)])
    ph.run()


class Gemm:
    NPS = 4

    def __init__(self, ph, nk, wcols=512, pfx=""):
        self.ph = ph
        self.nk = nk
        self.wb = [ph.sb(f"{pfx}w{i}", [128, nk, wcols], BF) for i in range(2)]
        self.s_w = [ph.sem("w"), ph.sem("w")]
        self.s_pe = ph.sem("gpe")
        self.s_ep = ph.sem("gep")
        self.psb = [ph.ps(f"{pfx}g{i}", [128, 512]) for i in range(self.NPS)]
        self.T = 0
        self.J = 0
        self.job_pe_end = {}
        self.ep_of_tile = {}

    def plan(self, wsrcs):
        self.wsrcs = list(wsrcs)
        self.loaded = {}
        self.nxt = 0
        self.cur = 0

    def next_weights(self):
        J = self.nxt
        self.nxt += 1
        self.cur = J
        if J not in self.loaded:
            self.loaded[J] = self.load_w(self.wsrcs[J])
        if J + 1 < len(self.wsrcs) and (J + 1) not in self.loaded:
            self.loaded[J + 1] = self.load_w(self.wsrcs[J + 1])
        return self.loaded[J]

    def load_w(self, wsrc):
        ph = self.ph
        J = self.J
        b = J % 2
        ncols = wsrc.shape[1]
        w = [(self.s_pe, self.job_pe_end[J - 2])] if J >= 2 else []
        v = ph.dma("pool", self.wb[b][:, :, 0:ncols], wsrc.rearrange("(k p) c -> p k c", p=128), w, self.s_w[b])
        self.J += 1
        return b, v

    def tile(self, mm_list, evac, first_waits=()):
        ph = self.ph
        T = self.T
        ps = self.psb[T % self.NPS]
        n = len(mm_list)
        M = mm_list[0][0].shape[-1]
        N = mm_list[0][1].shape[-1]
        pso = ps[0:M, 0:N]
        for k, (l, r) in enumerate(mm_list):
            w = []
            if k == 0:
                w = list(first_waits)
                if T >= self.NPS:
                    w.append((self.s_ep, self.ep_of_tile[T - self.NPS]))
            v = ph.op("pe", lambda e, l=l, r=r, k=k, pso=pso: e.matmul(pso, l, r, start=(k == 0), stop=(k == n - 1)),
                      w, self.s_pe if k == n - 1 else None)
        before = self.s_ep.n
        evac(pso, [(self.s_pe, v)])
        assert self.s_ep.n == before + 1
        self.ep_of_tile[T] = self.s_ep.n
        self.T += 1
        return T

    def end_job(self):
        self.job_pe_end[self.cur] = self.s_pe.n


class SlabOut:
    def __init__(self, ph, dt=BF, nbuf=2, name="stg"):
        self.ph = ph
        self.buf = [ph.sb(f"{name}{i}", [128, NT], dt) for i in range(nbuf)]
        self.s_st = [ph.sem("st") for _ in range(nbuf)]
        self.s_x = [ph.sem("sx") for _ in range(nbuf)]
        self.n = 0

    def begin(self):
        i = self.n % len(self.buf)
        self.n += 1
        return i, [(self.s_st[i], self.s_st[i].n), (self.s_x[i], self.s_x[i].n)]

    def store(self, i, dst, waits, eng="sp"):
        return self.ph.dma(eng, dst, self.buf[i][:], waits, self.s_st[i])

    def drain(self, eng="sp"):
        self.ph.wait(eng, [(s, s.n) for s in self.s_st])


def phase_win(nc, name, actT, w_in_l, C, T):
    ph = Phase(nc, name)
    G = Gemm(ph, 16)
    so = SlabOut(ph)
    cosT = ph.sb("cos", [128, NT], F32)
    sinT = ph.sb("sin", [128, NT], F32)
    zb = [ph.sb(f"zb{i}", [128, 512], BF) for i in range(2)]
    t1 = [ph.sb(f"t1{i}", [128, 512], F32) for i in range(2)]
    t2 = [ph.sb(f"t2{i}", [128, 512], F32) for i in range(2)]
    ps2 = [ph.ps(f"r{i}", [128, 512]) for i in range(2)]
    vst = [ph.sb(f"vst{i}", [128, 512], BF) for i in range(2)]
    tails_x = ph.sb("tlx", [128, 16, 32], BF)
    tails_c = ph.sb("tlc", [128, 16, 32], BF)
    tails_u = ph.sb("tlu", [128, 16, 32], BF)
    s_c = ph.sem("c")
    s_pe2 = ph.sem("pe2")
    s_dv = ph.sem("dv")
    s_pl = ph.sem("pl")
    s_vst = [ph.sem("vst"), ph.sem("vst")]
    s_tl = ph.sem("tl")
    ph.dma("sp", cosT[:], C["cos"], sig=s_c)
    ph.dma("sp", sinT[:], C["sin"], sig=s_c)
    RT = C["RT"]
    rope_n = [0]
    pool_of_rope = {}

    def fm_job(c0, kind, dests, tails=None):
        b, wv = G.next_weights()
        for cb in range(4):
            slot, free_w = so.begin()
            last = None
            for tb in range(4):
                mm = [(G.wb[b][:, kc, cb * 128:(cb + 1) * 128], actT[:, kc, tb * 512:(tb + 1) * 512]) for kc in range(16)]
                fw = [(G.s_w[b], wv)] if (cb == 0 and tb == 0) else []
                dst = so.buf[slot][:, tb * 512:(tb + 1) * 512]
                if kind in ("copy", "sigmoid"):
                    func = AF.Copy if kind == "copy" else AF.Sigmoid

                    def evac(ps, w, dst=dst, func=func, tb=tb, free_w=free_w):
                        ww = list(w) + (free_w if tb == 0 else [])
                        ph.op("act", lambda e: e.activation(out=dst, in_=ps, func=func), ww, G.s_ep)

                    G.tile(mm, evac, fw)
                    last = (G.s_ep, G.s_ep.n)
                else:
                    n = rope_n[0]
                    rb = n % 2
                    rope_n[0] += 1

                    def evac(ps, w, rb=rb, n=n):
                        ww = list(w)
                        if n >= 2:
                            ww.append((s_pl, pool_of_rope[n - 2]))
                        ph.op("act", lambda e: e.activation(out=zb[rb][:], in_=ps, func=AF.Copy), ww, G.s_ep)

                    G.tile(mm, evac, fw)
                    epv = G.s_ep.n
                    pv = ph.op("pe", lambda e, rb=rb: e.matmul(ps2[rb][:], RT[:], zb[rb][:], start=True, stop=True),
                               [(G.s_ep, epv)], s_pe2)
                    ph.op("pool", lambda e, rb=rb, tb=tb: e.tensor_tensor(out=t1[rb][:], in0=zb[rb][:],
                                                                          in1=cosT[:, tb * 512:(tb + 1) * 512], op=ALU.mult),
                          [(G.s_ep, epv), (s_c, 32)])
                    dv = ph.op("dve", lambda e, rb=rb, tb=tb: e.tensor_tensor(out=t2[rb][:], in0=ps2[rb][:],
                                                                              in1=sinT[:, tb * 512:(tb + 1) * 512], op=ALU.mult),
                               [(s_pe2, pv), (s_c, 32)], s_dv)
                    pl = ph.op("pool", lambda e, rb=rb, dst=dst: e.tensor_tensor(out=dst, in0=t1[rb][:], in1=t2[rb][:], op=ALU.add),
                               [(s_dv, dv)] + (free_w if tb == 0 else []), s_pl)
                    pool_of_rope[n] = pl
                    last = (s_pl, pl)
            if tails is not None:
                tl, idx = tails
                ph.op("pool", lambda e, slot=slot, tl=tl, idx=idx: e.tensor_copy(
                    out=tl[:, idx, :].rearrange("p (i t) -> p i t", t=2),
                    in_=so.buf[slot][:].rearrange("p (i j) -> p i j", j=128)[:, :, 126:128]),
                    [last], so.s_x[slot])
                tails = (tl, idx + 1)
            so.store(slot, dests[cb], [last])
        G.end_job()

    def tok_job(c0, ncols, kind, dst_fn):
        b, wv = G.next_weights()
        for i in range(16):
            mm = [(actT[:, kc, i * 128:(i + 1) * 128], G.wb[b][:, kc, 0:ncols]) for kc in range(16)]
            fw = [(G.s_w[b], wv)] if i == 0 else []
            if kind == "v":
                vb = i % 2

                def evac(ps, w, vb=vb):
                    ph.op("act", lambda e: e.activation(out=vst[vb][:], in_=ps, func=AF.Copy),
                          list(w) + [(s_vst[vb], s_vst[vb].n)], G.s_ep)

                G.tile(mm, evac, fw)
                ph.dma("sp", dst_fn(i), vst[vb][:], [(G.s_ep, G.s_ep.n)], s_vst[vb])
            else:
                def evac(ps, w, i=i):
                    ph.op("act", lambda e: e.activation(out=dst_fn(i), in_=ps, func=AF.Sigmoid), w, G.s_ep)

                G.tile(mm, evac, fw)
        G.end_job()

    gk = T["gk_in"]
    fm = T["fm"]
    cols = [(g * 512, 512) for g in range(4)] + [(2048 + j * 512, 512) for j in range(6)] + [(5120, 48)]
    cols += [(5168 + j * 512, 512) for j in range(12)] + [(11312 + j * 512, 512) for j in range(8)]
    G.plan([w_in_l[:, c0:c0 + n] for c0, n in cols])
    for g in range(4):
        fm_job(g * 512, "rope", [T["qT"][4 * g + r] for r in range(4)])
    kv0 = 2048
    fm_job(kv0 + 0 * 512, "copy", [gk[(0 + g) * 128:(1 + g) * 128, :] for g in range(4)])
    fm_job(kv0 + 1 * 512, "copy", [gk[(4 + g) * 128:(5 + g) * 128, :] for g in range(4)])
    fm_job(kv0 + 2 * 512, "rope", [gk[(8 + g) * 128:(9 + g) * 128, :] for g in range(4)])
    tok_job(kv0 + 3 * 512, 512, "v", lambda i: T["gv_in"][i * 128:(i + 1) * 128, 0:512])
    fm_job(kv0 + 4 * 512, "rope", [gk[(12 + g) * 128:(13 + g) * 128, :] for g in range(4)])
    tok_job(kv0 + 5 * 512, 512, "v", lambda i: T["gv_in"][i * 128:(i + 1) * 128, 512:1024])
    tok_job(5120, 48, "ng", lambda i: T["gates"][:, i, :])
    cv0 = 5168
    for j in range(4):
        fm_job(cv0 + j * 512, "copy", [fm[4 * j + r] for r in range(4)], tails=(tails_x, 4 * j))
    for j in range(4):
        fm_job(cv0 + 2048 + j * 512, "copy", [fm[16 + 4 * j + r] for r in range(4)])
    for j in range(4):
        fm_job(cv0 + 4096 + j * 512, "copy", [fm[32 + 4 * j + r] for r in range(4)], tails=(tails_c, 4 * j))
    mg0 = 11312
    for j in range(8):
        fm_job(mg0 + j * 512, "sigmoid", [fm[48 + 4 * j + r] for r in range(4)])
    tl_w = [(s, s.n) for s in so.s_x]
    pl = ph.op("pool", lambda e: e.tensor_tensor(out=tails_u[:], in0=tails_x[:], in1=tails_c[:], op=ALU.mult), tl_w, s_tl)
    ph.dma("sp", T["gt_in"].rearrange("(s p) c -> p s c", p=128), tails_u[:], [(s_tl, pl)], s_tl)
    so.drain()
    ph.wait("sp", [(s_tl, s_tl.n), (s_vst[0], s_vst[0].n), (s_vst[1], s_vst[1].n)])
    ph.run()


def phase_gather(nc, name, T):
    ph = Phase(nc, name)
    s = ph.sem("cc")
    for a, b, rows, rc in (("gk_in", "gk", 2048, 512), ("gv_in", "gv", 2048, 1024), ("gt_in", "gt", 2048, 2048)):
        for k in range(rows // rc):
            if rc == rows:
                i_ap = T[a + "_t"].ap().opt()
                o_ap = T[b + "_t"].ap().opt()
            else:
                i_ap = T[a + "_t"].ap()[k * rc:(k + 1) * rc, :].opt()
                o_ap = T[b + "_t"].ap()[2 * k * rc:2 * (k + 1) * rc, :].opt()
            ph.op("pool", lambda e, i_ap=i_ap, o_ap=o_ap: e.collective_compute("AllGather", ALU.bypass, PAIRS,
                                                                               ins=[i_ap], outs=[o_ap]), [], s, 1)
            ph.wait("pool", [(s, s.n)])
    ph.run()


def phase_compress(nc, name, L, W, C, T, KCT, VCA):
    ph = Phase(nc, name)
    gk = T["gk"]
    xin = [ph.sb(f"xin{i}", [128, S], BF) for i in range(2)]
    w1 = [ph.sb(f"w1{i}", [128, 32, 256], BF) for i in range(2)]
    w2 = [ph.sb(f"w2{i}", [128, 2, 128], BF) for i in range(2)]
    posT = [ph.sb(f"pos{i}", [128, 32], F32) for i in range(2)]
    posb = [ph.sb(f"posb{i}", [128, 32], BF) for i in range(2)]
    cbias = [ph.sb(f"cb{i}", [128, 2], F32) for i in range(2)]
    hid = ph.sb("hid", [128, 2, 256], BF)
    zb = ph.sb("zb", [128, 256], BF)
    t1 = ph.sb("t1", [128, 256], F32)
    t2 = ph.sb("t2", [128, 256], F32)
    cosc = ph.sb("cosc", [128, 256], F32)
    sinc = ph.sb("sinc", [128, 256], F32)
    pb = ph.ps("pb", [128, 2])
    phid = [ph.ps(f"ph{i}", [128, 256]) for i in range(2)]
    pk = ph.ps("pk", [128, 256])
    pr = ph.ps("pr", [128, 256])
    pv = ph.ps("pv", [128, 2, 128])
    s_ld = ph.sem("ld")
    s_x = [ph.sem("x"), ph.sem("x")]
    s_pe = ph.sem("pe")
    s_act = ph.sem("act")
    s_dve = ph.sem("dve")
    s_pl = ph.sem("pl")
    RT = C["RT"]
    names = [("cmp_w1_k", "cmp_w2_k", "cmp_pos_kT"), ("cmp_w1_v", "cmp_w2_v", "cmp_pos_vT")]
    for kv in range(2):
        ph.dma("pool", w1[kv][:], W[names[kv][0]][L].rearrange("(l p) c -> p l c", p=128), sig=s_ld)
        ph.dma("pool", w2[kv][:], W[names[kv][1]][L].rearrange("(j p) c -> p j c", p=128), sig=s_ld)
        ph.dma("sp", posT[kv][:], W[names[kv][2]][L], sig=s_ld)
    ph.dma("sp", cosc[:], C["cosc"], sig=s_ld)
    ph.dma("sp", sinc[:], C["sinc"], sig=s_ld)
    ph.op("dve", lambda e: e.memset(KCT[:], 0.0), [], s_dve)
    ph.op("dve", lambda e: e.memset(VCA[:], 0.0), [], s_dve)
    ph.wait("dve", [(s_dve, s_dve.n)])
    ph.op("dve", lambda e: e.memset(VCA[:, :, 0, 128:129], 1.0), [], s_dve)
    ph.op("dve", lambda e: e.memset(VCA[0:127, :, 1, 128:129], 1.0), [], s_dve)
    ph.wait("pool", [(s_dve, s_dve.n)])
    for g in range(4):
        ph.dma("pool", VCA[:, g, :, 129:193], C["msel"], sig=s_ld)
    LD_ALL = 16 * 12
    for kv in range(2):
        d0 = ph.op("dve", lambda e, kv=kv: e.tensor_copy(out=posb[kv][:], in_=posT[kv][:]), [(s_ld, LD_ALL)], s_dve)
        for jc in range(2):
            for l in range(32):
                w = [(s_dve, d0), (s_act, s_act.n)] if l == 0 else []
                v = ph.op("pe", lambda e, kv=kv, jc=jc, l=l: e.matmul(pb[:, jc:jc + 1], w1[kv][:, l, jc * 128:(jc + 1) * 128],
                                                                       posb[kv][:, l:l + 1], start=(l == 0), stop=(l == 31)),
                          w, s_pe if l == 31 else None)
        ph.op("act", lambda e, kv=kv: e.activation(out=cbias[kv][:], in_=pb[:], func=AF.Copy), [(s_pe, v)], s_act)
    pe_end = {}
    it = 0
    for g in range(4):
        for kv in range(2):
            xb = it % 2
            slab = kv * 4 + g
            w = [(s_pe, pe_end[it - 2])] if it >= 2 else []
            for r in range(2):
                ph.dma("sp", xin[xb][:].rearrange("p (i r j) -> p i r j", r=2, j=128)[:, :, r, :],
                       gk[gk_row(r, slab): gk_row(r, slab) + 128, :].rearrange("p (i j) -> p i j", j=128),
                       w, s_x[xb])
            xv = s_x[xb].n
            for jc in range(2):
                for l in range(32):
                    w = []
                    if l == 0:
                        w = [(s_x[xb], xv), (s_act, s_act.n)]
                    rhs = xin[xb][:, l:l + 16 * 254 + 1:16]
                    v = ph.op("pe", lambda e, kv=kv, jc=jc, l=l, rhs=rhs: e.matmul(
                        phid[jc][:, 0:255], w1[kv][:, l, jc * 128:(jc + 1) * 128], rhs, start=(l == 0), stop=(l == 31)),
                        w, s_pe if l == 31 else None)
                ph.op("act", lambda e, kv=kv, jc=jc: e.activation(out=hid[:, jc, 0:255], in_=phid[jc][:, 0:255], func=AF.Silu,
                                                                  bias=cbias[kv][:, jc:jc + 1]),
                      [(s_pe, v), (s_pe, s_pe.n)], s_act)
            av = s_act.n
            if kv == 0:
                for jc in range(2):
                    v = ph.op("pe", lambda e, jc=jc: e.matmul(pk[:, 0:255], w2[0][:, jc, :], hid[:, jc, 0:255],
                                                               start=(jc == 0), stop=(jc == 1)),
                              [(s_act, av), (s_dve, s_dve.n), (s_pl, s_pl.n)] if jc == 0 else [], s_pe if jc == 1 else None)
                a2 = ph.op("act", lambda e: e.activation(out=zb[:, 0:255], in_=pk[:, 0:255], func=AF.Copy), [(s_pe, v)], s_act)
                v2 = ph.op("pe", lambda e: e.matmul(pr[:, 0:255], RT[:], zb[:, 0:255], start=True, stop=True), [(s_act, a2)], s_pe)
                ph.op("pool", lambda e: e.tensor_tensor(out=t1[:, 0:255], in0=zb[:, 0:255], in1=cosc[:, 0:255], op=ALU.mult),
                      [(s_act, a2)])
                dv = ph.op("dve", lambda e: e.tensor_tensor(out=t2[:, 0:255], in0=pr[:, 0:255], in1=sinc[:, 0:255], op=ALU.mult),
                           [(s_pe, v2)], s_dve)
                ph.op("pool", lambda e, g=g: e.tensor_tensor(out=KCT[:, g, 0:255], in0=t1[:, 0:255], in1=t2[:, 0:255], op=ALU.add),
                      [(s_dve, dv)], s_pl)
            else:
                for nt in range(2):
                    M = 128 if nt == 0 else 127
                    for jc in range(2):
                        v = ph.op("pe", lambda e, jc=jc, nt=nt, M=M: e.matmul(pv[0:M, nt, :], hid[:, jc, nt * 128:nt * 128 + M],
                                                                               w2[1][:, jc, :], start=(jc == 0), stop=(jc == 1)),
                                  [(s_act, av)] if (jc == 0 and nt == 0) else [], s_pe if (jc == 1 and nt == 1) else None)
                ph.op("act", lambda e, g=g: e.activation(out=VCA[:, g, 0, 0:128], in_=pv[:, 0, :], func=AF.Copy), [(s_pe, v)], None)
                ph.op("act", lambda e, g=g: e.activation(out=VCA[0:127, g, 1, 0:128], in_=pv[0:127, 1, :], func=AF.Copy), [], s_act)
            pe_end[it] = s_pe.n
            it += 1
    ph.wait("sp", [(s_act, s_act.n), (s_pl, s_pl.n), (s_ld, LD_ALL)])
    ph.run()


class SPipe:
    def __init__(self, ph):
        self.ph = ph
        self.buf = [ph.ps("sA", [128, 512]), ph.ps("sB", [128, 512])]
        self.s_qk = ph.sem("qk")
        self.s_ac = ph.sem("ac")
        self.k = 0
        self.ac_of = {}
        self.pending = None

    def tile(self, mm, M, N, act, post=None, first_waits=(), three_d=True):
        ph = self.ph
        k = self.k
        ps = self.buf[k % 2][0:M, 0:N]
        n = len(mm)
        v = 0
        for j, (l, r) in enumerate(mm):
            w = []
            if j == 0:
                w = list(first_waits)
                if k >= 2:
                    w.append((self.s_ac, self.ac_of[k - 2]))
            o = ps.rearrange("p (r q) -> p r q", r=4) if (three_d and len(r.shape) == 3) else ps
            v = ph.op("pe", lambda e, l=l, r=r, j=j, o=o: e.matmul(o, l, r, start=(j == 0), stop=(j == n - 1)),
                      w, self.s_qk if j == n - 1 else None)
        before = self.s_ac.n
        act(ps, [(self.s_qk, v)])
        assert self.s_ac.n == before + 1
        self.ac_of[k] = self.s_ac.n
        self.flush()
        if post is not None:
            acv = self.s_ac.n
            self.pending = lambda: post([(self.s_ac, acv)])
        self.k += 1

    def flush(self):
        if self.pending is not None:
            p = self.pending
            self.pending = None
            p()


def phase_attn(nc, name, C, T, KCT, VCA, gates):
    ph = Phase(nc, name)
    gk, gv = T["gk"], T["gv"]
    ident = C["ident"]
    qt0 = ph.sb("qt0", [128, 4, NT], BF)
    qt = [qt0, qt0]
    ksel = [ph.sb(f"ks{i}", [128, S], BF) for i in range(2)]
    kwin = [ph.sb(f"kw{i}", [128, S], BF) for i in range(2)]
    vsel = [ph.sb(f"vs{i}", [128, 32, 130], BF) for i in range(2)]
    vwin = [ph.sb(f"vw{i}", [128, 32, 130], BF) for i in range(2)]
    cmpmask = ph.sb("cmpm", [128, 2, NT], BF)
    fb = ph.sb("fb", [128, 16, 64], F32)
    expand = ph.sb("exp", [64, 32, 128], BF)
    cmaskS = ph.sb("cms", [128, 2, 128], BF)
    wmask = ph.sb("wm", [128, 6, 128], BF)
    PcT = ph.sb("pct", [128, 2, 512], BF)
    PT = [ph.sb(f"pt{i}", [128, 512], BF) for i in range(3)]
    dsafe = ph.sb("dsafe", [128, 12], F32)
    rec = ph.sb("rec", [128, 12], F32)
    coef = ph.sb("coef", [128, 12], F32)
    impm = ph.sb("impm", [128, 64], F32)
    impr = ph.sb("impr", [128, 64], F32)
    m8 = ph.sb("m8", [128, 16], F32)
    biasq = ph.sb("biasq", [128, 64], BF)
    BiasT = ph.sb("biasT", [64, 128], BF)
    o_acc = ph.sb("oacc", [128, 4, 128], F32)
    tmpo = ph.sb("tmpo", [128, 4, 128], F32)
    o_bf = ph.sb("obf", [128, 4, 128], BF)
    oTs = ph.sb("oTs", [128, 4, NT], BF)
    oc = ph.ps("oc", [128, 4, 256])
    osl = ph.ps("os", [128, 4, 256])
    ow = ph.ps("ow", [128, 4, 256])
    sp = SPipe(ph)
    s_c = ph.sem("c")
    s_ld = [ph.sem("ld"), ph.sem("ld")]
    s_pv = ph.sem("pv")
    s_dv = ph.sem("dv")
    s_pl = ph.sem("pl")
    s_st = ph.sem("st")
    s_ms = ph.sem("ms")
    for dst, src in ((cmpmask, "cmpmask"), (fb, "fb"), (expand, "expand"), (cmaskS, "cmaskS"), (wmask, "wmask")):
        ph.dma("sp", dst[:], C[src], sig=s_c)
    NC_C = 16 * 5
    for b in range(2):
        ph.op("dve", lambda e, b=b: e.memset(vsel[b][:, :, 129:130], 0.0), [], s_ms)
        ph.op("dve", lambda e, b=b: e.memset(vwin[b][:, :, 129:130], 0.0), [], s_ms)
        ph.op("dve", lambda e, b=b: e.memset(vsel[b][:, :, 128:129], 1.0), [], s_ms)
        ph.op("dve", lambda e, b=b: e.memset(vwin[b][:, :, 128:129], 1.0), [], s_ms)
    grp_pe_end = {}
    pt_n = [0]
    pv_of_pt = {}
    late = [None]
    dv_last_oc = [0]
    dv_last_os = [0]
    dv_last_ow = [0]
    pl_last = [0]

    def load_group(g):
        b = g % 2
        w = [(s_pv, grp_pe_end[g - 2]), (sp.s_qk, grp_pe_end[("qk", g - 2)])] if g >= 2 else []
        for r in range(2):
            for dst, slab in ((ksel[b], 8 + g), (kwin[b], 12 + g)):
                ph.dma("sp", dst[:].rearrange("p (i r j) -> p i r j", r=2, j=128)[:, :, r, :],
                       gk[gk_row(r, slab): gk_row(r, slab) + 128, :].rearrange("p (i j) -> p i j", j=128),
                       w, s_ld[b])
            for dst, c0 in ((vsel[b], 0), (vwin[b], 512)):
                for hf in range(2):
                    r0 = gv_row(r, hf * 1024)
                    ph.dma("sp", dst[:].rearrange("p (i r) c -> p i r c", r=2)[:, hf * 8:(hf + 1) * 8, r, 0:128],
                           gv[r0:r0 + 1024, c0 + g * 128: c0 + (g + 1) * 128].rearrange("(i j) c -> j i c", j=128),
                           w, s_ld[b])
        return s_ld[b].n

    ldv = {0: load_group(0)}
    for g in range(4):
        b = g % 2
        wq = [(s_pv, grp_pe_end[g - 1]), (sp.s_qk, grp_pe_end[("qk", g - 1)])] if g >= 1 else []
        ph.dma("sp", qt0[:], T["qT"][4 * g:4 * g + 4].rearrange("h p t -> p h t"), wq, s_ld[b])
        ldv[g] = s_ld[b].n
        if g + 1 < 4:
            ldv[g + 1] = load_group(g + 1)
        for i in range(16):
            first = [(s_ld[b], ldv[g]), (s_c, NC_C), (s_ms, 8)] if i == 0 else []
            qti = qt[b][:, :, i * 128:(i + 1) * 128]
            for nt in range(2):
                mm = [(KCT[:, g, nt * 128:(nt + 1) * 128], qti),
                      (ident[:], cmpmask[:, nt, i * 128:(i + 1) * 128].unsqueeze(1).broadcast_to([128, 4, 128]))]

                def act(ps, w, nt=nt):
                    ww = list(w)
                    if nt == 0:
                        ww.append((s_pv, s_pv.n))
                    ph.op("act", lambda e: e.activation(out=PcT[:, nt, :], in_=ps, func=AF.Exp, scale=SCALE), ww, sp.s_ac)

                post = None
                if nt == 1:
                    def post(w, g=g):
                        ww = list(w) + [(s_dv, dv_last_oc[0])]
                        for r in range(4):
                            for n2 in range(2):
                                ph.op("pe", lambda e, r=r, n2=n2: e.matmul(oc[:, r, 0:194], PcT[:, n2, r * 128:(r + 1) * 128],
                                                                          VCA[:, g, n2, :], start=(n2 == 0), stop=(n2 == 1)),
                                      ww if (r == 0 and n2 == 0) else [], s_pv if (r == 3 and n2 == 1) else None)
                sp.tile(mm, 128, 512, act, post, first if nt == 0 else [])
            sp.flush()
            pv_c = s_pv.n
            def dchain(fn, w=()):
                v = ph.op("dve", fn, [(s_dv, s_dv.n)] + list(w), s_dv)
                return v
            dchain(lambda e: e.tensor_scalar(out=dsafe[:, 0:4], in0=oc[:, :, 128], scalar1=1e-30, scalar2=None, op0=ALU.max),
                   [(s_pv, pv_c), (s_pl, pl_last[0])])
            dchain(lambda e: e.reciprocal(out=rec[:, 0:4], in_=dsafe[:, 0:4]))
            for r in range(4):
                src1 = fb[:, i, :] if r == 0 else impm[:]
                dchain(lambda e, r=r, src1=src1: e.scalar_tensor_tensor(out=impm[:], in0=oc[:, r, 129:193], scalar=rec[:, r:r + 1],
                                                                        in1=src1, op0=ALU.mult, op1=ALU.add))
            dchain(lambda e: e.max(out=m8[:, 0:8], in_=impm[:]))
            dchain(lambda e: e.match_replace(out=impr[:], in_to_replace=m8[:, 0:8], in_values=impm[:], imm_value=-1e9))
            dchain(lambda e: e.max(out=m8[:, 8:16], in_=impr[:]))
            bq = dchain(lambda e: e.tensor_scalar(out=biasq[:], in0=impm[:], scalar1=m8[:, 15:16], scalar2=NEG,
                                                  op0=ALU.is_lt, op1=ALU.mult), [(sp.s_qk, sp.s_qk.n)])
            gsl = gates[:, i, 12 * g:12 * g + 12].rearrange("p (h c) -> p h c", c=3)
            dchain(lambda e, gsl=gsl: e.tensor_tensor(out=coef[:, 0:4], in0=rec[:, 0:4], in1=gsl[:, :, 0], op=ALU.mult))
            dv_last_oc[0] = dchain(lambda e: e.tensor_tensor(out=o_acc[:], in0=oc[:, :, 0:128],
                                                             in1=coef[:, 0:4].unsqueeze(2).broadcast_to([128, 4, 128]), op=ALU.mult))
            wt = [j for j in range(6) if 2 * i - 4 + j >= 0]
            for j in wt:
                kt = 2 * i - 4 + j
                mm = [(kwin[b][:, kt * 128:(kt + 1) * 128], qti)]
                if j not in (2, 3):
                    mm.append((ident[:], wmask[:, j, :].unsqueeze(1).broadcast_to([128, 4, 128])))
                n = pt_n[0]
                pt_n[0] += 1
                pb = n % 3

                def act(ps, w, pb=pb, n=n):
                    ww = list(w)
                    if n >= 3:
                        ww.append((s_pv, pv_of_pt[n - 3]))
                    ph.op("act", lambda e: e.activation(out=PT[pb][:], in_=ps, func=AF.Exp, scale=SCALE), ww, sp.s_ac)

                def post(w, pb=pb, n=n, kt=kt, j=j, b=b, wt=wt):
                    ww = list(w)
                    if j == wt[0]:
                        ww.append((s_dv, dv_last_ow[0]))
                    for r in range(4):
                        v = ph.op("pe", lambda e, r=r: e.matmul(ow[:, r, 0:130], PT[pb][:, r * 128:(r + 1) * 128], vwin[b][:, kt, :],
                                                                 start=(j == wt[0] and r in (0, 2)), stop=(j == wt[-1]),
                                                                 skip_group_check=True),
                                  ww if r == 0 else [], s_pv if r == 3 else None)
                    pv_of_pt[n] = v

                sp.tile(mm, 128, 512, act, post)
            if late[0] is not None:
                late[0]()
                late[0] = None
            def actb(ps, w):
                ph.op("act", lambda e: e.activation(out=BiasT[:], in_=ps, func=AF.Copy), list(w) + [(s_pv, s_pv.n)], sp.s_ac)
            sp.tile([(biasq[:], ident[:])], 64, 128, actb, None, [(s_dv, bq)], three_d=False)
            bias_ready = sp.s_ac.n
            nkt = 2 * i + 2
            for kt in range(nkt):
                mm = [(ksel[b][:, kt * 128:(kt + 1) * 128], qti),
                      (expand[:, kt, :], BiasT[:].unsqueeze(1).broadcast_to([64, 4, 128]))]
                if kt >= 2 * i:
                    mm.append((ident[:], cmaskS[:, kt - 2 * i, :].unsqueeze(1).broadcast_to([128, 4, 128])))
                n = pt_n[0]
                pt_n[0] += 1
                pb = n % 3

                def act(ps, w, pb=pb, n=n):
                    ww = list(w)
                    if n >= 3:
                        ww.append((s_pv, pv_of_pt[n - 3]))
                    ph.op("act", lambda e: e.activation(out=PT[pb][:], in_=ps, func=AF.Exp, scale=SCALE), ww, sp.s_ac)

                def post(w, pb=pb, n=n, kt=kt, nkt=nkt, b=b):
                    ww = list(w)
                    if kt == 0:
                        ww.append((s_dv, dv_last_os[0]))
                    for r in range(4):
                        v = ph.op("pe", lambda e, r=r: e.matmul(osl[:, r, 0:130], PT[pb][:, r * 128:(r + 1) * 128], vsel[b][:, kt, :],
                                                                 start=(kt == 0 and r in (0, 2)), stop=(kt == nkt - 1),
                                                                 skip_group_check=True),
                                  ww if r == 0 else [], s_pv if r == 3 else None)
                    pv_of_pt[n] = v

                sp.tile(mm, 128, 512, act, post, [(sp.s_ac, bias_ready)] if kt == 0 else [])
            sp.flush()
            pv_end = s_pv.n
            dchain(lambda e: e.tensor_scalar(out=dsafe[:, 4:8], in0=osl[:, :, 128], scalar1=1e-30, scalar2=None, op0=ALU.max),
                   [(s_pv, pv_end)])
            dchain(lambda e: e.tensor_scalar(out=dsafe[:, 8:12], in0=ow[:, :, 128], scalar1=1e-30, scalar2=None, op0=ALU.max))
            dchain(lambda e: e.reciprocal(out=rec[:, 4:12], in_=dsafe[:, 4:12]))
            dchain(lambda e, gsl=gsl: e.tensor_tensor(out=coef[:, 4:8], in0=rec[:, 4:8], in1=gsl[:, :, 1], op=ALU.mult))
            dchain(lambda e, gsl=gsl: e.tensor_tensor(out=coef[:, 8:12], in0=rec[:, 8:12], in1=gsl[:, :, 2], op=ALU.mult))
            d1 = dchain(lambda e: e.tensor_tensor(out=tmpo[:], in0=osl[:, :, 0:128],
                                                  in1=coef[:, 4:8].unsqueeze(2).broadcast_to([128, 4, 128]), op=ALU.mult),
                        [(s_pl, s_pl.n)])
            dv_last_os[0] = d1
            p1 = ph.op("pool", lambda e: e.tensor_tensor(out=o_acc[:], in0=o_acc[:], in1=tmpo[:], op=ALU.add), [(s_dv, d1)], s_pl)
            d2 = dchain(lambda e: e.tensor_tensor(out=tmpo[:], in0=ow[:, :, 0:128],
                                                  in1=coef[:, 8:12].unsqueeze(2).broadcast_to([128, 4, 128]), op=ALU.mult),
                        [(s_pl, p1)])
            dv_last_ow[0] = d2
            p2 = ph.op("pool", lambda e: e.tensor_tensor(out=o_bf[:], in0=o_acc[:], in1=tmpo[:], op=ALU.add),
                       [(s_dv, d2), (sp.s_qk, sp.s_qk.n)], s_pl)
            pl_last[0] = p2

            def do_late(g=g, i=i, p2=p2):
                def actT(ps, w):
                    ww = list(w)
                    if i == 0 and g >= 1:
                        ww.append((s_st, 16 * g))
                    ph.op("act", lambda e: e.activation(out=oTs[:, :, i * 128:(i + 1) * 128],
                                                        in_=ps.rearrange("p (r q) -> p r q", r=4), func=AF.Copy), ww, sp.s_ac)
                k = sp.k
                pso = sp.buf[k % 2]
                w0 = [(s_pl, p2)]
                if k >= 2:
                    w0.append((sp.s_ac, sp.ac_of[k - 2]))
                v = 0
                for r in range(4):
                    v = ph.op("pe", lambda e, r=r: e.matmul(pso[:, r * 128:(r + 1) * 128], o_bf[:, r, :], ident[:], start=True, stop=True),
                              w0 if r == 0 else [], sp.s_qk if r == 3 else None)
                actT(pso[:, :], [(sp.s_qk, v)])
                sp.ac_of[k] = sp.s_ac.n
                sp.flush()
                sp.k += 1

            late[0] = do_late
            if "dbg_k" in T and g == 0 and i == 1:
                sd = ph.sem("dbgs")
                w = [(s_pl, p2), (s_dv, s_dv.n)]
                for nm, src in (("dbg_k", ksel[b][:]), ("dbg_kw", kwin[b][:]),
                                ("dbg_v", vsel[b][:].rearrange("p a c -> p (a c)")), ("dbg_vw", vwin[b][:].rearrange("p a c -> p (a c)")),
                                ("dbg_bt", BiasT[:]), ("dbg_imp", impm[:]), ("dbg_ds", dsafe[:]), ("dbg_coef", coef[:]),
                                ("dbg_obf", o_bf[:].rearrange("p a c -> p (a c)")), ("dbg_m8", m8[:]),
                                ("dbg_kct", KCT[:].rearrange("p a c -> p (a c)")), ("dbg_vca", VCA[:].rearrange("p a b c -> p (a b c)")),
                                ("dbg_gates", gates[:].rearrange("p a c -> p (a c)"))):
                    ph.dma("sp", T[nm], src, w, sd)
                ph.wait("sp", [(sd, sd.n)])
        late[0]()
        late[0] = None
        grp_pe_end[g] = s_pv.n
        grp_pe_end[("qk", g)] = sp.s_qk.n
        ph.dma("sp", T["oT"][4 * g:4 * g + 4].rearrange("h p t -> p h t"), oTs[:], [(sp.s_ac, sp.s_ac.n)], s_st)
    ph.wait("sp", [(s_st, 64)])
    ph.run()


def load_act(ph, actT, src, sem, nk=16):
    for k in range(0, nk, 4):
        ph.dma("sp", actT[:, k:k + 4, :], src[k:k + 4].rearrange("k p t -> p k t"), [], sem)
    return sem.n


def phase_proj(nc, name, actT, wsrc, mode, T, C, src_act=None, xsrc=None):
    ph = Phase(nc, name)
    G = Gemm(ph, 16)
    fm = T["fm"]
    s_a = ph.sem("a")
    av = load_act(ph, actT, src_act, s_a) if src_act is not None else 0
    yf = [ph.sb(f"yf{i}", [128, 512], F32) for i in range(2)]
    s_po = ph.sem("po")
    s_l = [ph.sem("l"), ph.sem("l")]
    if mode == "resid":
        so = SlabOut(ph, F32, 2, "xs")
        xl = [ph.sb(f"xl{i}", [128, NT], F32) for i in range(2)]
        aux = aux2 = None
    else:
        so = SlabOut(ph, BF, 2)
        aux = [ph.sb(f"ax{i}", [128, NT], BF) for i in range(2)]
        aux2 = [ph.sb(f"ay{i}", [128, NT], BF) for i in range(2)] if mode == "conv" else None
        tt = [ph.sb(f"tt{i}", [128, 512], F32) for i in range(2)]
    s_p2 = ph.sem("p2")
    tn = 0
    post_of = {}
    lv_of = {}

    def issue_load(c):
        sl = c % 2
        lw = [(s_po, post_of.get(c - 2, 0))]
        if mode == "resid":
            ph.dma("sp", xl[sl][:], xsrc[c], lw, s_l[sl])
        else:
            ph.dma("sp", aux[sl][:], fm[(48 if mode == "attn" else 64) + c], lw, s_l[sl])
            if mode == "conv":
                ph.dma("sp", aux2[sl][:], T["maT"][c], lw, s_l[sl])
        lv_of[c] = s_l[sl].n

    G.plan([wsrc[:, j * 512:(j + 1) * 512] for j in range(4)])
    for j in range(4):
        b, wv = G.next_weights()
        for cb in range(4):
            c = 4 * j + cb
            slot, free_w = so.begin()
            if c == 0:
                issue_load(0)
            if c + 1 < 16:
                issue_load(c + 1)
            lv = lv_of[c]
            last = 0
            for tb in range(4):
                mm = [(G.wb[b][:, kc, cb * 128:(cb + 1) * 128], actT[:, kc, tb * 512:(tb + 1) * 512]) for kc in range(16)]
                fw = [(G.s_w[b], wv), (s_a, av)] if (cb == 0 and tb == 0) else []
                yb = tn % 2
                tsl = slice(tb * 512, (tb + 1) * 512)

                def evac(ps, w, yb=yb, tn=tn):
                    ww = list(w) + [(s_po, post_of.get(("t", tn - 2), 0))]
                    ph.op("act", lambda e: e.activation(out=yf[yb][:], in_=ps, func=AF.Copy), ww, G.s_ep)

                G.tile(mm, evac, fw)
                ev = G.s_ep.n
                w0 = [(G.s_ep, ev), (s_l[slot], lv)] + (free_w if tb == 0 else [])
                if mode == "attn":
                    last = ph.op("pool", lambda e, yb=yb, slot=slot, tsl=tsl: e.tensor_tensor(
                        out=so.buf[slot][:, tsl], in0=yf[yb][:], in1=aux[slot][:, tsl], op=ALU.mult), w0, s_po)
                elif mode == "conv":
                    p = ph.op("pool", lambda e, yb=yb, slot=slot, tsl=tsl: e.tensor_tensor(
                        out=tt[yb][:], in0=yf[yb][:], in1=aux[slot][:, tsl], op=ALU.mult),
                        w0 + [(s_po, post_of.get(("t", tn - 2), 0))], s_p2)
                    last = ph.op("dve", lambda e, yb=yb, slot=slot, tsl=tsl: e.tensor_tensor(
                        out=so.buf[slot][:, tsl], in0=tt[yb][:], in1=aux2[slot][:, tsl], op=ALU.add), [(s_p2, p)], s_po)
                else:
                    last = ph.op("dve", lambda e, yb=yb, slot=slot, tsl=tsl: e.tensor_tensor(
                        out=so.buf[slot][:, tsl], in0=yf[yb][:], in1=xl[slot][:, tsl], op=ALU.add), w0, s_po)
                post_of[("t", tn)] = last
                tn += 1
            post_of[c] = last
            dst = T["xres"][c] if mode == "resid" else (T["maT"][c] if mode == "attn" else T["mT"][c])
            so.store(slot, dst, [(s_po, last)])
        G.end_job()
    so.drain()
    ph.run()


def phase_conv(nc, name, actT, W, L, C, T):
    ph = Phase(nc, name)
    fm, gt = T["fm"], T["gt"]
    xi = [ph.sb(f"xi{i}", [128, NT], BF) for i in range(2)]
    gc = [ph.sb(f"gc{i}", [128, NT], BF) for i in range(2)]
    gb = [ph.sb(f"gb{i}", [128, NT], BF) for i in range(2)]
    ub = [ph.sb(f"ub{i}", [128, 16, 130], F32) for i in range(2)]
    acc = [ph.sb(f"acc{i}", [128, 16, 128], F32) for i in range(2)]
    H0 = ph.sb("H0", [128, 16, 16, 2], BF)
    H1 = ph.sb("H1", [128, 16, 16, 2], BF)
    Ht = ph.sb("Ht", [128, 16, 16, 2], F32)
    halo = ph.sb("halo", [128, 16, 16, 2], F32)
    cw = ph.sb("cw", [128, 16, 3], F32)
    fl = ph.sb("fl", [128, 2], F32)
    s_c = ph.sem("c")
    s_l = [ph.sem("l"), ph.sem("l")]
    s_pl = ph.sem("pl")
    s_dv = ph.sem("dv")
    m0 = ph.op("dve", lambda e: e.memset(H0[:], 0.0), [], s_dv)
    ph.dma("sp", H0[:, :, 1:16, :].rearrange("p c i t -> p c (i t)"),
           gt[2048:4096, 0:30].rearrange("(c p) x -> p c x", p=128), [(s_dv, m0)], s_c)
    ph.dma("sp", H1[:].rearrange("p c i t -> p c (i t)"), gt[0:2048, :].rearrange("(c p) x -> p c x", p=128), [], s_c)
    ph.dma("sp", cw[:], W["conv_wT"][L], [], s_c)
    ph.dma("sp", fl[:], C["flags"], [], s_c)
    h1 = ph.op("dve", lambda e: e.tensor_scalar(out=Ht[:], in0=H0[:], scalar1=fl[:, 0:1], scalar2=None, op0=ALU.mult),
               [(s_c, 64)], s_dv)
    h2 = ph.op("dve", lambda e: e.scalar_tensor_tensor(out=halo[:], in0=H1[:], scalar=fl[:, 1:2], in1=Ht[:],
                                                       op0=ALU.mult, op1=ALU.add), [(s_dv, h1)], s_dv)
    dv_of = {}
    pl_of = {}
    for c in range(16):
        b = c % 2
        w = [(s_dv, dv_of[c - 2])] if c >= 2 else []
        for dst, off in ((xi[b], 0), (gb[b], 16), (gc[b], 32)):
            ph.dma("sp", dst[:], fm[off + c], w, s_l[b])
        lv = s_l[b].n
        ph.op("pool", lambda e, b=b, c=c: e.tensor_copy(out=ub[b][:, :, 0:2], in_=halo[:, c, :, :]),
              [(s_dv, h2)] + w, None)
        pl_of[c] = ph.op("pool", lambda e, b=b: e.tensor_tensor(out=ub[b][:, :, 2:130],
                                                                in0=xi[b][:].rearrange("p (i j) -> p i j", j=128),
                                                                in1=gc[b][:].rearrange("p (i j) -> p i j", j=128), op=ALU.mult),
                         [(s_l[b], lv)], s_pl)
        d = ph.op("dve", lambda e, b=b, c=c: e.tensor_scalar(out=acc[b][:], in0=ub[b][:, :, 2:130], scalar1=cw[:, c, 2:3],
                                                             scalar2=None, op0=ALU.mult), [(s_pl, pl_of[c])], s_dv)
        d = ph.op("dve", lambda e, b=b, c=c: e.scalar_tensor_tensor(out=acc[b][:], in0=ub[b][:, :, 1:129], scalar=cw[:, c, 1:2],
                                                                    in1=acc[b][:], op0=ALU.mult, op1=ALU.add), [(s_dv, d)], s_dv)
        d = ph.op("dve", lambda e, b=b, c=c: e.scalar_tensor_tensor(out=acc[b][:], in0=ub[b][:, :, 0:128], scalar=cw[:, c, 0:1],
                                                                    in1=acc[b][:], op0=ALU.mult, op1=ALU.add), [(s_dv, d)], s_dv)
        dv_of[c] = ph.op("dve", lambda e, b=b, c=c: e.tensor_tensor(out=actT[:, c, :].rearrange("p (i j) -> p i j", j=128),
                                                                    in0=acc[b][:], in1=gb[b][:].rearrange("p (i j) -> p i j", j=128),
                                                                    op=ALU.mult), [(s_dv, d)], s_dv)
    ph.wait("sp", [(s_dv, s_dv.n)])
    ph.run()


def phase_ffn(nc, name, actT, W, L, T):
    ph = Phase(nc, name)
    Gu = Gemm(ph, 16)
    Gd = Gemm(ph, 8, pfx="d")
    fT = ph.sb("fT", [128, 8, NT], BF)
    rf = [ph.sb(f"rf{i}", [128, 512], F32) for i in range(2)]
    yf = [ph.sb(f"yf{i}", [128, 512], F32) for i in range(2)]
    so = SlabOut(ph, F32, 2, "xs")
    xl = [ph.sb(f"xl{i}", [128, NT], F32) for i in range(2)]
    s_pl = ph.sem("pl")
    s_po = ph.sem("po")
    s_l = [ph.sem("l"), ph.sem("l")]
    xres = T["xres"]
    un = 0
    dn = 0
    pl_of = {}
    po_of = {}
    slab_store = {}
    slab_n = 0
    slab_last = {}
    lv_of = {}

    def issue_load(n):
        sl = n % 2
        c = n % 16
        lw = [(s_po, slab_last.get(n - 2, 0))]
        if c in slab_store:
            lw.append(slab_store[c])
        ph.dma("sp", xl[sl][:], xres[c], lw, s_l[sl])
        lv_of[n] = s_l[sl].n

    Gu.plan([W["w_up"][L][:, hg * 1024 + j * 512: hg * 1024 + (j + 1) * 512] for hg in range(8) for j in range(2)])
    Gd.plan([W["w_down"][L][hg * 1024:(hg + 1) * 1024, j * 512:(j + 1) * 512] for hg in range(8) for j in range(4)])
    for hg in range(8):
        for j in range(2):
            b, wv = Gu.next_weights()
            for cb in range(4):
                for tb in range(4):
                    mm = [(Gu.wb[b][:, kc, cb * 128:(cb + 1) * 128], actT[:, kc, tb * 512:(tb + 1) * 512]) for kc in range(16)]
                    fw = [(Gu.s_w[b], wv)] if (cb == 0 and tb == 0) else []
                    rb = un % 2

                    def evac(ps, w, rb=rb, un=un):
                        ph.op("act", lambda e: e.activation(out=rf[rb][:], in_=ps, func=AF.Relu),
                              list(w) + [(s_pl, pl_of.get(un - 2, 0))], Gu.s_ep)

                    Gu.tile(mm, evac, fw)
                    pl_of[un] = ph.op("pool", lambda e, rb=rb, j=j, cb=cb, tb=tb: e.tensor_tensor(
                        out=fT[:, j * 4 + cb, tb * 512:(tb + 1) * 512], in0=rf[rb][:], in1=rf[rb][:], op=ALU.mult),
                        [(Gu.s_ep, Gu.s_ep.n), (Gd.s_pe, Gd.s_pe.n)], s_pl)
                    un += 1
            Gu.end_job()
        f_ready = s_pl.n
        for j in range(4):
            b, wv = Gd.next_weights()
            for cb in range(4):
                c = 4 * j + cb
                slot, free_w = so.begin()
                if slab_n == 0:
                    issue_load(0)
                if slab_n + 1 < 128:
                    issue_load(slab_n + 1)
                lv = lv_of[slab_n]
                last = 0
                for tb in range(4):
                    mm = [(Gd.wb[b][:, kc, cb * 128:(cb + 1) * 128], fT[:, kc, tb * 512:(tb + 1) * 512]) for kc in range(8)]
                    fw = [(Gd.s_w[b], wv), (s_pl, f_ready)] if (cb == 0 and tb == 0) else []
                    yb = dn % 2
                    tsl = slice(tb * 512, (tb + 1) * 512)

                    def evac(ps, w, yb=yb, dn=dn):
                        ph.op("act", lambda e: e.activation(out=yf[yb][:], in_=ps, func=AF.Copy),
                              list(w) + [(s_po, po_of.get(dn - 2, 0))], Gd.s_ep)

                    Gd.tile(mm, evac, fw)
                    last = ph.op("dve", lambda e, yb=yb, slot=slot, tsl=tsl: e.tensor_tensor(
                        out=so.buf[slot][:, tsl], in0=yf[yb][:], in1=xl[slot][:, tsl], op=ALU.add),
                        [(Gd.s_ep, Gd.s_ep.n), (s_l[slot], lv)] + (list(free_w) if tb == 0 else []), s_po)
                    po_of[dn] = last
                    dn += 1
                slab_last[slab_n] = last
                slab_n += 1
                v = so.store(slot, xres[c], [(s_po, last)])
                slab_store[c] = (so.s_st[slot], v)
            Gd.end_job()
    so.drain()
    ph.run()


WEIGHT_USERS = {
    "w_in": ("win",), "cmp_w1_k": ("cmp",), "cmp_w2_k": ("cmp",), "cmp_pos_kT": ("cmp",),
    "cmp_w1_v": ("cmp",), "cmp_w2_v": ("cmp",), "cmp_pos_vT": ("cmp",), "conv_wT": ("cv",),
    "w_attn_proj": ("ap",), "w_conv_out": ("co",), "w_o": ("wo",), "w_up": ("ffn",), "w_down": ("ffn",),
    "norm1_gT": ("n1",), "norm2_gT": ("n2",), "final_gT": ("nf",),
}
LAST_SHAPES = {}


def build(sel=None, dbg=()):
    nc = bass.Bass("TRN2", target_bir_lowering=False)

    def on(L, nm):
        return sel is None or (L, nm) in sel

    def din(nm, shape, dt=F32):
        LAST_SHAPES[nm] = tuple(shape)
        return nc.dram_tensor(nm, shape, dt, kind="ExternalInput").ap()

    def scr(nm, shape, dt=BF):
        if nm in dbg:
            t = nc.dram_tensor(nm, shape, dt, kind="ExternalOutput")
        else:
            t = nc.dram_tensor(nm, shape, dt)
        return t

    W = {}
    for nm, shp in (("w_in", [DEPTH, D, IN_COLS]),
                    ("cmp_w1_k", [DEPTH, 4096, 256]), ("cmp_w2_k", [DEPTH, 256, 128]), ("cmp_pos_kT", [DEPTH, 128, 32]),
                    ("cmp_w1_v", [DEPTH, 4096, 256]), ("cmp_w2_v", [DEPTH, 256, 128]), ("cmp_pos_vT", [DEPTH, 128, 32]),
                    ("conv_wT", [DEPTH, 128, 16, 3]), ("w_attn_proj", [DEPTH, D, D]), ("w_conv_out", [DEPTH, D, D]),
                    ("w_o", [DEPTH, D, D]), ("w_up", [DEPTH, D, DFF]), ("w_down", [DEPTH, DFF, D]),
                    ("norm1_gT", [DEPTH, 128, 16]), ("norm2_gT", [DEPTH, 128, 16]), ("final_gT", [128, 16])):
        users = WEIGHT_USERS[nm]
        used = sel is None or any((L, u) in sel for L in range(DEPTH + 1) for u in users)
        if not used:
            shp = [1] * len(shp)
        elif sel is not None and nm != "final_gT" and not any((1, u) in sel for u in users):
            shp = [1] + list(shp[1:])
        W[nm] = din(nm, shp)
    xin = din("xT", [16, 128, NT])
    Cd = {}
    for nm, shp, dt in (("cos", [128, NT], F32), ("sin", [128, NT], F32), ("cosc", [128, 256], F32), ("sinc", [128, 256], F32),
                        ("RT", [128, 128], BF), ("ident", [128, 128], BF), ("ones", [128, 128], BF), ("msel", [128, 2, 64], BF),
                        ("cmpmask", [128, 2, NT], BF), ("fb", [128, 16, 64], F32), ("expand", [64, 32, 128], BF),
                        ("cmaskS", [128, 2, 128], BF), ("wmask", [128, 6, 128], BF), ("flags", [128, 2], F32)):
        Cd[nm] = din("c_" + nm, shp, dt)
    outT = nc.dram_tensor("outT", [16, 128, NT], F32, kind="ExternalOutput").ap()

    T = {}
    T["qT"] = scr("qT", [16, 128, NT]).ap()
    T["fm"] = scr("fm", [80, 128, NT]).ap()
    for nm, shp in (("gk_in", [2048, NT]), ("gk", [4096, NT]), ("gv_in", [2048, 1024]), ("gv", [4096, 1024]),
                    ("gt_in", [2048, 32]), ("gt", [4096, 32])):
        t = scr(nm, shp)
        T[nm + "_t"] = t
        T[nm] = t.ap()
    T["oT"] = scr("oT", [16, 128, NT]).ap()
    T["maT"] = scr("maT", [16, 128, NT]).ap()
    T["mT"] = scr("mT", [16, 128, NT]).ap()
    T["xres"] = scr("xres", [16, 128, NT], F32).ap()
    if "dbg_k" in dbg:
        for nm, shp, dt in (("dbg_k", [128, S], BF), ("dbg_kw", [128, S], BF), ("dbg_v", [128, 32 * 130], BF),
                            ("dbg_vw", [128, 32 * 130], BF), ("dbg_bt", [64, 128], BF), ("dbg_imp", [128, 64], F32),
                            ("dbg_ds", [128, 12], F32), ("dbg_coef", [128, 12], F32), ("dbg_obf", [128, 512], BF),
                            ("dbg_m8", [128, 16], F32), ("dbg_kct", [128, 1024], BF), ("dbg_vca", [128, 4 * 2 * 194], BF),
                            ("dbg_gates", [128, 16 * 48], F32), ("dbg_osl", [128, 1024], F32), ("dbg_ow", [128, 1024], F32),
                            ("dbg_oacc", [128, 512], F32), ("dbg_tmpo", [128, 512], F32)):
            T[nm] = scr(nm, shp, dt).ap()

    with contextlib.ExitStack() as es:
        del SEM_POOL[:]
        for i in range(24):
            SEM_POOL.append([es.enter_context(nc.semaphore(f"sem{i}")), 0])
        actT = es.enter_context(nc.sbuf_tensor("actT", [128, 16, NT], BF))
        gates = es.enter_context(nc.sbuf_tensor("gates", [128, 16, 48], F32))
        KCT = es.enter_context(nc.sbuf_tensor("KCT", [128, 4, 256], BF))
        VCA = es.enter_context(nc.sbuf_tensor("VCA", [128, 4, 2, 194], BF))
        RT = es.enter_context(nc.sbuf_tensor("RTs", [128, 128], BF))
        ident = es.enter_context(nc.sbuf_tensor("idents", [128, 128], BF))
        ones = es.enter_context(nc.sbuf_tensor("oness", [128, 128], BF))
        T["gates"] = gates
        C = dict(Cd)
        C["RT"], C["ident"], C["ones"] = RT, ident, ones
        ph = Phase(nc, "init")
        s = ph.sem("s")
        ph.dma("sp", RT[:], Cd["RT"], [], s)
        ph.dma("sp", ident[:], Cd["ident"], [], s)
        ph.dma("sp", ones[:], Cd["ones"], [], s)
        ph.wait("sp", [(s, 48)])
        ph.run()
        xcur = xin
        for L in range(DEPTH):
            if on(L, "n1"):
                phase_norm(nc, f"n1_{L}", xcur, W["norm1_gT"][L], ones, dst_sb=actT)
            if on(L, "win"):
                phase_win(nc, f"win{L}", actT, W["w_in"][L], C, T)
            if on(L, "ag"):
                phase_gather(nc, f"ag{L}", T)
            if on(L, "cmp"):
                phase_compress(nc, f"cmp{L}", L, W, C, T, KCT, VCA)
            if on(L, "att"):
                phase_attn(nc, f"att{L}", C, T, KCT, VCA, gates)
            if on(L, "ap"):
                phase_proj(nc, f"ap{L}", actT, W["w_attn_proj"][L], "attn", T, C, src_act=T["oT"])
            if on(L, "cv"):
                phase_conv(nc, f"cv{L}", actT, W, L, C, T)
            if on(L, "co"):
                phase_proj(nc, f"co{L}", actT, W["w_conv_out"][L], "conv", T, C)
            if on(L, "wo"):
                phase_proj(nc, f"wo{L}", actT, W["w_o"][L], "resid", T, C, src_act=T["mT"], xsrc=xcur)
            xcur = T["xres"]
            if on(L, "n2"):
                phase_norm(nc, f"n2_{L}", xcur, W["norm2_gT"][L], ones, dst_sb=actT)
            if on(L, "ffn"):
                phase_ffn(nc, f"ffn{L}", actT, W, L, T)
        if on(DEPTH, "nf"):
            phase_norm(nc, "nf", xcur, W["final_gT"], ones, dst_dram=outT)
    return nc


def _bf(a):
    return np.ascontiguousarray(a.astype(ml_dtypes.bfloat16))


def _consts(p):
    f32 = np.float32
    tl = np.arange(NT)
    pos = (128 * (2 * (tl // 128) + p) + tl % 128)
    half = 64
    inv_freq = np.exp(-math.log(10000.0) * np.arange(half, dtype=f32) / half).astype(f32)
    ang = pos.astype(f32)[None, :] * inv_freq[:, None]
    c = {}
    c["cos"] = np.concatenate([np.cos(ang), np.cos(ang)], 0).astype(f32)
    c["sin"] = np.concatenate([np.sin(ang), np.sin(ang)], 0).astype(f32)
    cpos = (np.arange(256) * 16 + 31).astype(f32)
    angc = cpos[None, :] * inv_freq[:, None]
    c["cosc"] = np.concatenate([np.cos(angc), np.cos(angc)], 0).astype(f32)
    c["sinc"] = np.concatenate([np.sin(angc), np.sin(angc)], 0).astype(f32)
    RT = np.zeros((128, 128), f32)
    for d in range(64):
        RT[d + 64, d] = -1.0
        RT[d, d + 64] = 1.0
    c["RT"] = _bf(RT)
    c["ident"] = _bf(np.eye(128, dtype=f32))
    c["ones"] = _bf(np.ones((128, 128), f32))
    n = np.arange(256)
    cs = n * 16
    ss = np.arange(64) * 64
    ov = np.minimum(cs[:, None] + 32, ss[None, :] + 64) - np.maximum(cs[:, None], ss[None, :])
    msel = (np.clip(ov, 0, None) / 32.0).astype(f32)
    msel[255] = 0.0
    c["msel"] = _bf(msel.reshape(2, 128, 64).transpose(1, 0, 2))
    cm = np.where((n[:, None] * 16 + 31 <= pos[None, :]) & (n[:, None] < 255), 0.0, NEG).astype(f32)
    c["cmpmask"] = _bf(cm.reshape(2, 128, NT).transpose(1, 0, 2))
    tb = pos // 64
    m = np.arange(64)
    valid = m[None, :] <= tb[:, None]
    forced = (m[None, :] == 0) | (m[None, :] == tb[:, None]) | (m[None, :] == tb[:, None] - 1)
    fb = np.where(valid, np.where(forced, 1e4, 0.0), -1e4).astype(f32)
    c["fb"] = np.ascontiguousarray(fb.reshape(16, 128, 64).transpose(1, 0, 2))
    ex = np.zeros((64, 32, 128), f32)
    for kt in range(32):
        ex[2 * kt, kt, 0:64] = 1.0
        ex[2 * kt + 1, kt, 64:128] = 1.0
    c["expand"] = _bf(ex)
    k = np.arange(128)
    causal = np.where(k[:, None] <= k[None, :], 0.0, NEG).astype(f32)
    allm = np.full((128, 128), NEG, f32)
    zero = np.zeros((128, 128), f32)
    anti = np.where(k[:, None] > k[None, :], 0.0, NEG).astype(f32)
    if p == 0:
        cms = [causal, allm]
        wm = [anti, zero, zero, zero, causal, allm]
    else:
        cms = [zero, causal]
        wm = [allm, anti, zero, zero, zero, causal]
    c["cmaskS"] = _bf(np.stack(cms, 1))
    c["wmask"] = _bf(np.stack(wm, 1))
    fl = np.zeros((128, 2), f32)
    fl[:, p] = 1.0
    c["flags"] = fl
    return c


def kernel(**inputs):
    f32 = np.float32
    x = np.asarray(inputs["x"], f32)

    def gT(a):
        a = np.asarray(a, f32)
        return np.ascontiguousarray(np.swapaxes(a.reshape(a.shape[:-1] + (16, 128)), -1, -2))

    shared = {
        "w_in": np.asarray(inputs["w_in"], f32),
        "cmp_w1_k": np.asarray(inputs["cmp_w1_k"], f32), "cmp_w2_k": np.asarray(inputs["cmp_w2_k"], f32),
        "cmp_pos_kT": np.ascontiguousarray(np.swapaxes(np.asarray(inputs["cmp_pos_k"], f32), 1, 2)),
        "cmp_w1_v": np.asarray(inputs["cmp_w1_v"], f32), "cmp_w2_v": np.asarray(inputs["cmp_w2_v"], f32),
        "cmp_pos_vT": np.ascontiguousarray(np.swapaxes(np.asarray(inputs["cmp_pos_v"], f32), 1, 2)),
        "conv_wT": np.ascontiguousarray(np.asarray(inputs["conv_w"], f32).reshape(DEPTH, 3, 16, 128).transpose(0, 3, 2, 1)),
        "w_attn_proj": np.asarray(inputs["w_attn_proj"], f32), "w_conv_out": np.asarray(inputs["w_conv_out"], f32),
        "w_o": np.asarray(inputs["w_o"], f32), "w_up": np.asarray(inputs["w_up"], f32), "w_down": np.asarray(inputs["w_down"], f32),
        "norm1_gT": gT(inputs["norm1_g"]), "norm2_gT": gT(inputs["norm2_g"]), "final_gT": gT(inputs["final_g"]),
    }
    consts = [_consts(0), _consts(1)]
    in_maps = []
    for c in range(8):
        b, p = c // 2, c % 2
        xo = x[b].reshape(16, 2, 128, D)[:, p].reshape(NT, D)
        m = dict(shared)
        m["xT"] = np.ascontiguousarray(xo.T.reshape(16, 128, NT))
        for k, v in consts[p].items():
            m["c_" + k] = v
        in_maps.append(m)
    nc = build()
    res = run_bass_kernel_spmd(nc, in_maps, core_ids=list(range(8)))
    out = np.empty((4, S, D), f32)
    for c in range(8):
        b, p = c // 2, c % 2
        o = np.asarray(res.results[c]["outT"], f32).reshape(D, NT).T
        out[b].reshape(16, 2, 128, D)[:, p] = o.reshape(16, 128, D)
    return out
```

```python
import contextlib
import math

import ml_dtypes
import numpy as np

import concourse.bass as bass
import concourse.mybir as mybir
from concourse.bass_utils import run_bass_kernel_spmd

F32 = mybir.dt.float32
BF = mybir.dt.bfloat16
ALU = mybir.AluOpType
AF = mybir.ActivationFunctionType

D = 2048
S = 4096
NT = 2048
DEPTH = 2
NH = 16
DFF = 8192
IN_COLS = 15408
NEG = -30000.0
PAIRS = [[0, 1], [2, 3], [4, 5], [6, 7]]
SCALE = 128 ** -0.5


def gk_row(r, slab):
    return (slab // 4) * 1024 + r * 512 + (slab % 4) * 128


def gv_row(r, t):
    return (t // 1024) * 2048 + r * 1024 + (t % 1024)


class Sem:
    def __init__(self, slot):
        self.slot = slot
        self.h = slot[0]
        self.base = slot[1]
        self.n = 0


SEM_POOL = []


class Phase:
    def __init__(self, nc, name):
        self.nc = nc
        self.name = name
        self.es = contextlib.ExitStack()
        self.q = {e: [] for e in ("pe", "act", "dve", "pool", "sp")}
        self.k = 0
        self.sems = []

    def sem(self, nm):
        sm = Sem(SEM_POOL[self.k])
        self.k += 1
        self.sems.append(sm)
        return sm

    def sb(self, nm, shape, dt):
        return self.es.enter_context(self.nc.sbuf_tensor(f"{self.name}_{nm}", shape, dt))

    def ps(self, nm, shape, dt=F32):
        return self.es.enter_context(self.nc.psum_tensor(f"{self.name}_{nm}", shape, dt))

    def op(self, eng, fn, waits=(), sig=None, inc=1):
        waits = [(s, v) for (s, v) in waits if v > 0]

        def thunk(e, fn=fn, waits=waits, sig=sig, inc=inc):
            for s, v in waits:
                e.wait_ge(s.h, s.base + v)
            ins = fn(e)
            if sig is not None:
                ins.then_inc(sig.h, inc)

        self.q[eng].append(thunk)
        if sig is not None:
            sig.n += inc
            return sig.n
        return 0

    def dma(self, eng, out, in_, waits=(), sig=None):
        return self.op(eng, lambda e, out=out, in_=in_: e.dma_start(out=out, in_=in_), waits, sig, 16)

    def wait(self, eng, waits):
        waits = [(s, v) for (s, v) in waits if v > 0]

        def thunk(e, waits=waits):
            for s, v in waits:
                e.wait_ge(s.h, s.base + v)

        self.q[eng].append(thunk)

    def run(self):
        q = self.q
        with self.nc.Block() as block:
            @block.tensor
            def _(e):
                for f in q["pe"]:
                    f(e)

            @block.scalar
            def _(e):
                for f in q["act"]:
                    f(e)

            @block.vector
            def _(e):
                for f in q["dve"]:
                    f(e)

            @block.gpsimd
            def _(e):
                for f in q["pool"]:
                    f(e)

            @block.sync
            def _(e):
                for f in q["sp"]:
                    f(e)
        for sm in self.sems:
            sm.slot[1] += sm.n
        self.es.close()


def phase_norm(nc, name, xsrc, gain_ap, ones_bf, dst_sb=None, dst_dram=None):
    ph = Phase(nc, name)
    xb = [ph.sb(f"xb{i}", [128, 16, 512], F32) for i in range(2)]
    sq = ph.sb("sq", [128, 16, 512], BF)
    g = ph.sb("g", [128, 16], F32)
    tmp = ph.sb("tmp", [128, 512], F32)
    tmp2 = ph.sb("tmp2", [128, 512], F32)
    rstd = ph.sb("rstd", [128, 512], F32)
    pss = ph.ps("ss", [128, 512])
    ob = ph.sb("ob", [128, 16, 512], F32) if dst_dram is not None else None
    s_ld = [ph.sem("ld"), ph.sem("ld")]
    s_g = ph.sem("g")
    s_act = ph.sem("act")
    s_pe = ph.sem("pe")
    s_dve = ph.sem("dve")
    s_st = ph.sem("st")
    ph.dma("sp", g[:], gain_ap, sig=s_g)
    dve_done = {}
    pe_done = {}
    dve_a = {}
    for tb in range(4):
        b = tb % 2
        w = [(s_dve, dve_done[tb - 2])] if tb >= 2 else []
        ld = ph.dma("sp", xb[b][:], xsrc[:, :, tb * 512:(tb + 1) * 512].rearrange("k p t -> p k t"), w, s_ld[b])
        for kc in range(16):
            w = []
            if kc == 0:
                w = [(s_ld[b], ld)]
                if tb >= 1:
                    w.append((s_pe, pe_done[tb - 1]))
            a_sq = ph.op("act", lambda e, kc=kc, b=b: e.activation(out=sq[:, kc, :], in_=xb[b][:, kc, :], func=AF.Square),
                         w, s_act if kc == 15 else None)
        for kc in range(16):
            w = []
            if kc == 0:
                w = [(s_act, a_sq)]
                if tb >= 1:
                    w.append((s_dve, dve_a[tb - 1]))
            pe_done[tb] = ph.op("pe", lambda e, kc=kc: e.matmul(pss[:], ones_bf[:], sq[:, kc, :], start=(kc == 0), stop=(kc == 15)),
                                w, s_pe if kc == 15 else None)
        dve_a[tb] = ph.op("dve", lambda e: e.tensor_scalar(out=tmp[:], in0=pss[:], scalar1=1.0 / D, scalar2=1e-6,
                                                           op0=ALU.mult, op1=ALU.add),
                          [(s_pe, pe_done[tb])], s_dve)
        a_sqrt = ph.op("act", lambda e: e.activation(out=tmp2[:], in_=tmp[:], func=AF.Sqrt), [(s_dve, dve_a[tb])], s_act)
        d_b = ph.op("dve", lambda e: e.reciprocal(out=rstd[:], in_=tmp2[:]), [(s_act, a_sqrt)], s_dve)
        for kc in range(16):
            w = []
            if kc == 0:
                w = [(s_dve, d_b), (s_g, 16)]
                if ob is not None and tb >= 1:
                    w.append((s_st, 16 * tb))
            if ob is None:
                o = dst_sb[:, kc, tb * 512:(tb + 1) * 512]
            else:
                o = ob[:, kc, :]
            dve_done[tb] = ph.op("dve", lambda e, kc=kc, b=b, o=o: e.scalar_tensor_tensor(
                out=o, in0=xb[b][:, kc, :], scalar=g[:, kc:kc + 1], in1=rstd[:], op0=ALU.mult, op1=ALU.mult),
                w, s_dve if kc == 15 else None)
        if ob is not None:
            ph.dma("sp", dst_dram[:, :, tb * 512:(tb + 1) * 512].rearrange("k p t -> p k t"), ob[:],
                   [(s_dve, dve_done[tb])], s_st)
    if ob is not None:
        ph.wait("sp", [(s_st, 64)])
    ph.run()


class Gemm:
    NPS = 4

    def __init__(self, ph, nk, wcols=512, pfx=""):
        self.ph = ph
        self.nk = nk
        self.wb = [ph.sb(f"{pfx}w{i}", [128, nk, wcols], BF) for i in range(2)]
        self.s_w = [ph.sem("w"), ph.sem("w")]
        self.s_pe = ph.sem("gpe")
        self.s_ep = ph.sem("gep")
        self.psb = [ph.ps(f"{pfx}g{i}", [128, 512]) for i in range(self.NPS)]
        self.T = 0
        self.J = 0
        self.job_pe_end = {}
        self.ep_of_tile = {}

    def plan(self, wsrcs):
        self.wsrcs = list(wsrcs)
        self.loaded = {}
        self.nxt = 0
        self.cur = 0

    def next_weights(self):
        J = self.nxt
        self.nxt += 1
        self.cur = J
        if J not in self.loaded:
            self.loaded[J] = self.load_w(self.wsrcs[J])
        if J + 1 < len(self.wsrcs) and (J + 1) not in self.loaded:
            self.loaded[J + 1] = self.load_w(self.wsrcs[J + 1])
        return self.loaded[J]

    def load_w(self, wsrc):
        ph = self.ph
        J = self.J
        b = J % 2
        ncols = wsrc.shape[1]
        w = [(self.s_pe, self.job_pe_end[J - 2])] if J >= 2 else []
        v = ph.dma("pool", self.wb[b][:, :, 0:ncols], wsrc.rearrange("(k p) c -> p k c", p=128), w, self.s_w[b])
        self.J += 1
        return b, v

    def tile(self, mm_list, evac, first_waits=()):
        ph = self.ph
        T = self.T
        ps = self.psb[T % self.NPS]
        n = len(mm_list)
        M = mm_list[0][0].shape[-1]
        N = mm_list[0][1].shape[-1]
        pso = ps[0:M, 0:N]
        for k, (l, r) in enumerate(mm_list):
            w = []
            if k == 0:
                w = list(first_waits)
                if T >= self.NPS:
                    w.append((self.s_ep, self.ep_of_tile[T - self.NPS]))
            v = ph.op("pe", lambda e, l=l, r=r, k=k, pso=pso: e.matmul(pso, l, r, start=(k == 0), stop=(k == n - 1)),
                      w, self.s_pe if k == n - 1 else None)
        before = self.s_ep.n
        evac(pso, [(self.s_pe, v)])
        assert self.s_ep.n == before + 1
        self.ep_of_tile[T] = self.s_ep.n
        self.T += 1
        return T

    def end_job(self):
        self.job_pe_end[self.cur] = self.s_pe.n


class SlabOut:
    def __init__(self, ph, dt=BF, nbuf=2, name="stg"):
        self.ph = ph
        self.buf = [ph.sb(f"{name}{i}", [128, NT], dt) for i in range(nbuf)]
        self.s_st = [ph.sem("st") for _ in range(nbuf)]
        self.s_x = [ph.sem("sx") for _ in range(nbuf)]
        self.n = 0

    def begin(self):
        i = self.n % len(self.buf)
        self.n += 1
        return i, [(self.s_st[i], self.s_st[i].n), (self.s_x[i], self.s_x[i].n)]

    def store(self, i, dst, waits, eng="sp"):
        return self.ph.dma(eng, dst, self.buf[i][:], waits, self.s_st[i])

    def drain(self, eng="sp"):
        self.ph.wait(eng, [(s, s.n) for s in self.s_st])


def phase_win(nc, name, actT, w_in_l, C, T):
    ph = Phase(nc, name)
    G = Gemm(ph, 16)
    so = SlabOut(ph)
    cosT = ph.sb("cos", [128, NT], F32)
    sinT = ph.sb("sin", [128, NT], F32)
    zb = [ph.sb(f"zb{i}", [128, 512], BF) for i in range(2)]
    t1 = [ph.sb(f"t1{i}", [128, 512], F32) for i in range(2)]
    t2 = [ph.sb(f"t2{i}", [128, 512], F32) for i in range(2)]
    ps2 = [ph.ps(f"r{i}", [128, 512]) for i in range(2)]
    vst = [ph.sb(f"vst{i}", [128, 512], BF) for i in range(2)]
    tails_x = ph.sb("tlx", [128, 16, 32], BF)
    tails_c = ph.sb("tlc", [128, 16, 32], BF)
    tails_u = ph.sb("tlu", [128, 16, 32], BF)
    s_c = ph.sem("c")
    s_pe2 = ph.sem("pe2")
    s_dv = ph.sem("dv")
    s_pl = ph.sem("pl")
    s_vst = [ph.sem("vst"), ph.sem("vst")]
    s_tl = ph.sem("tl")
    ph.dma("sp", cosT[:], C["cos"], sig=s_c)
    ph.dma("sp", sinT[:], C["sin"], sig=s_c)
    RT = C["RT"]
    rope_n = [0]
    pool_of_rope = {}
    pend = [None]
    last_rope = [None]

    def fm_job(c0, kind, dests, tails=None):
        b, wv = G.next_weights()
        for cb in range(4):
            slot, free_w = so.begin()
            last = None
            for tb in range(4):
                mm = [(G.wb[b][:, kc, cb * 128:(cb + 1) * 128], actT[:, kc, tb * 512:(tb + 1) * 512]) for kc in range(16)]
                fw = [(G.s_w[b], wv)] if (cb == 0 and tb == 0) else []
                dst = so.buf[slot][:, tb * 512:(tb + 1) * 512]
                if kind in ("copy", "sigmoid"):
                    func = AF.Copy if kind == "copy" else AF.Sigmoid

                    def evac(ps, w, dst=dst, func=func, tb=tb, free_w=free_w):
                        ww = list(w) + (free_w if tb == 0 else [])
                        ph.op("act", lambda e: e.activation(out=dst, in_=ps, func=func), ww, G.s_ep)

                    G.tile(mm, evac, fw)
                    last = (G.s_ep, G.s_ep.n)
                else:
                    n = rope_n[0]
                    rb = n % 2
                    rope_n[0] += 1

                    def evac(ps, w, rb=rb, n=n):
                        ww = list(w)
                        if n >= 2:
                            ww.append((s_pl, pool_of_rope[n - 2]))
                        ph.op("act", lambda e: e.activation(out=zb[rb][:], in_=ps, func=AF.Copy), ww, G.s_ep)

                    G.tile(mm, evac, fw)
                    epv = G.s_ep.n
                    if pend[0] is not None:
                        pend[0]()
                        pend[0] = None

                    def post_rope(epv=epv, rb=rb, tb=tb, dst=dst, n=n, fw0=(free_w if tb == 0 else [])):
                        pv = ph.op("pe", lambda e: e.matmul(ps2[rb][:], RT[:], zb[rb][:], start=True, stop=True),
                                   [(G.s_ep, epv)], s_pe2)
                        ph.op("pool", lambda e: e.tensor_tensor(out=t1[rb][:], in0=zb[rb][:],
                                                                in1=cosT[:, tb * 512:(tb + 1) * 512], op=ALU.mult),
                              [(G.s_ep, epv), (s_c, 32)])
                        dv = ph.op("dve", lambda e: e.tensor_tensor(out=t2[rb][:], in0=ps2[rb][:],
                                                                    in1=sinT[:, tb * 512:(tb + 1) * 512], op=ALU.mult),
                                   [(s_pe2, pv), (s_c, 32)], s_dv)
                        pl = ph.op("pool", lambda e: e.tensor_tensor(out=dst, in0=t1[rb][:], in1=t2[rb][:], op=ALU.add),
                                   [(s_dv, dv)] + list(fw0), s_pl)
                        pool_of_rope[n] = pl
                        last_rope[0] = (s_pl, pl)

                    pend[0] = post_rope
            if kind == "rope":
                pend[0]()
                pend[0] = None
                last = last_rope[0]
            if tails is not None:
                tl, idx = tails
                ph.op("pool", lambda e, slot=slot, tl=tl, idx=idx: e.tensor_copy(
                    out=tl[:, idx, :].rearrange("p (i t) -> p i t", t=2),
                    in_=so.buf[slot][:].rearrange("p (i j) -> p i j", j=128)[:, :, 126:128]),
                    [last], so.s_x[slot])
                tails = (tl, idx + 1)
            so.store(slot, dests[cb], [last])
        G.end_job()

    def tok_job(c0, ncols, kind, dst_fn):
        b, wv = G.next_weights()
        for i in range(16):
            mm = [(actT[:, kc, i * 128:(i + 1) * 128], G.wb[b][:, kc, 0:ncols]) for kc in range(16)]
            fw = [(G.s_w[b], wv)] if i == 0 else []
            if kind == "v":
                vb = i % 2

                def evac(ps, w, vb=vb):
                    ph.op("act", lambda e: e.activation(out=vst[vb][:], in_=ps, func=AF.Copy),
                          list(w) + [(s_vst[vb], s_vst[vb].n)], G.s_ep)

                G.tile(mm, evac, fw)
                ph.dma("sp", dst_fn(i), vst[vb][:], [(G.s_ep, G.s_ep.n)], s_vst[vb])
            else:
                def evac(ps, w, i=i):
                    ph.op("act", lambda e: e.activation(out=dst_fn(i), in_=ps, func=AF.Sigmoid), w, G.s_ep)

                G.tile(mm, evac, fw)
        G.end_job()

    gk = T["gk_in"]
    fm = T["fm"]
    cols = [(g * 512, 512) for g in range(4)] + [(2048 + j * 512, 512) for j in range(6)] + [(5120, 48)]
    cols += [(5168 + j * 512, 512) for j in range(12)] + [(11312 + j * 512, 512) for j in range(8)]
    G.plan([w_in_l[:, c0:c0 + n] for c0, n in cols])
    for g in range(4):
        fm_job(g * 512, "rope", [T["qT"][4 * g + r] for r in range(4)])
    kv0 = 2048
    fm_job(kv0 + 0 * 512, "copy", [gk[(0 + g) * 128:(1 + g) * 128, :] for g in range(4)])
    fm_job(kv0 + 1 * 512, "copy", [gk[(4 + g) * 128:(5 + g) * 128, :] for g in range(4)])
    fm_job(kv0 + 2 * 512, "rope", [gk[(8 + g) * 128:(9 + g) * 128, :] for g in range(4)])
    tok_job(kv0 + 3 * 512, 512, "v", lambda i: T["gv_in"][i * 128:(i + 1) * 128, 0:512])
    fm_job(kv0 + 4 * 512, "rope", [gk[(12 + g) * 128:(13 + g) * 128, :] for g in range(4)])
    tok_job(kv0 + 5 * 512, 512, "v", lambda i: T["gv_in"][i * 128:(i + 1) * 128, 512:1024])
    tok_job(5120, 48, "ng", lambda i: T["gates"][:, i, :])
    cv0 = 5168
    for j in range(4):
        fm_job(cv0 + j * 512, "copy", [fm[4 * j + r] for r in range(4)], tails=(tails_x, 4 * j))
    for j in range(4):
        fm_job(cv0 + 2048 + j * 512, "copy", [fm[16 + 4 * j + r] for r in range(4)])
    for j in range(4):
        fm_job(cv0 + 4096 + j * 512, "copy", [fm[32 + 4 * j + r] for r in range(4)], tails=(tails_c, 4 * j))
    mg0 = 11312
    for j in range(8):
        fm_job(mg0 + j * 512, "sigmoid", [fm[48 + 4 * j + r] for r in range(4)])
    tl_w = [(s, s.n) for s in so.s_x]
    pl = ph.op("pool", lambda e: e.tensor_tensor(out=tails_u[:], in0=tails_x[:], in1=tails_c[:], op=ALU.mult), tl_w, s_tl)
    ph.dma("sp", T["gt_in"].rearrange("(s p) c -> p s c", p=128), tails_u[:], [(s_tl, pl)], s_tl)
    so.drain()
    ph.wait("sp", [(s_tl, s_tl.n), (s_vst[0], s_vst[0].n), (s_vst[1], s_vst[1].n)])
    ph.run()


def phase_gather(nc, name, T):
    ph = Phase(nc, name)
    s = ph.sem("cc")
    for a, b, rows, rc in (("gk_in", "gk", 2048, 512), ("gv_in", "gv", 2048, 1024), ("gt_in", "gt", 2048, 2048)):
        for k in range(rows // rc):
            if rc == rows:
                i_ap = T[a + "_t"].ap().opt()
                o_ap = T[b + "_t"].ap().opt()
            else:
                i_ap = T[a + "_t"].ap()[k * rc:(k + 1) * rc, :].opt()
                o_ap = T[b + "_t"].ap()[2 * k * rc:2 * (k + 1) * rc, :].opt()
            ph.op("pool", lambda e, i_ap=i_ap, o_ap=o_ap: e.collective_compute("AllGather", ALU.bypass, PAIRS,
                                                                               ins=[i_ap], outs=[o_ap]), [], s, 1)
            ph.wait("pool", [(s, s.n)])
    ph.run()


def phase_compress(nc, name, L, W, C, T, KCT, VCA):
    ph = Phase(nc, name)
    gk = T["gk"]
    xin = [ph.sb(f"xin{i}", [128, S], BF) for i in range(2)]
    w1 = [ph.sb(f"w1{i}", [128, 32, 256], BF) for i in range(2)]
    w2 = [ph.sb(f"w2{i}", [128, 2, 128], BF) for i in range(2)]
    posT = [ph.sb(f"pos{i}", [128, 32], F32) for i in range(2)]
    posb = [ph.sb(f"posb{i}", [128, 32], BF) for i in range(2)]
    cbias = [ph.sb(f"cb{i}", [128, 2], F32) for i in range(2)]
    hid = ph.sb("hid", [128, 2, 256], BF)
    zb = ph.sb("zb", [128, 256], BF)
    t1 = ph.sb("t1", [128, 256], F32)
    t2 = ph.sb("t2", [128, 256], F32)
    cosc = ph.sb("cosc", [128, 256], F32)
    sinc = ph.sb("sinc", [128, 256], F32)
    pb = ph.ps("pb", [128, 2])
    phid = [ph.ps(f"ph{i}", [128, 256]) for i in range(2)]
    pk = ph.ps("pk", [128, 256])
    pr = ph.ps("pr", [128, 256])
    pv = ph.ps("pv", [128, 2, 128])
    s_ld = ph.sem("ld")
    s_x = [ph.sem("x"), ph.sem("x")]
    s_pe = ph.sem("pe")
    s_act = ph.sem("act")
    s_dve = ph.sem("dve")
    s_pl = ph.sem("pl")
    RT = C["RT"]
    names = [("cmp_w1_k", "cmp_w2_k", "cmp_pos_kT"), ("cmp_w1_v", "cmp_w2_v", "cmp_pos_vT")]
    for kv in range(2):
        ph.dma("pool", w1[kv][:], W[names[kv][0]][L].rearrange("(l p) c -> p l c", p=128), sig=s_ld)
        ph.dma("pool", w2[kv][:], W[names[kv][1]][L].rearrange("(j p) c -> p j c", p=128), sig=s_ld)
        ph.dma("sp", posT[kv][:], W[names[kv][2]][L], sig=s_ld)
    ph.dma("sp", cosc[:], C["cosc"], sig=s_ld)
    ph.dma("sp", sinc[:], C["sinc"], sig=s_ld)
    ph.op("dve", lambda e: e.memset(KCT[:], 0.0), [], s_dve)
    ph.op("dve", lambda e: e.memset(VCA[:], 0.0), [], s_dve)
    ph.wait("dve", [(s_dve, s_dve.n)])
    ph.op("dve", lambda e: e.memset(VCA[:, :, 0, 128:129], 1.0), [], s_dve)
    ph.op("dve", lambda e: e.memset(VCA[0:127, :, 1, 128:129], 1.0), [], s_dve)
    ph.wait("pool", [(s_dve, s_dve.n)])
    for g in range(4):
        ph.dma("pool", VCA[:, g, :, 129:193], C["msel"], sig=s_ld)
    LD_ALL = 16 * 12
    for kv in range(2):
        d0 = ph.op("dve", lambda e, kv=kv: e.tensor_copy(out=posb[kv][:], in_=posT[kv][:]), [(s_ld, LD_ALL)], s_dve)
        for jc in range(2):
            for l in range(32):
                w = [(s_dve, d0), (s_act, s_act.n)] if l == 0 else []
                v = ph.op("pe", lambda e, kv=kv, jc=jc, l=l: e.matmul(pb[:, jc:jc + 1], w1[kv][:, l, jc * 128:(jc + 1) * 128],
                                                                       posb[kv][:, l:l + 1], start=(l == 0), stop=(l == 31)),
                          w, s_pe if l == 31 else None)
        ph.op("act", lambda e, kv=kv: e.activation(out=cbias[kv][:], in_=pb[:], func=AF.Copy), [(s_pe, v)], s_act)
    pe_end = {}
    it = 0
    for g in range(4):
        for kv in range(2):
            xb = it % 2
            slab = kv * 4 + g
            w = [(s_pe, pe_end[it - 2])] if it >= 2 else []
            for r in range(2):
                ph.dma("sp", xin[xb][:].rearrange("p (i r j) -> p i r j", r=2, j=128)[:, :, r, :],
                       gk[gk_row(r, slab): gk_row(r, slab) + 128, :].rearrange("p (i j) -> p i j", j=128),
                       w, s_x[xb])
            xv = s_x[xb].n
            for jc in range(2):
                for l in range(32):
                    w = []
                    if l == 0:
                        w = [(s_x[xb], xv), (s_act, s_act.n)]
                    rhs = xin[xb][:, l:l + 16 * 254 + 1:16]
                    v = ph.op("pe", lambda e, kv=kv, jc=jc, l=l, rhs=rhs: e.matmul(
                        phid[jc][:, 0:255], w1[kv][:, l, jc * 128:(jc + 1) * 128], rhs, start=(l == 0), stop=(l == 31)),
                        w, s_pe if l == 31 else None)
                ph.op("act", lambda e, kv=kv, jc=jc: e.activation(out=hid[:, jc, 0:255], in_=phid[jc][:, 0:255], func=AF.Silu,
                                                                  bias=cbias[kv][:, jc:jc + 1]),
                      [(s_pe, v), (s_pe, s_pe.n)], s_act)
            av = s_act.n
            if kv == 0:
                for jc in range(2):
                    v = ph.op("pe", lambda e, jc=jc: e.matmul(pk[:, 0:255], w2[0][:, jc, :], hid[:, jc, 0:255],
                                                               start=(jc == 0), stop=(jc == 1)),
                              [(s_act, av), (s_dve, s_dve.n), (s_pl, s_pl.n)] if jc == 0 else [], s_pe if jc == 1 else None)
                a2 = ph.op("act", lambda e: e.activation(out=zb[:, 0:255], in_=pk[:, 0:255], func=AF.Copy), [(s_pe, v)], s_act)
                v2 = ph.op("pe", lambda e: e.matmul(pr[:, 0:255], RT[:], zb[:, 0:255], start=True, stop=True), [(s_act, a2)], s_pe)
                ph.op("pool", lambda e: e.tensor_tensor(out=t1[:, 0:255], in0=zb[:, 0:255], in1=cosc[:, 0:255], op=ALU.mult),
                      [(s_act, a2)])
                dv = ph.op("dve", lambda e: e.tensor_tensor(out=t2[:, 0:255], in0=pr[:, 0:255], in1=sinc[:, 0:255], op=ALU.mult),
                           [(s_pe, v2)], s_dve)
                ph.op("pool", lambda e, g=g: e.tensor_tensor(out=KCT[:, g, 0:255], in0=t1[:, 0:255], in1=t2[:, 0:255], op=ALU.add),
                      [(s_dve, dv)], s_pl)
            else:
                for nt in range(2):
                    M = 128 if nt == 0 else 127
                    for jc in range(2):
                        v = ph.op("pe", lambda e, jc=jc, nt=nt, M=M: e.matmul(pv[0:M, nt, :], hid[:, jc, nt * 128:nt * 128 + M],
                                                                               w2[1][:, jc, :], start=(jc == 0), stop=(jc == 1)),
                                  [(s_act, av)] if (jc == 0 and nt == 0) else [], s_pe if (jc == 1 and nt == 1) else None)
                ph.op("act", lambda e, g=g: e.activation(out=VCA[:, g, 0, 0:128], in_=pv[:, 0, :], func=AF.Copy), [(s_pe, v)], None)
                ph.op("act", lambda e, g=g: e.activation(out=VCA[0:127, g, 1, 0:128], in_=pv[0:127, 1, :], func=AF.Copy), [], s_act)
            pe_end[it] = s_pe.n
            it += 1
    ph.wait("sp", [(s_act, s_act.n), (s_pl, s_pl.n), (s_ld, LD_ALL)])
    ph.run()


class SPipe:
    def __init__(self, ph):
        self.ph = ph
        self.buf = [ph.ps("sA", [128, 512]), ph.ps("sB", [128, 512])]
        self.s_qk = ph.sem("qk")
        self.s_ac = ph.sem("ac")
        self.k = 0
        self.ac_of = {}
        self.pending = None

    def tile(self, mm, M, N, act, post=None, first_waits=(), three_d=True):
        ph = self.ph
        k = self.k
        ps = self.buf[k % 2][0:M, 0:N]
        n = len(mm)
        v = 0
        for j, (l, r) in enumerate(mm):
            w = []
            if j == 0:
                w = list(first_waits)
                if k >= 2:
                    w.append((self.s_ac, self.ac_of[k - 2]))
            o = ps.rearrange("p (r q) -> p r q", r=4) if (three_d and len(r.shape) == 3) else ps
            v = ph.op("pe", lambda e, l=l, r=r, j=j, o=o: e.matmul(o, l, r, start=(j == 0), stop=(j == n - 1)),
                      w, self.s_qk if j == n - 1 else None)
        before = self.s_ac.n
        act(ps, [(self.s_qk, v)])
        assert self.s_ac.n == before + 1
        self.ac_of[k] = self.s_ac.n
        self.flush()
        if post is not None:
            acv = self.s_ac.n
            self.pending = lambda: post([(self.s_ac, acv)])
        self.k += 1

    def flush(self):
        if self.pending is not None:
            p = self.pending
            self.pending = None
            p()


def phase_attn(nc, name, C, T, KCT, VCA, gates):
    ph = Phase(nc, name)
    gk, gv = T["gk"], T["gv"]
    ident = C["ident"]
    qt0 = ph.sb("qt0", [128, 4, NT], BF)
    qt = [qt0, qt0]
    ksel = [ph.sb(f"ks{i}", [128, S], BF) for i in range(2)]
    kwin = [ph.sb(f"kw{i}", [128, S], BF) for i in range(2)]
    vsel = [ph.sb(f"vs{i}", [128, 32, 130], BF) for i in range(2)]
    vwin = [ph.sb(f"vw{i}", [128, 32, 130], BF) for i in range(2)]
    cmpmask = ph.sb("cmpm", [128, 2, NT], BF)
    fb = ph.sb("fb", [128, 16, 64], F32)
    expand = ph.sb("exp", [64, 32, 128], BF)
    cmaskS = ph.sb("cms", [128, 2, 128], BF)
    wmask = ph.sb("wm", [128, 6, 128], BF)
    PcT = ph.sb("pct", [128, 2, 512], BF)
    PT = [ph.sb(f"pt{i}", [128, 512], BF) for i in range(3)]
    dsafe = ph.sb("dsafe", [128, 12], F32)
    rec = ph.sb("rec", [128, 12], F32)
    coef = ph.sb("coef", [128, 12], F32)
    impm = ph.sb("impm", [128, 64], F32)
    impr = ph.sb("impr", [128, 64], F32)
    m8 = ph.sb("m8", [128, 16], F32)
    biasq = ph.sb("biasq", [128, 64], BF)
    BiasT = ph.sb("biasT", [64, 128], BF)
    o_acc = ph.sb("oacc", [128, 4, 128], F32)
    tmpo = ph.sb("tmpo", [128, 4, 128], F32)
    o_bf = ph.sb("obf", [128, 4, 128], BF)
    oTs = ph.sb("oTs", [128, 4, NT], BF)
    oc = ph.ps("oc", [128, 4, 256])
    osl = ph.ps("os", [128, 4, 256])
    ow = ph.ps("ow", [128, 4, 256])
    sp = SPipe(ph)
    s_c = ph.sem("c")
    s_ld = [ph.sem("ld"), ph.sem("ld")]
    s_pv = ph.sem("pv")
    s_dv = ph.sem("dv")
    s_pl = ph.sem("pl")
    s_st = ph.sem("st")
    s_ms = ph.sem("ms")
    for dst, src in ((cmpmask, "cmpmask"), (fb, "fb"), (expand, "expand"), (cmaskS, "cmaskS"), (wmask, "wmask")):
        ph.dma("sp", dst[:], C[src], sig=s_c)
    NC_C = 16 * 5
    for b in range(2):
        ph.op("dve", lambda e, b=b: e.memset(vsel[b][:, :, 129:130], 0.0), [], s_ms)
        ph.op("dve", lambda e, b=b: e.memset(vwin[b][:, :, 129:130], 0.0), [], s_ms)
        ph.op("dve", lambda e, b=b: e.memset(vsel[b][:, :, 128:129], 1.0), [], s_ms)
        ph.op("dve", lambda e, b=b: e.memset(vwin[b][:, :, 128:129], 1.0), [], s_ms)
    grp_pe_end = {}
    pt_n = [0]
    pv_of_pt = {}
    late = [None]
    dv_last_oc = [0]
    dv_last_os = [0]
    dv_last_ow = [0]
    pl_last = [0]

    def load_group(g):
        b = g % 2
        w = [(s_pv, grp_pe_end[g - 2]), (sp.s_qk, grp_pe_end[("qk", g - 2)])] if g >= 2 else []
        for r in range(2):
            for dst, slab in ((ksel[b], 8 + g), (kwin[b], 12 + g)):
                ph.dma("sp", dst[:].rearrange("p (i r j) -> p i r j", r=2, j=128)[:, :, r, :],
                       gk[gk_row(r, slab): gk_row(r, slab) + 128, :].rearrange("p (i j) -> p i j", j=128),
                       w, s_ld[b])
            for dst, c0 in ((vsel[b], 0), (vwin[b], 512)):
                for hf in range(2):
                    r0 = gv_row(r, hf * 1024)
                    ph.dma("sp", dst[:].rearrange("p (i r) c -> p i r c", r=2)[:, hf * 8:(hf + 1) * 8, r, 0:128],
                           gv[r0:r0 + 1024, c0 + g * 128: c0 + (g + 1) * 128].rearrange("(i j) c -> j i c", j=128),
                           w, s_ld[b])
        return s_ld[b].n

    ldv = {0: load_group(0)}
    for g in range(4):
        b = g % 2
        wq = [(s_pv, grp_pe_end[g - 1]), (sp.s_qk, grp_pe_end[("qk", g - 1)])] if g >= 1 else []
        ph.dma("sp", qt0[:], T["qT"][4 * g:4 * g + 4].rearrange("h p t -> p h t"), wq, s_ld[b])
        ldv[g] = s_ld[b].n
        if g + 1 < 4:
            ldv[g + 1] = load_group(g + 1)
        for i in range(16):
            first = [(s_ld[b], ldv[g]), (s_c, NC_C), (s_ms, 8)] if i == 0 else []
            qti = qt[b][:, :, i * 128:(i + 1) * 128]
            for nt in range(2):
                mm = [(KCT[:, g, nt * 128:(nt + 1) * 128], qti),
                      (ident[:], cmpmask[:, nt, i * 128:(i + 1) * 128].unsqueeze(1).broadcast_to([128, 4, 128]))]

                def act(ps, w, nt=nt):
                    ww = list(w)
                    if nt == 0:
                        ww.append((s_pv, s_pv.n))
                    ph.op("act", lambda e: e.activation(out=PcT[:, nt, :], in_=ps, func=AF.Exp, scale=SCALE), ww, sp.s_ac)

                post = None
                if nt == 1:
                    def post(w, g=g):
                        ww = list(w) + [(s_dv, dv_last_oc[0])]
                        for r in range(4):
                            for n2 in range(2):
                                ph.op("pe", lambda e, r=r, n2=n2: e.matmul(oc[:, r, 0:194], PcT[:, n2, r * 128:(r + 1) * 128],
                                                                          VCA[:, g, n2, :], start=(n2 == 0), stop=(n2 == 1)),
                                      ww if (r == 0 and n2 == 0) else [], s_pv if (r == 3 and n2 == 1) else None)
                sp.tile(mm, 128, 512, act, post, first if nt == 0 else [])
            sp.flush()
            pv_c = s_pv.n
            def dchain(fn, w=()):
                v = ph.op("dve", fn, [(s_dv, s_dv.n)] + list(w), s_dv)
                return v
            dchain(lambda e: e.tensor_scalar(out=dsafe[:, 0:4], in0=oc[:, :, 128], scalar1=1e-30, scalar2=None, op0=ALU.max),
                   [(s_pv, pv_c), (s_pl, pl_last[0])])
            dchain(lambda e: e.reciprocal(out=rec[:, 0:4], in_=dsafe[:, 0:4]))
            for r in range(4):
                src1 = fb[:, i, :] if r == 0 else impm[:]
                dchain(lambda e, r=r, src1=src1: e.scalar_tensor_tensor(out=impm[:], in0=oc[:, r, 129:193], scalar=rec[:, r:r + 1],
                                                                        in1=src1, op0=ALU.mult, op1=ALU.add))
            dchain(lambda e: e.max(out=m8[:, 0:8], in_=impm[:]))
            dchain(lambda e: e.match_replace(out=impr[:], in_to_replace=m8[:, 0:8], in_values=impm[:], imm_value=-1e9))
            dchain(lambda e: e.max(out=m8[:, 8:16], in_=impr[:]))
            bq = dchain(lambda e: e.tensor_scalar(out=biasq[:], in0=impm[:], scalar1=m8[:, 15:16], scalar2=NEG,
                                                  op0=ALU.is_lt, op1=ALU.mult), [(sp.s_qk, sp.s_qk.n)])
            gsl = gates[:, i, 12 * g:12 * g + 12].rearrange("p (h c) -> p h c", c=3)
            dchain(lambda e, gsl=gsl: e.tensor_tensor(out=coef[:, 0:4], in0=rec[:, 0:4], in1=gsl[:, :, 0], op=ALU.mult))
            dv_last_oc[0] = dchain(lambda e: e.tensor_tensor(out=o_acc[:], in0=oc[:, :, 0:128],
                                                             in1=coef[:, 0:4].unsqueeze(2).broadcast_to([128, 4, 128]), op=ALU.mult))
            wt = [j for j in range(6) if 2 * i - 4 + j >= 0]
            for j in wt:
                kt = 2 * i - 4 + j
                mm = [(kwin[b][:, kt * 128:(kt + 1) * 128], qti)]
                if j not in (2, 3):
                    mm.append((ident[:], wmask[:, j, :].unsqueeze(1).broadcast_to([128, 4, 128])))
                n = pt_n[0]
                pt_n[0] += 1
                pb = n % 3

                def act(ps, w, pb=pb, n=n):
                    ww = list(w)
                    if n >= 3:
                        ww.append((s_pv, pv_of_pt[n - 3]))
                    ph.op("act", lambda e: e.activation(out=PT[pb][:], in_=ps, func=AF.Exp, scale=SCALE), ww, sp.s_ac)

                def post(w, pb=pb, n=n, kt=kt, j=j, b=b, wt=wt):
                    ww = list(w)
                    if j == wt[0]:
                        ww.append((s_dv, dv_last_ow[0]))
                    for r in range(4):
                        v = ph.op("pe", lambda e, r=r: e.matmul(ow[:, r, 0:130], PT[pb][:, r * 128:(r + 1) * 128], vwin[b][:, kt, :],
                                                                 start=(j == wt[0] and r in (0, 2)), stop=(j == wt[-1]),
                                                                 skip_group_check=True),
                                  ww if r == 0 else [], s_pv if r == 3 else None)
                    pv_of_pt[n] = v

                sp.tile(mm, 128, 512, act, post)
            if late[0] is not None:
                late[0]()
                late[0] = None
            def actb(ps, w):
                ph.op("act", lambda e: e.activation(out=BiasT[:], in_=ps, func=AF.Copy), list(w) + [(s_pv, s_pv.n)], sp.s_ac)
            sp.tile([(biasq[:], ident[:])], 64, 128, actb, None, [(s_dv, bq)], three_d=False)
            bias_ready = sp.s_ac.n
            nkt = 2 * i + 2
            for kt in range(nkt):
                mm = [(ksel[b][:, kt * 128:(kt + 1) * 128], qti),
                      (expand[:, kt, :], BiasT[:].unsqueeze(1).broadcast_to([64, 4, 128]))]
                if kt >= 2 * i:
                    mm.append((ident[:], cmaskS[:, kt - 2 * i, :].unsqueeze(1).broadcast_to([128, 4, 128])))
                n = pt_n[0]
                pt_n[0] += 1
                pb = n % 3

                def act(ps, w, pb=pb, n=n):
                    ww = list(w)
                    if n >= 3:
                        ww.append((s_pv, pv_of_pt[n - 3]))
                    ph.op("act", lambda e: e.activation(out=PT[pb][:], in_=ps, func=AF.Exp, scale=SCALE), ww, sp.s_ac)

                def post(w, pb=pb, n=n, kt=kt, nkt=nkt, b=b):
                    ww = list(w)
                    if kt == 0:
                        ww.append((s_dv, dv_last_os[0]))
                    for r in range(4):
                        v = ph.op("pe", lambda e, r=r: e.matmul(osl[:, r, 0:130], PT[pb][:, r * 128:(r + 1) * 128], vsel[b][:, kt, :],
                                                                 start=(kt == 0 and r in (0, 2)), stop=(kt == nkt - 1),
                                                                 skip_group_check=True),
                                  ww if r == 0 else [], s_pv if r == 3 else None)
                    pv_of_pt[n] = v

                sp.tile(mm, 128, 512, act, post, [(sp.s_ac, bias_ready)] if kt == 0 else [])
            sp.flush()
            pv_end = s_pv.n
            dchain(lambda e: e.tensor_scalar(out=dsafe[:, 4:8], in0=osl[:, :, 128], scalar1=1e-30, scalar2=None, op0=ALU.max),
                   [(s_pv, pv_end)])
            dchain(lambda e: e.tensor_scalar(out=dsafe[:, 8:12], in0=ow[:, :, 128], scalar1=1e-30, scalar2=None, op0=ALU.max))
            dchain(lambda e: e.reciprocal(out=rec[:, 4:12], in_=dsafe[:, 4:12]))
            dchain(lambda e, gsl=gsl: e.tensor_tensor(out=coef[:, 4:8], in0=rec[:, 4:8], in1=gsl[:, :, 1], op=ALU.mult))
            dchain(lambda e, gsl=gsl: e.tensor_tensor(out=coef[:, 8:12], in0=rec[:, 8:12], in1=gsl[:, :, 2], op=ALU.mult))
            d1 = dchain(lambda e: e.tensor_tensor(out=tmpo[:], in0=osl[:, :, 0:128],
                                                  in1=coef[:, 4:8].unsqueeze(2).broadcast_to([128, 4, 128]), op=ALU.mult),
                        [(s_pl, s_pl.n)])
            dv_last_os[0] = d1
            p1 = ph.op("pool", lambda e: e.tensor_tensor(out=o_acc[:], in0=o_acc[:], in1=tmpo[:], op=ALU.add), [(s_dv, d1)], s_pl)
            d2 = dchain(lambda e: e.tensor_tensor(out=tmpo[:], in0=ow[:, :, 0:128],
                                                  in1=coef[:, 8:12].unsqueeze(2).broadcast_to([128, 4, 128]), op=ALU.mult),
                        [(s_pl, p1)])
            dv_last_ow[0] = d2
            p2 = ph.op("pool", lambda e: e.tensor_tensor(out=o_bf[:], in0=o_acc[:], in1=tmpo[:], op=ALU.add),
                       [(s_dv, d2), (sp.s_qk, sp.s_qk.n)], s_pl)
            pl_last[0] = p2

            def do_late(g=g, i=i, p2=p2):
                def actT(ps, w):
                    ww = list(w)
                    if i == 0 and g >= 1:
                        ww.append((s_st, 16 * g))
                    ph.op("act", lambda e: e.activation(out=oTs[:, :, i * 128:(i + 1) * 128],
                                                        in_=ps.rearrange("p (r q) -> p r q", r=4), func=AF.Copy), ww, sp.s_ac)
                k = sp.k
                pso = sp.buf[k % 2]
                w0 = [(s_pl, p2)]
                if k >= 2:
                    w0.append((sp.s_ac, sp.ac_of[k - 2]))
                v = 0
                for r in range(4):
                    v = ph.op("pe", lambda e, r=r: e.matmul(pso[:, r * 128:(r + 1) * 128], o_bf[:, r, :], ident[:], start=True, stop=True),
                              w0 if r == 0 else [], sp.s_qk if r == 3 else None)
                actT(pso[:, :], [(sp.s_qk, v)])
                sp.ac_of[k] = sp.s_ac.n
                sp.flush()
                sp.k += 1

            late[0] = do_late
            if "dbg_k" in T and g == 0 and i == 1:
                sd = ph.sem("dbgs")
                w = [(s_pl, p2), (s_dv, s_dv.n)]
                for nm, src in (("dbg_k", ksel[b][:]), ("dbg_kw", kwin[b][:]),
                                ("dbg_v", vsel[b][:].rearrange("p a c -> p (a c)")), ("dbg_vw", vwin[b][:].rearrange("p a c -> p (a c)")),
                                ("dbg_bt", BiasT[:]), ("dbg_imp", impm[:]), ("dbg_ds", dsafe[:]), ("dbg_coef", coef[:]),
                                ("dbg_obf", o_bf[:].rearrange("p a c -> p (a c)")), ("dbg_m8", m8[:]),
                                ("dbg_kct", KCT[:].rearrange("p a c -> p (a c)")), ("dbg_vca", VCA[:].rearrange("p a b c -> p (a b c)")),
                                ("dbg_gates", gates[:].rearrange("p a c -> p (a c)"))):
                    ph.dma("sp", T[nm], src, w, sd)
                ph.wait("sp", [(sd, sd.n)])
        late[0]()
        late[0] = None
        grp_pe_end[g] = s_pv.n
        grp_pe_end[("qk", g)] = sp.s_qk.n
        ph.dma("sp", T["oT"][4 * g:4 * g + 4].rearrange("h p t -> p h t"), oTs[:], [(sp.s_ac, sp.s_ac.n)], s_st)
    ph.wait("sp", [(s_st, 64)])
    ph.run()


def load_act(ph, actT, src, sem, nk=16):
    for k in range(0, nk, 4):
        ph.dma("sp", actT[:, k:k + 4, :], src[k:k + 4].rearrange("k p t -> p k t"), [], sem)
    return sem.n


def phase_proj(nc, name, actT, wsrc, mode, T, C, src_act=None, xsrc=None):
    ph = Phase(nc, name)
    G = Gemm(ph, 16)
    fm = T["fm"]
    s_a = ph.sem("a")
    av = load_act(ph, actT, src_act, s_a) if src_act is not None else 0
    yf = [ph.sb(f"yf{i}", [128, 512], F32) for i in range(2)]
    s_po = ph.sem("po")
    s_l = [ph.sem("l"), ph.sem("l")]
    if mode == "resid":
        so = SlabOut(ph, F32, 2, "xs")
        xl = [ph.sb(f"xl{i}", [128, NT], F32) for i in range(2)]
        aux = aux2 = None
    else:
        so = SlabOut(ph, BF, 2)
        aux = [ph.sb(f"ax{i}", [128, NT], BF) for i in range(2)]
        aux2 = [ph.sb(f"ay{i}", [128, NT], BF) for i in range(2)] if mode == "conv" else None
        tt = [ph.sb(f"tt{i}", [128, 512], F32) for i in range(2)]
    s_p2 = ph.sem("p2")
    tn = 0
    post_of = {}
    lv_of = {}

    def issue_load(c):
        sl = c % 2
        lw = [(s_po, post_of.get(c - 2, 0))]
        if mode == "resid":
            ph.dma("sp", xl[sl][:], xsrc[c], lw, s_l[sl])
        else:
            ph.dma("sp", aux[sl][:], fm[(48 if mode == "attn" else 64) + c], lw, s_l[sl])
            if mode == "conv":
                ph.dma("sp", aux2[sl][:], T["maT"][c], lw, s_l[sl])
        lv_of[c] = s_l[sl].n

    G.plan([wsrc[:, j * 512:(j + 1) * 512] for j in range(4)])
    for j in range(4):
        b, wv = G.next_weights()
        for cb in range(4):
            c = 4 * j + cb
            slot, free_w = so.begin()
            if c == 0:
                issue_load(0)
            if c + 1 < 16:
                issue_load(c + 1)
            lv = lv_of[c]
            last = 0
            for tb in range(4):
                mm = [(G.wb[b][:, kc, cb * 128:(cb + 1) * 128], actT[:, kc, tb * 512:(tb + 1) * 512]) for kc in range(16)]
                fw = [(G.s_w[b], wv), (s_a, av)] if (cb == 0 and tb == 0) else []
                yb = tn % 2
                tsl = slice(tb * 512, (tb + 1) * 512)

                def evac(ps, w, yb=yb, tn=tn):
                    ww = list(w) + [(s_po, post_of.get(("t", tn - 2), 0))]
                    ph.op("act", lambda e: e.activation(out=yf[yb][:], in_=ps, func=AF.Copy), ww, G.s_ep)

                G.tile(mm, evac, fw)
                ev = G.s_ep.n
                w0 = [(G.s_ep, ev), (s_l[slot], lv)] + (free_w if tb == 0 else [])
                if mode == "attn":
                    last = ph.op("pool", lambda e, yb=yb, slot=slot, tsl=tsl: e.tensor_tensor(
                        out=so.buf[slot][:, tsl], in0=yf[yb][:], in1=aux[slot][:, tsl], op=ALU.mult), w0, s_po)
                elif mode == "conv":
                    p = ph.op("pool", lambda e, yb=yb, slot=slot, tsl=tsl: e.tensor_tensor(
                        out=tt[yb][:], in0=yf[yb][:], in1=aux[slot][:, tsl], op=ALU.mult),
                        w0 + [(s_po, post_of.get(("t", tn - 2), 0))], s_p2)
                    last = ph.op("dve", lambda e, yb=yb, slot=slot, tsl=tsl: e.tensor_tensor(
                        out=so.buf[slot][:, tsl], in0=tt[yb][:], in1=aux2[slot][:, tsl], op=ALU.add), [(s_p2, p)], s_po)
                else:
                    last = ph.op("dve", lambda e, yb=yb, slot=slot, tsl=tsl: e.tensor_tensor(
                        out=so.buf[slot][:, tsl], in0=yf[yb][:], in1=xl[slot][:, tsl], op=ALU.add), w0, s_po)
                post_of[("t", tn)] = last
                tn += 1
            post_of[c] = last
            dst = T["xres"][c] if mode == "resid" else (T["maT"][c] if mode == "attn" else T["mT"][c])
            so.store(slot, dst, [(s_po, last)])
        G.end_job()
    so.drain()
    ph.run()


def phase_conv(nc, name, actT, W, L, C, T):
    ph = Phase(nc, name)
    fm, gt = T["fm"], T["gt"]
    xi = [ph.sb(f"xi{i}", [128, NT], BF) for i in range(2)]
    gc = [ph.sb(f"gc{i}", [128, NT], BF) for i in range(2)]
    gb = [ph.sb(f"gb{i}", [128, NT], BF) for i in range(2)]
    ub = [ph.sb(f"ub{i}", [128, 16, 130], F32) for i in range(2)]
    acc = [ph.sb(f"acc{i}", [128, 16, 128], F32) for i in range(2)]
    H0 = ph.sb("H0", [128, 16, 16, 2], BF)
    H1 = ph.sb("H1", [128, 16, 16, 2], BF)
    Ht = ph.sb("Ht", [128, 16, 16, 2], F32)
    halo = ph.sb("halo", [128, 16, 16, 2], F32)
    cw = ph.sb("cw", [128, 16, 3], F32)
    fl = ph.sb("fl", [128, 2], F32)
    s_c = ph.sem("c")
    s_l = [ph.sem("l"), ph.sem("l")]
    s_pl = ph.sem("pl")
    s_dv = ph.sem("dv")
    m0 = ph.op("dve", lambda e: e.memset(H0[:], 0.0), [], s_dv)
    ph.dma("sp", H0[:, :, 1:16, :].rearrange("p c i t -> p c (i t)"),
           gt[2048:4096, 0:30].rearrange("(c p) x -> p c x", p=128), [(s_dv, m0)], s_c)
    ph.dma("sp", H1[:].rearrange("p c i t -> p c (i t)"), gt[0:2048, :].rearrange("(c p) x -> p c x", p=128), [], s_c)
    ph.dma("sp", cw[:], W["conv_wT"][L], [], s_c)
    ph.dma("sp", fl[:], C["flags"], [], s_c)
    h1 = ph.op("dve", lambda e: e.tensor_scalar(out=Ht[:], in0=H0[:], scalar1=fl[:, 0:1], scalar2=None, op0=ALU.mult),
               [(s_c, 64)], s_dv)
    h2 = ph.op("dve", lambda e: e.scalar_tensor_tensor(out=halo[:], in0=H1[:], scalar=fl[:, 1:2], in1=Ht[:],
                                                       op0=ALU.mult, op1=ALU.add), [(s_dv, h1)], s_dv)
    dv_of = {}
    pl_of = {}
    for c in range(16):
        b = c % 2
        w = [(s_dv, dv_of[c - 2])] if c >= 2 else []
        for dst, off in ((xi[b], 0), (gb[b], 16), (gc[b], 32)):
            ph.dma("sp", dst[:], fm[off + c], w, s_l[b])
        lv = s_l[b].n
        ph.op("pool", lambda e, b=b, c=c: e.tensor_copy(out=ub[b][:, :, 0:2], in_=halo[:, c, :, :]),
              [(s_dv, h2)] + w, None)
        pl_of[c] = ph.op("pool", lambda e, b=b: e.tensor_tensor(out=ub[b][:, :, 2:130],
                                                                in0=xi[b][:].rearrange("p (i j) -> p i j", j=128),
                                                                in1=gc[b][:].rearrange("p (i j) -> p i j", j=128), op=ALU.mult),
                         [(s_l[b], lv)], s_pl)
        d = ph.op("dve", lambda e, b=b, c=c: e.tensor_scalar(out=acc[b][:], in0=ub[b][:, :, 2:130], scalar1=cw[:, c, 2:3],
                                                             scalar2=None, op0=ALU.mult), [(s_pl, pl_of[c])], s_dv)
        d = ph.op("dve", lambda e, b=b, c=c: e.scalar_tensor_tensor(out=acc[b][:], in0=ub[b][:, :, 1:129], scalar=cw[:, c, 1:2],
                                                                    in1=acc[b][:], op0=ALU.mult, op1=ALU.add), [(s_dv, d)], s_dv)
        d = ph.op("dve", lambda e, b=b, c=c: e.scalar_tensor_tensor(out=acc[b][:], in0=ub[b][:, :, 0:128], scalar=cw[:, c, 0:1],
                                                                    in1=acc[b][:], op0=ALU.mult, op1=ALU.add), [(s_dv, d)], s_dv)
        dv_of[c] = ph.op("dve", lambda e, b=b, c=c: e.tensor_tensor(out=actT[:, c, :].rearrange("p (i j) -> p i j", j=128),
                                                                    in0=acc[b][:], in1=gb[b][:].rearrange("p (i j) -> p i j", j=128),
                                                                    op=ALU.mult), [(s_dv, d)], s_dv)
    ph.wait("sp", [(s_dv, s_dv.n)])
    ph.run()


def phase_ffn(nc, name, actT, W, L, T):
    ph = Phase(nc, name)
    Gu = Gemm(ph, 16)
    Gd = Gemm(ph, 8, pfx="d")
    fT = ph.sb("fT", [128, 8, NT], BF)
    rf = [ph.sb(f"rf{i}", [128, 512], F32) for i in range(2)]
    yf = [ph.sb(f"yf{i}", [128, 512], F32) for i in range(2)]
    so = SlabOut(ph, F32, 2, "xs")
    xl = [ph.sb(f"xl{i}", [128, NT], F32) for i in range(2)]
    s_pl = ph.sem("pl")
    s_po = ph.sem("po")
    s_l = [ph.sem("l"), ph.sem("l")]
    xres = T["xres"]
    un = 0
    dn = 0
    pl_of = {}
    po_of = {}
    slab_store = {}
    slab_n = 0
    slab_last = {}
    lv_of = {}

    def issue_load(n):
        sl = n % 2
        c = n % 16
        lw = [(s_po, slab_last.get(n - 2, 0))]
        if c in slab_store:
            lw.append(slab_store[c])
        ph.dma("sp", xl[sl][:], xres[c], lw, s_l[sl])
        lv_of[n] = s_l[sl].n

    Gu.plan([W["w_up"][L][:, hg * 1024 + j * 512: hg * 1024 + (j + 1) * 512] for hg in range(8) for j in range(2)])
    Gd.plan([W["w_down"][L][hg * 1024:(hg + 1) * 1024, j * 512:(j + 1) * 512] for hg in range(8) for j in range(4)])
    for hg in range(8):
        for j in range(2):
            b, wv = Gu.next_weights()
            for cb in range(4):
                for tb in range(4):
                    mm = [(Gu.wb[b][:, kc, cb * 128:(cb + 1) * 128], actT[:, kc, tb * 512:(tb + 1) * 512]) for kc in range(16)]
                    fw = [(Gu.s_w[b], wv)] if (cb == 0 and tb == 0) else []
                    rb = un % 2

                    def evac(ps, w, rb=rb, un=un):
                        ph.op("act", lambda e: e.activation(out=rf[rb][:], in_=ps, func=AF.Relu),
                              list(w) + [(s_pl, pl_of.get(un - 2, 0))], Gu.s_ep)

                    Gu.tile(mm, evac, fw)
                    pl_of[un] = ph.op("pool", lambda e, rb=rb, j=j, cb=cb, tb=tb: e.tensor_tensor(
                        out=fT[:, j * 4 + cb, tb * 512:(tb + 1) * 512], in0=rf[rb][:], in1=rf[rb][:], op=ALU.mult),
                        [(Gu.s_ep, Gu.s_ep.n), (Gd.s_pe, Gd.s_pe.n)], s_pl)
                    un += 1
            Gu.end_job()
        f_ready = s_pl.n
        for j in range(4):
            b, wv = Gd.next_weights()
            for cb in range(4):
                c = 4 * j + cb
                slot, free_w = so.begin()
                if slab_n == 0:
                    issue_load(0)
                if slab_n + 1 < 128:
                    issue_load(slab_n + 1)
                lv = lv_of[slab_n]
                last = 0
                for tb in range(4):
                    mm = [(Gd.wb[b][:, kc, cb * 128:(cb + 1) * 128], fT[:, kc, tb * 512:(tb + 1) * 512]) for kc in range(8)]
                    fw = [(Gd.s_w[b], wv), (s_pl, f_ready)] if (cb == 0 and tb == 0) else []
                    yb = dn % 2
                    tsl = slice(tb * 512, (tb + 1) * 512)

                    def evac(ps, w, yb=yb, dn=dn):
                        ph.op("act", lambda e: e.activation(out=yf[yb][:], in_=ps, func=AF.Copy),
                              list(w) + [(s_po, po_of.get(dn - 2, 0))], Gd.s_ep)

                    Gd.tile(mm, evac, fw)
                    last = ph.op("dve", lambda e, yb=yb, slot=slot, tsl=tsl: e.tensor_tensor(
                        out=so.buf[slot][:, tsl], in0=yf[yb][:], in1=xl[slot][:, tsl], op=ALU.add),
                        [(Gd.s_ep, Gd.s_ep.n), (s_l[slot], lv)] + (list(free_w) if tb == 0 else []), s_po)
                    po_of[dn] = last
                    dn += 1
                slab_last[slab_n] = last
                slab_n += 1
                v = so.store(slot, xres[c], [(s_po, last)])
                slab_store[c] = (so.s_st[slot], v)
            Gd.end_job()
    so.drain()
    ph.run()


WEIGHT_USERS = {
    "w_in": ("win",), "cmp_w1_k": ("cmp",), "cmp_w2_k": ("cmp",), "cmp_pos_kT": ("cmp",),
    "cmp_w1_v": ("cmp",), "cmp_w2_v": ("cmp",), "cmp_pos_vT": ("cmp",), "conv_wT": ("cv",),
    "w_attn_proj": ("ap",), "w_conv_out": ("co",), "w_o": ("wo",), "w_up": ("ffn",), "w_down": ("ffn",),
    "norm1_gT": ("n1",), "norm2_gT": ("n2",), "final_gT": ("nf",),
}
LAST_SHAPES = {}


def build(sel=None, dbg=()):
    nc = bass.Bass("TRN2", target_bir_lowering=False)

    def on(L, nm):
        return sel is None or (L, nm) in sel

    def din(nm, shape, dt=F32):
        LAST_SHAPES[nm] = tuple(shape)
        return nc.dram_tensor(nm, shape, dt, kind="ExternalInput").ap()

    def scr(nm, shape, dt=BF):
        if nm in dbg:
            t = nc.dram_tensor(nm, shape, dt, kind="ExternalOutput")
        else:
            t = nc.dram_tensor(nm, shape, dt)
        return t

    W = {}
    for nm, shp in (("w_in", [DEPTH, D, IN_COLS]),
                    ("cmp_w1_k", [DEPTH, 4096, 256]), ("cmp_w2_k", [DEPTH, 256, 128]), ("cmp_pos_kT", [DEPTH, 128, 32]),
                    ("cmp_w1_v", [DEPTH, 4096, 256]), ("cmp_w2_v", [DEPTH, 256, 128]), ("cmp_pos_vT", [DEPTH, 128, 32]),
                    ("conv_wT", [DEPTH, 128, 16, 3]), ("w_attn_proj", [DEPTH, D, D]), ("w_conv_out", [DEPTH, D, D]),
                    ("w_o", [DEPTH, D, D]), ("w_up", [DEPTH, D, DFF]), ("w_down", [DEPTH, DFF, D]),
                    ("norm1_gT", [DEPTH, 128, 16]), ("norm2_gT", [DEPTH, 128, 16]), ("final_gT", [128, 16])):
        users = WEIGHT_USERS[nm]
        used = sel is None or any((L, u) in sel for L in range(DEPTH + 1) for u in users)
        if not used:
            shp = [1] * len(shp)
        elif sel is not None and nm != "final_gT" and not any((1, u) in sel for u in users):
            shp = [1] + list(shp[1:])
        W[nm] = din(nm, shp)
    xin = din("xT", [16, 128, NT])
    Cd = {}
    for nm, shp, dt in (("cos", [128, NT], F32), ("sin", [128, NT], F32), ("cosc", [128, 256], F32), ("sinc", [128, 256], F32),
                        ("RT", [128, 128], BF), ("ident", [128, 128], BF), ("ones", [128, 128], BF), ("msel", [128, 2, 64], BF),
                        ("cmpmask", [128, 2, NT], BF), ("fb", [128, 16, 64], F32), ("expand", [64, 32, 128], BF),
                        ("cmaskS", [128, 2, 128], BF), ("wmask", [128, 6, 128], BF), ("flags", [128, 2], F32)):
        Cd[nm] = din("c_" + nm, shp, dt)
    outT = nc.dram_tensor("outT", [16, 128, NT], F32, kind="ExternalOutput").ap()

    T = {}
    T["qT"] = scr("qT", [16, 128, NT]).ap()
    T["fm"] = scr("fm", [80, 128, NT]).ap()
    for nm, shp in (("gk_in", [2048, NT]), ("gk", [4096, NT]), ("gv_in", [2048, 1024]), ("gv", [4096, 1024]),
                    ("gt_in", [2048, 32]), ("gt", [4096, 32])):
        t = scr(nm, shp)
        T[nm + "_t"] = t
        T[nm] = t.ap()
    T["oT"] = scr("oT", [16, 128, NT]).ap()
    T["maT"] = scr("maT", [16, 128, NT]).ap()
    T["mT"] = scr("mT", [16, 128, NT]).ap()
    T["xres"] = scr("xres", [16, 128, NT], F32).ap()
    if "dbg_k" in dbg:
        for nm, shp, dt in (("dbg_k", [128, S], BF), ("dbg_kw", [128, S], BF), ("dbg_v", [128, 32 * 130], BF),
                            ("dbg_vw", [128, 32 * 130], BF), ("dbg_bt", [64, 128], BF), ("dbg_imp", [128, 64], F32),
                            ("dbg_ds", [128, 12], F32), ("dbg_coef", [128, 12], F32), ("dbg_obf", [128, 512], BF),
                            ("dbg_m8", [128, 16], F32), ("dbg_kct", [128, 1024], BF), ("dbg_vca", [128, 4 * 2 * 194], BF),
                            ("dbg_gates", [128, 16 * 48], F32), ("dbg_osl", [128, 1024], F32), ("dbg_ow", [128, 1024], F32),
                            ("dbg_oacc", [128, 512], F32), ("dbg_tmpo", [128, 512], F32)):
            T[nm] = scr(nm, shp, dt).ap()

    with contextlib.ExitStack() as es:
        del SEM_POOL[:]
        for i in range(24):
            SEM_POOL.append([es.enter_context(nc.semaphore(f"sem{i}")), 0])
        actT = es.enter_context(nc.sbuf_tensor("actT", [128, 16, NT], BF))
        gates = es.enter_context(nc.sbuf_tensor("gates", [128, 16, 48], F32))
        KCT = es.enter_context(nc.sbuf_tensor("KCT", [128, 4, 256], BF))
        VCA = es.enter_context(nc.sbuf_tensor("VCA", [128, 4, 2, 194], BF))
        RT = es.enter_context(nc.sbuf_tensor("RTs", [128, 128], BF))
        ident = es.enter_context(nc.sbuf_tensor("idents", [128, 128], BF))
        ones = es.enter_context(nc.sbuf_tensor("oness", [128, 128], BF))
        T["gates"] = gates
        C = dict(Cd)
        C["RT"], C["ident"], C["ones"] = RT, ident, ones
        ph = Phase(nc, "init")
        s = ph.sem("s")
        ph.dma("sp", RT[:], Cd["RT"], [], s)
        ph.dma("sp", ident[:], Cd["ident"], [], s)
        ph.dma("sp", ones[:], Cd["ones"], [], s)
        ph.wait("sp", [(s, 48)])
        ph.run()
        xcur = xin
        for L in range(DEPTH):
            if on(L, "n1"):
                phase_norm(nc, f"n1_{L}", xcur, W["norm1_gT"][L], ones, dst_sb=actT)
            if on(L, "win"):
                phase_win(nc, f"win{L}", actT, W["w_in"][L], C, T)
            if on(L, "ag"):
                phase_gather(nc, f"ag{L}", T)
            if on(L, "cmp"):
                phase_compress(nc, f"cmp{L}", L, W, C, T, KCT, VCA)
            if on(L, "att"):
                phase_attn(nc, f"att{L}", C, T, KCT, VCA, gates)
            if on(L, "ap"):
                phase_proj(nc, f"ap{L}", actT, W["w_attn_proj"][L], "attn", T, C, src_act=T["oT"])
            if on(L, "cv"):
                phase_conv(nc, f"cv{L}", actT, W, L, C, T)
            if on(L, "co"):
                phase_proj(nc, f"co{L}", actT, W["w_conv_out"][L], "conv", T, C)
            if on(L, "wo"):
                phase_proj(nc, f"wo{L}", actT, W["w_o"][L], "resid", T, C, src_act=T["mT"], xsrc=xcur)
            xcur = T["xres"]
            if on(L, "n2"):
                phase_norm(nc, f"n2_{L}", xcur, W["norm2_gT"][L], ones, dst_sb=actT)
            if on(L, "ffn"):
                phase_ffn(nc, f"ffn{L}", actT, W, L, T)
        if on(DEPTH, "nf"):
            phase_norm(nc, "nf", xcur, W["final_gT"], ones, dst_dram=outT)
    return nc


def _bf(a):
    return np.ascontiguousarray(a.astype(ml_dtypes.bfloat16))


def _consts(p):
    f32 = np.float32
    tl = np.arange(NT)
    pos = (128 * (2 * (tl // 128) + p) + tl % 128)
    half = 64
    inv_freq = np.exp(-math.log(10000.0) * np.arange(half, dtype=f32) / half).astype(f32)
    ang = pos.astype(f32)[None, :] * inv_freq[:, None]
    c = {}
    c["cos"] = np.concatenate([np.cos(ang), np.cos(ang)], 0).astype(f32)
    c["sin"] = np.concatenate([np.sin(ang), np.sin(ang)], 0).astype(f32)
    cpos = (np.arange(256) * 16 + 31).astype(f32)
    angc = cpos[None, :] * inv_freq[:, None]
    c["cosc"] = np.concatenate([np.cos(angc), np.cos(angc)], 0).astype(f32)
    c["sinc"] = np.concatenate([np.sin(angc), np.sin(angc)], 0).astype(f32)
    RT = np.zeros((128, 128), f32)
    for d in range(64):
        RT[d + 64, d] = -1.0
        RT[d, d + 64] = 1.0
    c["RT"] = _bf(RT)
    c["ident"] = _bf(np.eye(128, dtype=f32))
    c["ones"] = _bf(np.ones((128, 128), f32))
    n = np.arange(256)
    cs = n * 16
    ss = np.arange(64) * 64
    ov = np.minimum(cs[:, None] + 32, ss[None, :] + 64) - np.maximum(cs[:, None], ss[None, :])
    msel = (np.clip(ov, 0, None) / 32.0).astype(f32)
    msel[255] = 0.0
    c["msel"] = _bf(msel.reshape(2, 128, 64).transpose(1, 0, 2))
    cm = np.where((n[:, None] * 16 + 31 <= pos[None, :]) & (n[:, None] < 255), 0.0, NEG).astype(f32)
    c["cmpmask"] = _bf(cm.reshape(2, 128, NT).transpose(1, 0, 2))
    tb = pos // 64
    m = np.arange(64)
    valid = m[None, :] <= tb[:, None]
    forced = (m[None, :] == 0) | (m[None, :] == tb[:, None]) | (m[None, :] == tb[:, None] - 1)
    fb = np.where(valid, np.where(forced, 1e4, 0.0), -1e4).astype(f32)
    c["fb"] = np.ascontiguousarray(fb.reshape(16, 128, 64).transpose(1, 0, 2))
    ex = np.zeros((64, 32, 128), f32)
    for kt in range(32):
        ex[2 * kt, kt, 0:64] = 1.0
        ex[2 * kt + 1, kt, 64:128] = 1.0
    c["expand"] = _bf(ex)
    k = np.arange(128)
    causal = np.where(k[:, None] <= k[None, :], 0.0, NEG).astype(f32)
    allm = np.full((128, 128), NEG, f32)
    zero = np.zeros((128, 128), f32)
    anti = np.where(k[:, None] > k[None, :], 0.0, NEG).astype(f32)
    if p == 0:
        cms = [causal, allm]
        wm = [anti, zero, zero, zero, causal, allm]
    else:
        cms = [zero, causal]
        wm = [allm, anti, zero, zero, zero, causal]
    c["cmaskS"] = _bf(np.stack(cms, 1))
    c["wmask"] = _bf(np.stack(wm, 1))
    fl = np.zeros((128, 2), f32)
    fl[:, p] = 1.0
    c["flags"] = fl
    return c


def kernel(**inputs):
    f32 = np.float32
    x = np.asarray(inputs["x"], f32)

    def gT(a):
        a = np.asarray(a, f32)
        return np.ascontiguousarray(np.swapaxes(a.reshape(a.shape[:-1] + (16, 128)), -1, -2))

    shared = {
        "w_in": np.asarray(inputs["w_in"], f32),
        "cmp_w1_k": np.asarray(inputs["cmp_w1_k"], f32), "cmp_w2_k": np.asarray(inputs["cmp_w2_k"], f32),
        "cmp_pos_kT": np.ascontiguousarray(np.swapaxes(np.asarray(inputs["cmp_pos_k"], f32), 1, 2)),
        "cmp_w1_v": np.asarray(inputs["cmp_w1_v"], f32), "cmp_w2_v": np.asarray(inputs["cmp_w2_v"], f32),
        "cmp_pos_vT": np.ascontiguousarray(np.swapaxes(np.asarray(inputs["cmp_pos_v"], f32), 1, 2)),
        "conv_wT": np.ascontiguousarray(np.asarray(inputs["conv_w"], f32).reshape(DEPTH, 3, 16, 128).transpose(0, 3, 2, 1)),
        "w_attn_proj": np.asarray(inputs["w_attn_proj"], f32), "w_conv_out": np.asarray(inputs["w_conv_out"], f32),
        "w_o": np.asarray(inputs["w_o"], f32), "w_up": np.asarray(inputs["w_up"], f32), "w_down": np.asarray(inputs["w_down"], f32),
        "norm1_gT": gT(inputs["norm1_g"]), "norm2_gT": gT(inputs["norm2_g"]), "final_gT": gT(inputs["final_g"]),
    }
    consts = [_consts(0), _consts(1)]
    in_maps = []
    for c in range(8):
        b, p = c // 2, c % 2
        xo = x[b].reshape(16, 2, 128, D)[:, p].reshape(NT, D)
        m = dict(shared)
        m["xT"] = np.ascontiguousarray(xo.T.reshape(16, 128, NT))
        for k, v in consts[p].items():
            m["c_" + k] = v
        in_maps.append(m)
    nc = build()
    res = run_bass_kernel_spmd(nc, in_maps, core_ids=list(range(8)))
    out = np.empty((4, S, D), f32)
    for c in range(8):
        b, p = c // 2, c % 2
        o = np.asarray(res.results[c]["outT"], f32).reshape(D, NT).T
        out[b].reshape(16, 2, 128, D)[:, p] = o.reshape(16, 128, D)
    return out
```

```python
import contextlib
import math

import ml_dtypes
import numpy as np

import concourse.bass as bass
import concourse.mybir as mybir
from concourse.bass_utils import run_bass_kernel_spmd

F32 = mybir.dt.float32
BF = mybir.dt.bfloat16
ALU = mybir.AluOpType
AF = mybir.ActivationFunctionType

D = 2048
S = 4096
NT = 2048
DEPTH = 2
NH = 16
DFF = 8192
IN_COLS = 15408
NEG = -30000.0
PAIRS = [[0, 1], [2, 3], [4, 5], [6, 7]]
SCALE = 128 ** -0.5


def gk_row(r, slab):
    return (slab // 4) * 1024 + r * 512 + (slab % 4) * 128


def gv_row(r, t):
    return (t // 1024) * 2048 + r * 1024 + (t % 1024)


class Sem:
    def __init__(self, slot):
        self.slot = slot
        self.h = slot[0]
        self.base = slot[1]
        self.n = 0


SEM_POOL = []


class Phase:
    def __init__(self, nc, name):
        self.nc = nc
        self.name = name
        self.es = contextlib.ExitStack()
        self.q = {e: [] for e in ("pe", "act", "dve", "pool", "sp")}
        self.k = 0
        self.sems = []

    def sem(self, nm):
        sm = Sem(SEM_POOL[self.k])
        self.k += 1
        self.sems.append(sm)
        return sm

    def sb(self, nm, shape, dt):
        return self.es.enter_context(self.nc.sbuf_tensor(f"{self.name}_{nm}", shape, dt))

    def ps(self, nm, shape, dt=F32):
        return self.es.enter_context(self.nc.psum_tensor(f"{self.name}_{nm}", shape, dt))

    def op(self, eng, fn, waits=(), sig=None, inc=1):
        waits = [(s, v) for (s, v) in waits if v > 0]

        def thunk(e, fn=fn, waits=waits, sig=sig, inc=inc):
            for s, v in waits:
                e.wait_ge(s.h, s.base + v)
            ins = fn(e)
            if sig is not None:
                ins.then_inc(sig.h, inc)

        self.q[eng].append(thunk)
        if sig is not None:
            sig.n += inc
            return sig.n
        return 0

    def dma(self, eng, out, in_, waits=(), sig=None):
        return self.op(eng, lambda e, out=out, in_=in_: e.dma_start(out=out, in_=in_), waits, sig, 16)

    def wait(self, eng, waits):
        waits = [(s, v) for (s, v) in waits if v > 0]

        def thunk(e, waits=waits):
            for s, v in waits:
                e.wait_ge(s.h, s.base + v)

        self.q[eng].append(thunk)

    def run(self):
        q = self.q
        with self.nc.Block() as block:
            @block.tensor
            def _(e):
                for f in q["pe"]:
                    f(e)

            @block.scalar
            def _(e):
                for f in q["act"]:
                    f(e)

            @block.vector
            def _(e):
                for f in q["dve"]:
                    f(e)

            @block.gpsimd
            def _(e):
                for f in q["pool"]:
                    f(e)

            @block.sync
            def _(e):
                for f in q["sp"]:
                    f(e)
        for sm in self.sems:
            sm.slot[1] += sm.n
        self.es.close()


def phase_norm(nc, name, xsrc, gain_ap, ones_bf, dst_sb=None, dst_dram=None):
    ph = Phase(nc, name)
    xb = [ph.sb(f"xb{i}", [128, 16, 512], F32) for i in range(2)]
    sq = ph.sb("sq", [128, 16, 512], BF)
    g = ph.sb("g", [128, 16], F32)
    tmp = ph.sb("tmp", [128, 512], F32)
    tmp2 = ph.sb("tmp2", [128, 512], F32)
    rstd = ph.sb("rstd", [128, 512], F32)
    pss = ph.ps("ss", [128, 512])
    ob = ph.sb("ob", [128, 16, 512], F32) if dst_dram is not None else None
    s_ld = [ph.sem("ld"), ph.sem("ld")]
    s_g = ph.sem("g")
    s_act = ph.sem("act")
    s_pe = ph.sem("pe")
    s_dve = ph.sem("dve")
    s_st = ph.sem("st")
    ph.dma("sp", g[:], gain_ap, sig=s_g)
    dve_done = {}
    pe_done = {}
    dve_a = {}
    for tb in range(4):
        b = tb % 2
        w = [(s_dve, dve_done[tb - 2])] if tb >= 2 else []
        ld = ph.dma("sp", xb[b][:], xsrc[:, :, tb * 512:(tb + 1) * 512].rearrange("k p t -> p k t"), w, s_ld[b])
        for kc in range(16):
            w = []
            if kc == 0:
                w = [(s_ld[b], ld)]
                if tb >= 1:
                    w.append((s_pe, pe_done[tb - 1]))
            a_sq = ph.op("act", lambda e, kc=kc, b=b: e.activation(out=sq[:, kc, :], in_=xb[b][:, kc, :], func=AF.Square),
                         w, s_act if kc == 15 else None)
        for kc in range(16):
            w = []
            if kc == 0:
                w = [(s_act, a_sq)]
                if tb >= 1:
                    w.append((s_dve, dve_a[tb - 1]))
            pe_done[tb] = ph.op("pe", lambda e, kc=kc: e.matmul(pss[:], ones_bf[:], sq[:, kc, :], start=(kc == 0), stop=(kc == 15)),
                                w, s_pe if kc == 15 else None)
        dve_a[tb] = ph.op("dve", lambda e: e.tensor_scalar(out=tmp[:], in0=pss[:], scalar1=1.0 / D, scalar2=1e-6,
                                                           op0=ALU.mult, op1=ALU.add),
                          [(s_pe, pe_done[tb])], s_dve)
        a_sqrt = ph.op("act", lambda e: e.activation(out=tmp2[:], in_=tmp[:], func=AF.Sqrt), [(s_dve, dve_a[tb])], s_act)
        d_b = ph.op("dve", lambda e: e.reciprocal(out=rstd[:], in_=tmp2[:]), [(s_act, a_sqrt)], s_dve)
        for kc in range(16):
            w = []
            if kc == 0:
                w = [(s_dve, d_b), (s_g, 16)]
                if ob is not None and tb >= 1:
                    w.append((s_st, 16 * tb))
            if ob is None:
                o = dst_sb[:, kc, tb * 512:(tb + 1) * 512]
            else:
                o = ob[:, kc, :]
            dve_done[tb] = ph.op("dve", lambda e, kc=kc, b=b, o=o: e.scalar_tensor_tensor(
                out=o, in0=xb[b][:, kc, :], scalar=g[:, kc:kc + 1], in1=rstd[:], op0=ALU.mult, op1=ALU.mult),
                w, s_dve if kc == 15 else None)
        if ob is not None:
            ph.dma("sp", dst_dram[:, :, tb * 512:(tb + 1) * 512].rearrange("k p t -> p k t"), ob[:],
                   [(s_dve, dve_done[tb])], s_st)
    if ob is not None:
        ph.wait("sp", [(s_st, 64)])
    ph.run()


class Gemm:
    NPS = 4

    def __init__(self, ph, nk, wcols=512, pfx=""):
        self.ph = ph
        self.nk = nk
        self.wb = [ph.sb(f"{pfx}w{i}", [128, nk, wcols], BF) for i in range(2)]
        self.s_w = [ph.sem("w"), ph.sem("w")]
        self.s_pe = ph.sem("gpe")
        self.s_ep = ph.sem("gep")
        self.psb = [ph.ps(f"{pfx}g{i}", [128, 512]) for i in range(self.NPS)]
        self.T = 0
        self.J = 0
        self.job_pe_end = {}
        self.ep_of_tile = {}

    def plan(self, wsrcs):
        self.wsrcs = list(wsrcs)
        self.loaded = {}
        self.nxt = 0
        self.cur = 0

    def next_weights(self):
        J = self.nxt
        self.nxt += 1
        self.cur = J
        if J not in self.loaded:
            self.loaded[J] = self.load_w(self.wsrcs[J])
        if J + 1 < len(self.wsrcs) and (J + 1) not in self.loaded:
            self.loaded[J + 1] = self.load_w(self.wsrcs[J + 1])
        return self.loaded[J]

    def load_w(self, wsrc):
        ph = self.ph
        J = self.J
        b = J % 2
        ncols = wsrc.shape[1]
        w = [(self.s_pe, self.job_pe_end[J - 2])] if J >= 2 else []
        v = ph.dma("pool", self.wb[b][:, :, 0:ncols], wsrc.rearrange("(k p) c -> p k c", p=128), w, self.s_w[b])
        self.J += 1
        return b, v

    def tile(self, mm_list, evac, first_waits=()):
        ph = self.ph
        T = self.T
        ps = self.psb[T % self.NPS]
        n = len(mm_list)
        M = mm_list[0][0].shape[-1]
        N = mm_list[0][1].shape[-1]
        pso = ps[0:M, 0:N]
        for k, (l, r) in enumerate(mm_list):
            w = []
            if k == 0:
                w = list(first_waits)
                if T >= self.NPS:
                    w.append((self.s_ep, self.ep_of_tile[T - self.NPS]))
            v = ph.op("pe", lambda e, l=l, r=r, k=k, pso=pso: e.matmul(pso, l, r, start=(k == 0), stop=(k == n - 1)),
                      w, self.s_pe if k == n - 1 else None)
        before = self.s_ep.n
        evac(pso, [(self.s_pe, v)])
        assert self.s_ep.n == before + 1
        self.ep_of_tile[T] = self.s_ep.n
        self.T += 1
        return T

    def end_job(self):
        self.job_pe_end[self.cur] = self.s_pe.n


class SlabOut:
    def __init__(self, ph, dt=BF, nbuf=2, name="stg"):
        self.ph = ph
        self.buf = [ph.sb(f"{name}{i}", [128, NT], dt) for i in range(nbuf)]
        self.s_st = [ph.sem("st") for _ in range(nbuf)]
        self.s_x = [ph.sem("sx") for _ in range(nbuf)]
        self.n = 0

    def begin(self):
        i = self.n % len(self.buf)
        self.n += 1
        return i, [(self.s_st[i], self.s_st[i].n), (self.s_x[i], self.s_x[i].n)]

    def store(self, i, dst, waits, eng="sp"):
        return self.ph.dma(eng, dst, self.buf[i][:], waits, self.s_st[i])

    def drain(self, eng="sp"):
        self.ph.wait(eng, [(s, s.n) for s in self.s_st])


def phase_win(nc, name, actT, w_in_l, C, T):
    ph = Phase(nc, name)
    G = Gemm(ph, 16)
    so = SlabOut(ph)
    cosT = ph.sb("cos", [128, NT], F32)
    sinT = ph.sb("sin", [128, NT], F32)
    zb = [ph.sb(f"zb{i}", [128, 512], BF) for i in range(2)]
    t1 = [ph.sb(f"t1{i}", [128, 512], F32) for i in range(2)]
    t2 = [ph.sb(f"t2{i}", [128, 512], F32) for i in range(2)]
    ps2 = [ph.ps(f"r{i}", [128, 512]) for i in range(2)]
    vst = [ph.sb(f"vst{i}", [128, 512], BF) for i in range(2)]
    tails_x = ph.sb("tlx", [128, 16, 32], BF)
    tails_c = ph.sb("tlc", [128, 16, 32], BF)
    tails_u = ph.sb("tlu", [128, 16, 32], BF)
    s_c = ph.sem("c")
    s_pe2 = ph.sem("pe2")
    s_dv = ph.sem("dv")
    s_pl = ph.sem("pl")
    s_vst = [ph.sem("vst"), ph.sem("vst")]
    s_tl = ph.sem("tl")
    ph.dma("sp", cosT[:], C["cos"], sig=s_c)
    ph.dma("sp", sinT[:], C["sin"], sig=s_c)
    RT = C["RT"]
    rope_n = [0]
    pool_of_rope = {}
    pend = [None]
    last_rope = [None]

    def fm_job(c0, kind, dests, tails=None):
        b, wv = G.next_weights()
        for cb in range(4):
            slot, free_w = so.begin()
            last = None
            for tb in range(4):
                mm = [(G.wb[b][:, kc, cb * 128:(cb + 1) * 128], actT[:, kc, tb * 512:(tb + 1) * 512]) for kc in range(16)]
                fw = [(G.s_w[b], wv)] if (cb == 0 and tb == 0) else []
                dst = so.buf[slot][:, tb * 512:(tb + 1) * 512]
                if kind in ("copy", "sigmoid"):
                    func = AF.Copy if kind == "copy" else AF.Sigmoid

                    def evac(ps, w, dst=dst, func=func, tb=tb, free_w=free_w):
                        ww = list(w) + (free_w if tb == 0 else [])
                        ph.op("act", lambda e: e.activation(out=dst, in_=ps, func=func), ww, G.s_ep)

                    G.tile(mm, evac, fw)
                    last = (G.s_ep, G.s_ep.n)
                else:
                    n = rope_n[0]
                    rb = n % 2
                    rope_n[0] += 1

                    def evac(ps, w, rb=rb, n=n):
                        ww = list(w)
                        if n >= 2:
                            ww.append((s_pl, pool_of_rope[n - 2]))
                        ph.op("act", lambda e: e.activation(out=zb[rb][:], in_=ps, func=AF.Copy), ww, G.s_ep)

                    G.tile(mm, evac, fw)
                    epv = G.s_ep.n
                    if pend[0] is not None:
                        pend[0]()
                        pend[0] = None

                    def post_rope(epv=epv, rb=rb, tb=tb, dst=dst, n=n, fw0=(free_w if tb == 0 else [])):
                        pv = ph.op("pe", lambda e: e.matmul(ps2[rb][:], RT[:], zb[rb][:], start=True, stop=True),
                                   [(G.s_ep, epv)], s_pe2)
                        ph.op("pool", lambda e: e.tensor_tensor(out=t1[rb][:], in0=zb[rb][:],
                                                                in1=cosT[:, tb * 512:(tb + 1) * 512], op=ALU.mult),
                              [(G.s_ep, epv), (s_c, 32)])
                        dv = ph.op("dve", lambda e: e.tensor_tensor(out=t2[rb][:], in0=ps2[rb][:],
                                                                    in1=sinT[:, tb * 512:(tb + 1) * 512], op=ALU.mult),
                                   [(s_pe2, pv), (s_c, 32)], s_dv)
                        pl = ph.op("pool", lambda e: e.tensor_tensor(out=dst, in0=t1[rb][:], in1=t2[rb][:], op=ALU.add),
                                   [(s_dv, dv)] + list(fw0), s_pl)
                        pool_of_rope[n] = pl
                        last_rope[0] = (s_pl, pl)

                    pend[0] = post_rope
            if kind == "rope":
                pend[0]()
                pend[0] = None
                last = last_rope[0]
            if tails is not None:
                tl, idx = tails
                ph.op("pool", lambda e, slot=slot, tl=tl, idx=idx: e.tensor_copy(
                    out=tl[:, idx, :].rearrange("p (i t) -> p i t", t=2),
                    in_=so.buf[slot][:].rearrange("p (i j) -> p i j", j=128)[:, :, 126:128]),
                    [last], so.s_x[slot])
                tails = (tl, idx + 1)
            so.store(slot, dests[cb], [last])
        G.end_job()

    def tok_job(c0, ncols, kind, dst_fn):
        b, wv = G.next_weights()
        for i in range(16):
            mm = [(actT[:, kc, i * 128:(i + 1) * 128], G.wb[b][:, kc, 0:ncols]) for kc in range(16)]
            fw = [(G.s_w[b], wv)] if i == 0 else []
            if kind == "v":
                vb = i % 2

                def evac(ps, w, vb=vb):
                    ph.op("act", lambda e: e.activation(out=vst[vb][:], in_=ps, func=AF.Copy),
                          list(w) + [(s_vst[vb], s_vst[vb].n)], G.s_ep)

                G.tile(mm, evac, fw)
                ph.dma("sp", dst_fn(i), vst[vb][:], [(G.s_ep, G.s_ep.n)], s_vst[vb])
            else:
                def evac(ps, w, i=i):
                    ph.op("act", lambda e: e.activation(out=dst_fn(i), in_=ps, func=AF.Sigmoid), w, G.s_ep)

                G.tile(mm, evac, fw)
        G.end_job()

    gk = T["gk_in"]
    fm = T["fm"]
    cols = [(g * 512, 512) for g in range(4)] + [(2048 + j * 512, 512) for j in range(6)] + [(5120, 48)]
    cols += [(5168 + j * 512, 512) for j in range(12)] + [(11312 + j * 512, 512) for j in range(8)]
    G.plan([w_in_l[:, c0:c0 + n] for c0, n in cols])
    for g in range(4):
        fm_job(g * 512, "rope", [T["qT"][4 * g + r] for r in range(4)])
    kv0 = 2048
    fm_job(kv0 + 0 * 512, "copy", [gk[(0 + g) * 128:(1 + g) * 128, :] for g in range(4)])
    fm_job(kv0 + 1 * 512, "copy", [gk[(4 + g) * 128:(5 + g) * 128, :] for g in range(4)])
    fm_job(kv0 + 2 * 512, "rope", [gk[(8 + g) * 128:(9 + g) * 128, :] for g in range(4)])
    tok_job(kv0 + 3 * 512, 512, "v", lambda i: T["gv_in"][i * 128:(i + 1) * 128, 0:512])
    fm_job(kv0 + 4 * 512, "rope", [gk[(12 + g) * 128:(13 + g) * 128, :] for g in range(4)])
    tok_job(kv0 + 5 * 512, 512, "v", lambda i: T["gv_in"][i * 128:(i + 1) * 128, 512:1024])
    s_cc = ph.sem("cc")
    cw = [(sm, sm.n) for sm in so.s_st] + [(s_vst[0], s_vst[0].n), (s_vst[1], s_vst[1].n)]
    for a, b2, rows, rc in (("gk_in", "gk", 2048, 512), ("gv_in", "gv", 2048, 1024)):
        for k in range(rows // rc):
            i_ap = T[a + "_t"].ap()[k * rc:(k + 1) * rc, :].opt()
            o_ap = T[b2 + "_t"].ap()[2 * k * rc:2 * (k + 1) * rc, :].opt()
            ph.op("pool", lambda e, i_ap=i_ap, o_ap=o_ap: e.collective_compute("AllGather", ALU.bypass, PAIRS,
                                                                               ins=[i_ap], outs=[o_ap]), cw, s_cc, 1)
            cw = []
    tok_job(5120, 48, "ng", lambda i: T["gates"][:, i, :])
    cv0 = 5168
    for j in range(4):
        fm_job(cv0 + j * 512, "copy", [fm[4 * j + r] for r in range(4)], tails=(tails_x, 4 * j))
    for j in range(4):
        fm_job(cv0 + 2048 + j * 512, "copy", [fm[16 + 4 * j + r] for r in range(4)])
    for j in range(4):
        fm_job(cv0 + 4096 + j * 512, "copy", [fm[32 + 4 * j + r] for r in range(4)], tails=(tails_c, 4 * j))
    mg0 = 11312
    for j in range(8):
        fm_job(mg0 + j * 512, "sigmoid", [fm[48 + 4 * j + r] for r in range(4)])
    tl_w = [(s, s.n) for s in so.s_x]
    pl = ph.op("pool", lambda e: e.tensor_tensor(out=tails_u[:], in0=tails_x[:], in1=tails_c[:], op=ALU.mult), tl_w, s_tl)
    ph.dma("sp", T["gt_in"].rearrange("(s p) c -> p s c", p=128), tails_u[:], [(s_tl, pl)], s_tl)
    so.drain()
    ph.wait("sp", [(s_tl, s_tl.n), (s_vst[0], s_vst[0].n), (s_vst[1], s_vst[1].n)])
    ph.wait("pool", [(s_cc, s_cc.n)])
    ph.run()


def phase_gather(nc, name, T):
    ph = Phase(nc, name)
    s = ph.sem("cc")
    for a, b, rows, rc in (("gt_in", "gt", 2048, 2048),):
        for k in range(rows // rc):
            if rc == rows:
                i_ap = T[a + "_t"].ap().opt()
                o_ap = T[b + "_t"].ap().opt()
            else:
                i_ap = T[a + "_t"].ap()[k * rc:(k + 1) * rc, :].opt()
                o_ap = T[b + "_t"].ap()[2 * k * rc:2 * (k + 1) * rc, :].opt()
            ph.op("pool", lambda e, i_ap=i_ap, o_ap=o_ap: e.collective_compute("AllGather", ALU.bypass, PAIRS,
                                                                               ins=[i_ap], outs=[o_ap]), [], s, 1)
            ph.wait("pool", [(s, s.n)])
    ph.run()


def phase_compress(nc, name, L, W, C, T, KCT, VCA):
    ph = Phase(nc, name)
    gk = T["gk"]
    xin = [ph.sb(f"xin{i}", [128, S], BF) for i in range(2)]
    w1 = [ph.sb(f"w1{i}", [128, 32, 256], BF) for i in range(2)]
    w2 = [ph.sb(f"w2{i}", [128, 2, 128], BF) for i in range(2)]
    posT = [ph.sb(f"pos{i}", [128, 32], F32) for i in range(2)]
    posb = [ph.sb(f"posb{i}", [128, 32], BF) for i in range(2)]
    cbias = [ph.sb(f"cb{i}", [128, 2], F32) for i in range(2)]
    hid = ph.sb("hid", [128, 2, 256], BF)
    zb = ph.sb("zb", [128, 256], BF)
    t1 = ph.sb("t1", [128, 256], F32)
    t2 = ph.sb("t2", [128, 256], F32)
    cosc = ph.sb("cosc", [128, 256], F32)
    sinc = ph.sb("sinc", [128, 256], F32)
    pb = ph.ps("pb", [128, 2])
    phid = [ph.ps(f"ph{i}", [128, 256]) for i in range(2)]
    pk = ph.ps("pk", [128, 256])
    pr = ph.ps("pr", [128, 256])
    pv = ph.ps("pv", [128, 2, 128])
    s_ld = ph.sem("ld")
    s_x = [ph.sem("x"), ph.sem("x")]
    s_pe = ph.sem("pe")
    s_act = ph.sem("act")
    s_dve = ph.sem("dve")
    s_pl = ph.sem("pl")
    RT = C["RT"]
    names = [("cmp_w1_k", "cmp_w2_k", "cmp_pos_kT"), ("cmp_w1_v", "cmp_w2_v", "cmp_pos_vT")]
    for kv in range(2):
        ph.dma("pool", w1[kv][:], W[names[kv][0]][L].rearrange("(l p) c -> p l c", p=128), sig=s_ld)
        ph.dma("pool", w2[kv][:], W[names[kv][1]][L].rearrange("(j p) c -> p j c", p=128), sig=s_ld)
        ph.dma("sp", posT[kv][:], W[names[kv][2]][L], sig=s_ld)
    ph.dma("sp", cosc[:], C["cosc"], sig=s_ld)
    ph.dma("sp", sinc[:], C["sinc"], sig=s_ld)
    ph.op("dve", lambda e: e.memset(KCT[:], 0.0), [], s_dve)
    ph.op("dve", lambda e: e.memset(VCA[:], 0.0), [], s_dve)
    ph.wait("dve", [(s_dve, s_dve.n)])
    ph.op("dve", lambda e: e.memset(VCA[:, :, 0, 128:129], 1.0), [], s_dve)
    ph.op("dve", lambda e: e.memset(VCA[0:127, :, 1, 128:129], 1.0), [], s_dve)
    ph.wait("pool", [(s_dve, s_dve.n)])
    for g in range(4):
        ph.dma("pool", VCA[:, g, :, 129:193], C["msel"], sig=s_ld)
    LD_ALL = 16 * 12
    for kv in range(2):
        d0 = ph.op("dve", lambda e, kv=kv: e.tensor_copy(out=posb[kv][:], in_=posT[kv][:]), [(s_ld, LD_ALL)], s_dve)
        for jc in range(2):
            for l in range(32):
                w = [(s_dve, d0), (s_act, s_act.n)] if l == 0 else []
                v = ph.op("pe", lambda e, kv=kv, jc=jc, l=l: e.matmul(pb[:, jc:jc + 1], w1[kv][:, l, jc * 128:(jc + 1) * 128],
                                                                       posb[kv][:, l:l + 1], start=(l == 0), stop=(l == 31)),
                          w, s_pe if l == 31 else None)
        ph.op("act", lambda e, kv=kv: e.activation(out=cbias[kv][:], in_=pb[:], func=AF.Copy), [(s_pe, v)], s_act)
    pe_end = {}
    it = 0
    for g in range(4):
        for kv in range(2):
            xb = it % 2
            slab = kv * 4 + g
            w = [(s_pe, pe_end[it - 2])] if it >= 2 else []
            for r in range(2):
                ph.dma("sp", xin[xb][:].rearrange("p (i r j) -> p i r j", r=2, j=128)[:, :, r, :],
                       gk[gk_row(r, slab): gk_row(r, slab) + 128, :].rearrange("p (i j) -> p i j", j=128),
                       w, s_x[xb])
            xv = s_x[xb].n
            for jc in range(2):
                for l in range(32):
                    w = []
                    if l == 0:
                        w = [(s_x[xb], xv), (s_act, s_act.n)]
                    rhs = xin[xb][:, l:l + 16 * 254 + 1:16]
                    v = ph.op("pe", lambda e, kv=kv, jc=jc, l=l, rhs=rhs: e.matmul(
                        phid[jc][:, 0:255], w1[kv][:, l, jc * 128:(jc + 1) * 128], rhs, start=(l == 0), stop=(l == 31)),
                        w, s_pe if l == 31 else None)
                ph.op("act", lambda e, kv=kv, jc=jc: e.activation(out=hid[:, jc, 0:255], in_=phid[jc][:, 0:255], func=AF.Silu,
                                                                  bias=cbias[kv][:, jc:jc + 1]),
                      [(s_pe, v), (s_pe, s_pe.n)], s_act)
            av = s_act.n
            if kv == 0:
                for jc in range(2):
                    v = ph.op("pe", lambda e, jc=jc: e.matmul(pk[:, 0:255], w2[0][:, jc, :], hid[:, jc, 0:255],
                                                               start=(jc == 0), stop=(jc == 1)),
                              [(s_act, av), (s_dve, s_dve.n), (s_pl, s_pl.n)] if jc == 0 else [], s_pe if jc == 1 else None)
                a2 = ph.op("act", lambda e: e.activation(out=zb[:, 0:255], in_=pk[:, 0:255], func=AF.Copy), [(s_pe, v)], s_act)
                v2 = ph.op("pe", lambda e: e.matmul(pr[:, 0:255], RT[:], zb[:, 0:255], start=True, stop=True), [(s_act, a2)], s_pe)
                ph.op("pool", lambda e: e.tensor_tensor(out=t1[:, 0:255], in0=zb[:, 0:255], in1=cosc[:, 0:255], op=ALU.mult),
                      [(s_act, a2)])
                dv = ph.op("dve", lambda e: e.tensor_tensor(out=t2[:, 0:255], in0=pr[:, 0:255], in1=sinc[:, 0:255], op=ALU.mult),
                           [(s_pe, v2)], s_dve)
                ph.op("pool", lambda e, g=g: e.tensor_tensor(out=KCT[:, g, 0:255], in0=t1[:, 0:255], in1=t2[:, 0:255], op=ALU.add),
                      [(s_dve, dv)], s_pl)
            else:
                for nt in range(2):
                    M = 128 if nt == 0 else 127
                    for jc in range(2):
                        v = ph.op("pe", lambda e, jc=jc, nt=nt, M=M: e.matmul(pv[0:M, nt, :], hid[:, jc, nt * 128:nt * 128 + M],
                                                                               w2[1][:, jc, :], start=(jc == 0), stop=(jc == 1)),
                                  [(s_act, av)] if (jc == 0 and nt == 0) else [], s_pe if (jc == 1 and nt == 1) else None)
                ph.op("act", lambda e, g=g: e.activation(out=VCA[:, g, 0, 0:128], in_=pv[:, 0, :], func=AF.Copy), [(s_pe, v)], None)
                ph.op("act", lambda e, g=g: e.activation(out=VCA[0:127, g, 1, 0:128], in_=pv[0:127, 1, :], func=AF.Copy), [], s_act)
            pe_end[it] = s_pe.n
            it += 1
    ph.wait("sp", [(s_act, s_act.n), (s_pl, s_pl.n), (s_ld, LD_ALL)])
    ph.run()


class SPipe:
    def __init__(self, ph):
        self.ph = ph
        self.buf = [ph.ps("sA", [128, 512]), ph.ps("sB", [128, 512])]
        self.s_qk = ph.sem("qk")
        self.s_ac = ph.sem("ac")
        self.k = 0
        self.ac_of = {}
        self.pending = None

    def tile(self, mm, M, N, act, post=None, first_waits=(), three_d=True):
        ph = self.ph
        k = self.k
        ps = self.buf[k % 2][0:M, 0:N]
        n = len(mm)
        v = 0
        for j, (l, r) in enumerate(mm):
            w = []
            if j == 0:
                w = list(first_waits)
                if k >= 2:
                    w.append((self.s_ac, self.ac_of[k - 2]))
            o = ps.rearrange("p (r q) -> p r q", r=4) if (three_d and len(r.shape) == 3) else ps
            v = ph.op("pe", lambda e, l=l, r=r, j=j, o=o: e.matmul(o, l, r, start=(j == 0), stop=(j == n - 1)),
                      w, self.s_qk if j == n - 1 else None)
        before = self.s_ac.n
        act(ps, [(self.s_qk, v)])
        assert self.s_ac.n == before + 1
        self.ac_of[k] = self.s_ac.n
        self.flush()
        if post is not None:
            acv = self.s_ac.n
            self.pending = lambda: post([(self.s_ac, acv)])
        self.k += 1

    def flush(self):
        if self.pending is not None:
            p = self.pending
            self.pending = None
            p()


def phase_attn(nc, name, C, T, KCT, VCA, gates):
    ph = Phase(nc, name)
    gk, gv = T["gk"], T["gv"]
    ident = C["ident"]
    qt0 = ph.sb("qt0", [128, 4, NT], BF)
    qt = [qt0, qt0]
    ksel = [ph.sb(f"ks{i}", [128, S], BF) for i in range(2)]
    kwin = [ph.sb(f"kw{i}", [128, S], BF) for i in range(2)]
    vsel = [ph.sb(f"vs{i}", [128, 32, 130], BF) for i in range(2)]
    vwin = [ph.sb(f"vw{i}", [128, 32, 130], BF) for i in range(2)]
    cmpmask = ph.sb("cmpm", [128, 2, NT], BF)
    fb = ph.sb("fb", [128, 16, 64], F32)
    expand = ph.sb("exp", [64, 32, 128], BF)
    cmaskS = ph.sb("cms", [128, 2, 128], BF)
    wmask = ph.sb("wm", [128, 6, 128], BF)
    PcT = ph.sb("pct", [128, 2, 512], BF)
    PT = [ph.sb(f"pt{i}", [128, 512], BF) for i in range(3)]
    dsafe = ph.sb("dsafe", [128, 12], F32)
    rec = ph.sb("rec", [128, 12], F32)
    coef = ph.sb("coef", [128, 12], F32)
    impm = ph.sb("impm", [128, 64], F32)
    impr = ph.sb("impr", [128, 64], F32)
    m8 = ph.sb("m8", [128, 16], F32)
    biasq = ph.sb("biasq", [128, 64], BF)
    BiasT = ph.sb("biasT", [64, 128], BF)
    o_acc = ph.sb("oacc", [128, 4, 128], F32)
    tmpo = ph.sb("tmpo", [128, 4, 128], F32)
    o_bf = ph.sb("obf", [128, 4, 128], BF)
    oTs = ph.sb("oTs", [128, 4, NT], BF)
    oc = ph.ps("oc", [128, 4, 256])
    osl = ph.ps("os", [128, 4, 256])
    ow = ph.ps("ow", [128, 4, 256])
    sp = SPipe(ph)
    s_c = ph.sem("c")
    s_ld = [ph.sem("ld"), ph.sem("ld")]
    s_pv = ph.sem("pv")
    s_dv = ph.sem("dv")
    s_pl = ph.sem("pl")
    s_st = ph.sem("st")
    s_ms = ph.sem("ms")
    for dst, src in ((cmpmask, "cmpmask"), (fb, "fb"), (expand, "expand"), (cmaskS, "cmaskS"), (wmask, "wmask")):
        ph.dma("sp", dst[:], C[src], sig=s_c)
    NC_C = 16 * 5
    for b in range(2):
        ph.op("dve", lambda e, b=b: e.memset(vsel[b][:, :, 129:130], 0.0), [], s_ms)
        ph.op("dve", lambda e, b=b: e.memset(vwin[b][:, :, 129:130], 0.0), [], s_ms)
        ph.op("dve", lambda e, b=b: e.memset(vsel[b][:, :, 128:129], 1.0), [], s_ms)
        ph.op("dve", lambda e, b=b: e.memset(vwin[b][:, :, 128:129], 1.0), [], s_ms)
    grp_pe_end = {}
    pt_n = [0]
    pv_of_pt = {}
    late = [None]
    dv_last_oc = [0]
    dv_last_os = [0]
    dv_last_ow = [0]
    pl_last = [0]

    def load_group(g):
        b = g % 2
        w = [(s_pv, grp_pe_end[g - 2]), (sp.s_qk, grp_pe_end[("qk", g - 2)])] if g >= 2 else []
        for r in range(2):
            for dst, slab in ((ksel[b], 8 + g), (kwin[b], 12 + g)):
                ph.dma("sp", dst[:].rearrange("p (i r j) -> p i r j", r=2, j=128)[:, :, r, :],
                       gk[gk_row(r, slab): gk_row(r, slab) + 128, :].rearrange("p (i j) -> p i j", j=128),
                       w, s_ld[b])
            for dst, c0 in ((vsel[b], 0), (vwin[b], 512)):
                for hf in range(2):
                    r0 = gv_row(r, hf * 1024)
                    ph.dma("sp", dst[:].rearrange("p (i r) c -> p i r c", r=2)[:, hf * 8:(hf + 1) * 8, r, 0:128],
                           gv[r0:r0 + 1024, c0 + g * 128: c0 + (g + 1) * 128].rearrange("(i j) c -> j i c", j=128),
                           w, s_ld[b])
        return s_ld[b].n

    ldv = {0: load_group(0)}
    for g in range(4):
        b = g % 2
        wq = [(s_pv, grp_pe_end[g - 1]), (sp.s_qk, grp_pe_end[("qk", g - 1)])] if g >= 1 else []
        ph.dma("sp", qt0[:], T["qT"][4 * g:4 * g + 4].rearrange("h p t -> p h t"), wq, s_ld[b])
        ldv[g] = s_ld[b].n
        if g + 1 < 4:
            ldv[g + 1] = load_group(g + 1)
        for i in range(16):
            first = [(s_ld[b], ldv[g]), (s_c, NC_C), (s_ms, 8)] if i == 0 else []
            qti = qt[b][:, :, i * 128:(i + 1) * 128]
            for nt in range(2):
                mm = [(KCT[:, g, nt * 128:(nt + 1) * 128], qti),
                      (ident[:], cmpmask[:, nt, i * 128:(i + 1) * 128].unsqueeze(1).broadcast_to([128, 4, 128]))]

                def act(ps, w, nt=nt):
                    ww = list(w)
                    if nt == 0:
                        ww.append((s_pv, s_pv.n))
                    ph.op("act", lambda e: e.activation(out=PcT[:, nt, :], in_=ps, func=AF.Exp, scale=SCALE), ww, sp.s_ac)

                post = None
                if nt == 1:
                    def post(w, g=g):
                        ww = list(w) + [(s_dv, dv_last_oc[0])]
                        for r in range(4):
                            for n2 in range(2):
                                ph.op("pe", lambda e, r=r, n2=n2: e.matmul(oc[:, r, 0:194], PcT[:, n2, r * 128:(r + 1) * 128],
                                                                          VCA[:, g, n2, :], start=(n2 == 0), stop=(n2 == 1)),
                                      ww if (r == 0 and n2 == 0) else [], s_pv if (r == 3 and n2 == 1) else None)
                sp.tile(mm, 128, 512, act, post, first if nt == 0 else [])
            sp.flush()
            pv_c = s_pv.n
            def dchain(fn, w=()):
                v = ph.op("dve", fn, [(s_dv, s_dv.n)] + list(w), s_dv)
                return v
            dchain(lambda e: e.tensor_scalar(out=dsafe[:, 0:4], in0=oc[:, :, 128], scalar1=1e-30, scalar2=None, op0=ALU.max),
                   [(s_pv, pv_c), (s_pl, pl_last[0])])
            dchain(lambda e: e.reciprocal(out=rec[:, 0:4], in_=dsafe[:, 0:4]))
            for r in range(4):
                src1 = fb[:, i, :] if r == 0 else impm[:]
                dchain(lambda e, r=r, src1=src1: e.scalar_tensor_tensor(out=impm[:], in0=oc[:, r, 129:193], scalar=rec[:, r:r + 1],
                                                                        in1=src1, op0=ALU.mult, op1=ALU.add))
            dchain(lambda e: e.max(out=m8[:, 0:8], in_=impm[:]))
            dchain(lambda e: e.match_replace(out=impr[:], in_to_replace=m8[:, 0:8], in_values=impm[:], imm_value=-1e9))
            dchain(lambda e: e.max(out=m8[:, 8:16], in_=impr[:]))
            bq = dchain(lambda e: e.tensor_scalar(out=biasq[:], in0=impm[:], scalar1=m8[:, 15:16], scalar2=NEG,
                                                  op0=ALU.is_lt, op1=ALU.mult), [(sp.s_qk, sp.s_qk.n)])
            gsl = gates[:, i, 12 * g:12 * g + 12].rearrange("p (h c) -> p h c", c=3)
            dchain(lambda e, gsl=gsl: e.tensor_tensor(out=coef[:, 0:4], in0=rec[:, 0:4], in1=gsl[:, :, 0], op=ALU.mult))
            dv_last_oc[0] = dchain(lambda e: e.tensor_tensor(out=o_acc[:], in0=oc[:, :, 0:128],
                                                             in1=coef[:, 0:4].unsqueeze(2).broadcast_to([128, 4, 128]), op=ALU.mult))
            wt = [j for j in range(6) if 2 * i - 4 + j >= 0]
            for j in wt:
                kt = 2 * i - 4 + j
                mm = [(kwin[b][:, kt * 128:(kt + 1) * 128], qti)]
                if j not in (2, 3):
                    mm.append((ident[:], wmask[:, j, :].unsqueeze(1).broadcast_to([128, 4, 128])))
                n = pt_n[0]
                pt_n[0] += 1
                pb = n % 3

                def act(ps, w, pb=pb, n=n):
                    ww = list(w)
                    if n >= 3:
                        ww.append((s_pv, pv_of_pt[n - 3]))
                    ph.op("act", lambda e: e.activation(out=PT[pb][:], in_=ps, func=AF.Exp, scale=SCALE), ww, sp.s_ac)

                def post(w, pb=pb, n=n, kt=kt, j=j, b=b, wt=wt):
                    ww = list(w)
                    if j == wt[0]:
                        ww.append((s_dv, dv_last_ow[0]))
                    for r in range(4):
                        v = ph.op("pe", lambda e, r=r: e.matmul(ow[:, r, 0:130], PT[pb][:, r * 128:(r + 1) * 128], vwin[b][:, kt, :],
                                                                 start=(j == wt[0] and r in (0, 2)), stop=(j == wt[-1]),
                                                                 skip_group_check=True),
                                  ww if r == 0 else [], s_pv if r == 3 else None)
                    pv_of_pt[n] = v

                sp.tile(mm, 128, 512, act, post)
            if late[0] is not None:
                late[0]()
                late[0] = None
            def actb(ps, w):
                ph.op("act", lambda e: e.activation(out=BiasT[:], in_=ps, func=AF.Copy), list(w) + [(s_pv, s_pv.n)], sp.s_ac)
            sp.tile([(biasq[:], ident[:])], 64, 128, actb, None, [(s_dv, bq)], three_d=False)
            bias_ready = sp.s_ac.n
            nkt = 2 * i + 2
            for kt in range(nkt):
                mm = [(ksel[b][:, kt * 128:(kt + 1) * 128], qti),
                      (expand[:, kt, :], BiasT[:].unsqueeze(1).broadcast_to([64, 4, 128]))]
                if kt >= 2 * i:
                    mm.append((ident[:], cmaskS[:, kt - 2 * i, :].unsqueeze(1).broadcast_to([128, 4, 128])))
                n = pt_n[0]
                pt_n[0] += 1
                pb = n % 3

                def act(ps, w, pb=pb, n=n):
                    ww = list(w)
                    if n >= 3:
                        ww.append((s_pv, pv_of_pt[n - 3]))
                    ph.op("act", lambda e: e.activation(out=PT[pb][:], in_=ps, func=AF.Exp, scale=SCALE), ww, sp.s_ac)

                def post(w, pb=pb, n=n, kt=kt, nkt=nkt, b=b):
                    ww = list(w)
                    if kt == 0:
                        ww.append((s_dv, dv_last_os[0]))
                    for r in range(4):
                        v = ph.op("pe", lambda e, r=r: e.matmul(osl[:, r, 0:130], PT[pb][:, r * 128:(r + 1) * 128], vsel[b][:, kt, :],
                                                                 start=(kt == 0 and r in (0, 2)), stop=(kt == nkt - 1),
                                                                 skip_group_check=True),
                                  ww if r == 0 else [], s_pv if r == 3 else None)
                    pv_of_pt[n] = v

                sp.tile(mm, 128, 512, act, post, [(sp.s_ac, bias_ready)] if kt == 0 else [])
            sp.flush()
            pv_end = s_pv.n
            dchain(lambda e: e.tensor_scalar(out=dsafe[:, 4:8], in0=osl[:, :, 128], scalar1=1e-30, scalar2=None, op0=ALU.max),
                   [(s_pv, pv_end)])
            dchain(lambda e: e.tensor_scalar(out=dsafe[:, 8:12], in0=ow[:, :, 128], scalar1=1e-30, scalar2=None, op0=ALU.max))
            dchain(lambda e: e.reciprocal(out=rec[:, 4:12], in_=dsafe[:, 4:12]))
            dchain(lambda e, gsl=gsl: e.tensor_tensor(out=coef[:, 4:8], in0=rec[:, 4:8], in1=gsl[:, :, 1], op=ALU.mult))
            dchain(lambda e, gsl=gsl: e.tensor_tensor(out=coef[:, 8:12], in0=rec[:, 8:12], in1=gsl[:, :, 2], op=ALU.mult))
            d1 = dchain(lambda e: e.tensor_tensor(out=tmpo[:], in0=osl[:, :, 0:128],
                                                  in1=coef[:, 4:8].unsqueeze(2).broadcast_to([128, 4, 128]), op=ALU.mult),
                        [(s_pl, s_pl.n)])
            dv_last_os[0] = d1
            p1 = ph.op("pool", lambda e: e.tensor_tensor(out=o_acc[:], in0=o_acc[:], in1=tmpo[:], op=ALU.add), [(s_dv, d1)], s_pl)
            d2 = dchain(lambda e: e.tensor_tensor(out=tmpo[:], in0=ow[:, :, 0:128],
                                                  in1=coef[:, 8:12].unsqueeze(2).broadcast_to([128, 4, 128]), op=ALU.mult),
                        [(s_pl, p1)])
            dv_last_ow[0] = d2
            p2 = ph.op("pool", lambda e: e.tensor_tensor(out=o_bf[:], in0=o_acc[:], in1=tmpo[:], op=ALU.add),
                       [(s_dv, d2), (sp.s_qk, sp.s_qk.n)], s_pl)
            pl_last[0] = p2

            def do_late(g=g, i=i, p2=p2):
                def actT(ps, w):
                    ww = list(w)
                    if i == 0 and g >= 1:
                        ww.append((s_st, 16 * g))
                    ph.op("act", lambda e: e.activation(out=oTs[:, :, i * 128:(i + 1) * 128],
                                                        in_=ps.rearrange("p (r q) -> p r q", r=4), func=AF.Copy), ww, sp.s_ac)
                k = sp.k
                pso = sp.buf[k % 2]
                w0 = [(s_pl, p2)]
                if k >= 2:
                    w0.append((sp.s_ac, sp.ac_of[k - 2]))
                v = 0
                for r in range(4):
                    v = ph.op("pe", lambda e, r=r: e.matmul(pso[:, r * 128:(r + 1) * 128], o_bf[:, r, :], ident[:], start=True, stop=True),
                              w0 if r == 0 else [], sp.s_qk if r == 3 else None)
                actT(pso[:, :], [(sp.s_qk, v)])
                sp.ac_of[k] = sp.s_ac.n
                sp.flush()
                sp.k += 1

            late[0] = do_late
            if "dbg_k" in T and g == 0 and i == 1:
                sd = ph.sem("dbgs")
                w = [(s_pl, p2), (s_dv, s_dv.n)]
                for nm, src in (("dbg_k", ksel[b][:]), ("dbg_kw", kwin[b][:]),
                                ("dbg_v", vsel[b][:].rearrange("p a c -> p (a c)")), ("dbg_vw", vwin[b][:].rearrange("p a c -> p (a c)")),
                                ("dbg_bt", BiasT[:]), ("dbg_imp", impm[:]), ("dbg_ds", dsafe[:]), ("dbg_coef", coef[:]),
                                ("dbg_obf", o_bf[:].rearrange("p a c -> p (a c)")), ("dbg_m8", m8[:]),
                                ("dbg_kct", KCT[:].rearrange("p a c -> p (a c)")), ("dbg_vca", VCA[:].rearrange("p a b c -> p (a b c)")),
                                ("dbg_gates", gates[:].rearrange("p a c -> p (a c)"))):
                    ph.dma("sp", T[nm], src, w, sd)
                ph.wait("sp", [(sd, sd.n)])
        late[0]()
        late[0] = None
        grp_pe_end[g] = s_pv.n
        grp_pe_end[("qk", g)] = sp.s_qk.n
        ph.dma("sp", T["oT"][4 * g:4 * g + 4].rearrange("h p t -> p h t"), oTs[:], [(sp.s_ac, sp.s_ac.n)], s_st)
    ph.wait("sp", [(s_st, 64)])
    ph.run()


def load_act(ph, actT, src, sem, nk=16):
    for k in range(0, nk, 4):
        ph.dma("sp", actT[:, k:k + 4, :], src[k:k + 4].rearrange("k p t -> p k t"), [], sem)
    return sem.n


def phase_proj(nc, name, actT, wsrc, mode, T, C, src_act=None, xsrc=None):
    ph = Phase(nc, name)
    G = Gemm(ph, 16)
    fm = T["fm"]
    s_a = ph.sem("a")
    av = load_act(ph, actT, src_act, s_a) if src_act is not None else 0
    yf = [ph.sb(f"yf{i}", [128, 512], F32) for i in range(2)]
    s_po = ph.sem("po")
    s_l = [ph.sem("l"), ph.sem("l")]
    if mode == "resid":
        so = SlabOut(ph, F32, 2, "xs")
        xl = [ph.sb(f"xl{i}", [128, NT], F32) for i in range(2)]
        aux = aux2 = None
    else:
        so = SlabOut(ph, BF, 2)
        aux = [ph.sb(f"ax{i}", [128, NT], BF) for i in range(2)]
        aux2 = [ph.sb(f"ay{i}", [128, NT], BF) for i in range(2)] if mode == "conv" else None
        tt = [ph.sb(f"tt{i}", [128, 512], F32) for i in range(2)]
    s_p2 = ph.sem("p2")
    tn = 0
    post_of = {}
    lv_of = {}

    def issue_load(c):
        sl = c % 2
        lw = [(s_po, post_of.get(c - 2, 0))]
        if mode == "resid":
            ph.dma("sp", xl[sl][:], xsrc[c], lw, s_l[sl])
        else:
            ph.dma("sp", aux[sl][:], fm[(48 if mode == "attn" else 64) + c], lw, s_l[sl])
            if mode == "conv":
                ph.dma("sp", aux2[sl][:], T["maT"][c], lw, s_l[sl])
        lv_of[c] = s_l[sl].n

    G.plan([wsrc[:, j * 512:(j + 1) * 512] for j in range(4)])
    for j in range(4):
        b, wv = G.next_weights()
        for cb in range(4):
            c = 4 * j + cb
            slot, free_w = so.begin()
            if c == 0:
                issue_load(0)
            if c + 1 < 16:
                issue_load(c + 1)
            lv = lv_of[c]
            last = 0
            for tb in range(4):
                mm = [(G.wb[b][:, kc, cb * 128:(cb + 1) * 128], actT[:, kc, tb * 512:(tb + 1) * 512]) for kc in range(16)]
                fw = [(G.s_w[b], wv), (s_a, av)] if (cb == 0 and tb == 0) else []
                yb = tn % 2
                tsl = slice(tb * 512, (tb + 1) * 512)

                def evac(ps, w, yb=yb, tn=tn):
                    ww = list(w) + [(s_po, post_of.get(("t", tn - 2), 0))]
                    ph.op("act", lambda e: e.activation(out=yf[yb][:], in_=ps, func=AF.Copy), ww, G.s_ep)

                G.tile(mm, evac, fw)
                ev = G.s_ep.n
                w0 = [(G.s_ep, ev), (s_l[slot], lv)] + (free_w if tb == 0 else [])
                if mode == "attn":
                    last = ph.op("pool", lambda e, yb=yb, slot=slot, tsl=tsl: e.tensor_tensor(
                        out=so.buf[slot][:, tsl], in0=yf[yb][:], in1=aux[slot][:, tsl], op=ALU.mult), w0, s_po)
                elif mode == "conv":
                    p = ph.op("pool", lambda e, yb=yb, slot=slot, tsl=tsl: e.tensor_tensor(
                        out=tt[yb][:], in0=yf[yb][:], in1=aux[slot][:, tsl], op=ALU.mult),
                        w0 + [(s_po, post_of.get(("t", tn - 2), 0))], s_p2)
                    last = ph.op("dve", lambda e, yb=yb, slot=slot, tsl=tsl: e.tensor_tensor(
                        out=so.buf[slot][:, tsl], in0=tt[yb][:], in1=aux2[slot][:, tsl], op=ALU.add), [(s_p2, p)], s_po)
                else:
                    last = ph.op("dve", lambda e, yb=yb, slot=slot, tsl=tsl: e.tensor_tensor(
                        out=so.buf[slot][:, tsl], in0=yf[yb][:], in1=xl[slot][:, tsl], op=ALU.add), w0, s_po)
                post_of[("t", tn)] = last
                tn += 1
            post_of[c] = last
            dst = T["xres"][c] if mode == "resid" else (T["maT"][c] if mode == "attn" else T["mT"][c])
            so.store(slot, dst, [(s_po, last)])
        G.end_job()
    so.drain()
    ph.run()


def phase_conv(nc, name, actT, W, L, C, T):
    ph = Phase(nc, name)
    fm, gt = T["fm"], T["gt"]
    xi = [ph.sb(f"xi{i}", [128, NT], BF) for i in range(2)]
    gc = [ph.sb(f"gc{i}", [128, NT], BF) for i in range(2)]
    gb = [ph.sb(f"gb{i}", [128, NT], BF) for i in range(2)]
    ub = [ph.sb(f"ub{i}", [128, 16, 130], F32) for i in range(2)]
    acc = [ph.sb(f"acc{i}", [128, 16, 128], F32) for i in range(2)]
    H0 = ph.sb("H0", [128, 16, 16, 2], BF)
    H1 = ph.sb("H1", [128, 16, 16, 2], BF)
    Ht = ph.sb("Ht", [128, 16, 16, 2], F32)
    halo = ph.sb("halo", [128, 16, 16, 2], F32)
    cw = ph.sb("cw", [128, 16, 3], F32)
    fl = ph.sb("fl", [128, 2], F32)
    s_c = ph.sem("c")
    s_l = [ph.sem("l"), ph.sem("l")]
    s_pl = ph.sem("pl")
    s_dv = ph.sem("dv")
    m0 = ph.op("dve", lambda e: e.memset(H0[:], 0.0), [], s_dv)
    ph.dma("sp", H0[:, :, 1:16, :].rearrange("p c i t -> p c (i t)"),
           gt[2048:4096, 0:30].rearrange("(c p) x -> p c x", p=128), [(s_dv, m0)], s_c)
    ph.dma("sp", H1[:].rearrange("p c i t -> p c (i t)"), gt[0:2048, :].rearrange("(c p) x -> p c x", p=128), [], s_c)
    ph.dma("sp", cw[:], W["conv_wT"][L], [], s_c)
    ph.dma("sp", fl[:], C["flags"], [], s_c)
    h1 = ph.op("dve", lambda e: e.tensor_scalar(out=Ht[:], in0=H0[:], scalar1=fl[:, 0:1], scalar2=None, op0=ALU.mult),
               [(s_c, 64)], s_dv)
    h2 = ph.op("dve", lambda e: e.scalar_tensor_tensor(out=halo[:], in0=H1[:], scalar=fl[:, 1:2], in1=Ht[:],
                                                       op0=ALU.mult, op1=ALU.add), [(s_dv, h1)], s_dv)
    dv_of = {}
    pl_of = {}
    for c in range(16):
        b = c % 2
        w = [(s_dv, dv_of[c - 2])] if c >= 2 else []
        for dst, off in ((xi[b], 0), (gb[b], 16), (gc[b], 32)):
            ph.dma("sp", dst[:], fm[off + c], w, s_l[b])
        lv = s_l[b].n
        ph.op("pool", lambda e, b=b, c=c: e.tensor_copy(out=ub[b][:, :, 0:2], in_=halo[:, c, :, :]),
              [(s_dv, h2)] + w, None)
        pl_of[c] = ph.op("pool", lambda e, b=b: e.tensor_tensor(out=ub[b][:, :, 2:130],
                                                                in0=xi[b][:].rearrange("p (i j) -> p i j", j=128),
                                                                in1=gc[b][:].rearrange("p (i j) -> p i j", j=128), op=ALU.mult),
                         [(s_l[b], lv)], s_pl)
        d = ph.op("dve", lambda e, b=b, c=c: e.tensor_scalar(out=acc[b][:], in0=ub[b][:, :, 2:130], scalar1=cw[:, c, 2:3],
                                                             scalar2=None, op0=ALU.mult), [(s_pl, pl_of[c])], s_dv)
        d = ph.op("dve", lambda e, b=b, c=c: e.scalar_tensor_tensor(out=acc[b][:], in0=ub[b][:, :, 1:129], scalar=cw[:, c, 1:2],
                                                                    in1=acc[b][:], op0=ALU.mult, op1=ALU.add), [(s_dv, d)], s_dv)
        d = ph.op("dve", lambda e, b=b, c=c: e.scalar_tensor_tensor(out=acc[b][:], in0=ub[b][:, :, 0:128], scalar=cw[:, c, 0:1],
                                                                    in1=acc[b][:], op0=ALU.mult, op1=ALU.add), [(s_dv, d)], s_dv)
        dv_of[c] = ph.op("dve", lambda e, b=b, c=c: e.tensor_tensor(out=actT[:, c, :].rearrange("p (i j) -> p i j", j=128),
                                                                    in0=acc[b][:], in1=gb[b][:].rearrange("p (i j) -> p i j", j=128),
                                                                    op=ALU.mult), [(s_dv, d)], s_dv)
    ph.wait("sp", [(s_dv, s_dv.n)])
    ph.run()


def phase_ffn(nc, name, actT, W, L, T):
    ph = Phase(nc, name)
    Gu = Gemm(ph, 16)
    Gd = Gemm(ph, 8, pfx="d")
    fT = ph.sb("fT", [128, 8, NT], BF)
    rf = [ph.sb(f"rf{i}", [128, 512], F32) for i in range(2)]
    yf = [ph.sb(f"yf{i}", [128, 512], F32) for i in range(2)]
    so = SlabOut(ph, F32, 2, "xs")
    xl = [ph.sb(f"xl{i}", [128, NT], F32) for i in range(2)]
    s_pl = ph.sem("pl")
    s_po = ph.sem("po")
    s_l = [ph.sem("l"), ph.sem("l")]
    xres = T["xres"]
    un = 0
    dn = 0
    pl_of = {}
    po_of = {}
    slab_store = {}
    slab_n = 0
    slab_last = {}
    lv_of = {}

    def issue_load(n):
        sl = n % 2
        c = n % 16
        lw = [(s_po, slab_last.get(n - 2, 0))]
        if c in slab_store:
            lw.append(slab_store[c])
        ph.dma("sp", xl[sl][:], xres[c], lw, s_l[sl])
        lv_of[n] = s_l[sl].n

    Gu.plan([W["w_up"][L][:, hg * 1024 + j * 512: hg * 1024 + (j + 1) * 512] for hg in range(8) for j in range(2)])
    Gd.plan([W["w_down"][L][hg * 1024:(hg + 1) * 1024, j * 512:(j + 1) * 512] for hg in range(8) for j in range(4)])
    for hg in range(8):
        for j in range(2):
            b, wv = Gu.next_weights()
            for cb in range(4):
                for tb in range(4):
                    mm = [(Gu.wb[b][:, kc, cb * 128:(cb + 1) * 128], actT[:, kc, tb * 512:(tb + 1) * 512]) for kc in range(16)]
                    fw = [(Gu.s_w[b], wv)] if (cb == 0 and tb == 0) else []
                    rb = un % 2

                    def evac(ps, w, rb=rb, un=un):
                        ph.op("act", lambda e: e.activation(out=rf[rb][:], in_=ps, func=AF.Relu),
                              list(w) + [(s_pl, pl_of.get(un - 2, 0))], Gu.s_ep)

                    Gu.tile(mm, evac, fw)
                    pl_of[un] = ph.op("pool", lambda e, rb=rb, j=j, cb=cb, tb=tb: e.tensor_tensor(
                        out=fT[:, j * 4 + cb, tb * 512:(tb + 1) * 512], in0=rf[rb][:], in1=rf[rb][:], op=ALU.mult),
                        [(Gu.s_ep, Gu.s_ep.n), (Gd.s_pe, Gd.s_pe.n)], s_pl)
                    un += 1
            Gu.end_job()
        f_ready = s_pl.n
        for j in range(4):
            b, wv = Gd.next_weights()
            for cb in range(4):
                c = 4 * j + cb
                slot, free_w = so.begin()
                if slab_n == 0:
                    issue_load(0)
                if slab_n + 1 < 128:
                    issue_load(slab_n + 1)
                lv = lv_of[slab_n]
                last = 0
                for tb in range(4):
                    mm = [(Gd.wb[b][:, kc, cb * 128:(cb + 1) * 128], fT[:, kc, tb * 512:(tb + 1) * 512]) for kc in range(8)]
                    fw = [(Gd.s_w[b], wv), (s_pl, f_ready)] if (cb == 0 and tb == 0) else []
                    yb = dn % 2
                    tsl = slice(tb * 512, (tb + 1) * 512)

                    def evac(ps, w, yb=yb, dn=dn):
                        ph.op("act", lambda e: e.activation(out=yf[yb][:], in_=ps, func=AF.Copy),
                              list(w) + [(s_po, po_of.get(dn - 2, 0))], Gd.s_ep)

                    Gd.tile(mm, evac, fw)
                    last = ph.op("dve", lambda e, yb=yb, slot=slot, tsl=tsl: e.tensor_tensor(
                        out=so.buf[slot][:, tsl], in0=yf[yb][:], in1=xl[slot][:, tsl], op=ALU.add),
                        [(Gd.s_ep, Gd.s_ep.n), (s_l[slot], lv)] + (list(free_w) if tb == 0 else []), s_po)
                    po_of[dn] = last
                    dn += 1
                slab_last[slab_n] = last
                slab_n += 1
                v = so.store(slot, xres[c], [(s_po, last)])
                slab_store[c] = (so.s_st[slot], v)
            Gd.end_job()
    so.drain()
    ph.run()


WEIGHT_USERS = {
    "w_in": ("win",), "cmp_w1_k": ("cmp",), "cmp_w2_k": ("cmp",), "cmp_pos_kT": ("cmp",),
    "cmp_w1_v": ("cmp",), "cmp_w2_v": ("cmp",), "cmp_pos_vT": ("cmp",), "conv_wT": ("cv",),
    "w_attn_proj": ("ap",), "w_conv_out": ("co",), "w_o": ("wo",), "w_up": ("ffn",), "w_down": ("ffn",),
    "norm1_gT": ("n1",), "norm2_gT": ("n2",), "final_gT": ("nf",),
}
LAST_SHAPES = {}


def build(sel=None, dbg=()):
    nc = bass.Bass("TRN2", target_bir_lowering=False)

    def on(L, nm):
        return sel is None or (L, nm) in sel

    def din(nm, shape, dt=F32):
        LAST_SHAPES[nm] = tuple(shape)
        return nc.dram_tensor(nm, shape, dt, kind="ExternalInput").ap()

    def scr(nm, shape, dt=BF):
        if nm in dbg:
            t = nc.dram_tensor(nm, shape, dt, kind="ExternalOutput")
        else:
            t = nc.dram_tensor(nm, shape, dt)
        return t

    W = {}
    for nm, shp in (("w_in", [DEPTH, D, IN_COLS]),
                    ("cmp_w1_k", [DEPTH, 4096, 256]), ("cmp_w2_k", [DEPTH, 256, 128]), ("cmp_pos_kT", [DEPTH, 128, 32]),
                    ("cmp_w1_v", [DEPTH, 4096, 256]), ("cmp_w2_v", [DEPTH, 256, 128]), ("cmp_pos_vT", [DEPTH, 128, 32]),
                    ("conv_wT", [DEPTH, 128, 16, 3]), ("w_attn_proj", [DEPTH, D, D]), ("w_conv_out", [DEPTH, D, D]),
                    ("w_o", [DEPTH, D, D]), ("w_up", [DEPTH, D, DFF]), ("w_down", [DEPTH, DFF, D]),
                    ("norm1_gT", [DEPTH, 128, 16]), ("norm2_gT", [DEPTH, 128, 16]), ("final_gT", [128, 16])):
        users = WEIGHT_USERS[nm]
        used = sel is None or any((L, u) in sel for L in range(DEPTH + 1) for u in users)
        if not used:
            shp = [1] * len(shp)
        elif sel is not None and nm != "final_gT" and not any((1, u) in sel for u in users):
            shp = [1] + list(shp[1:])
        W[nm] = din(nm, shp)
    xin = din("xT", [16, 128, NT])
    Cd = {}
    for nm, shp, dt in (("cos", [128, NT], F32), ("sin", [128, NT], F32), ("cosc", [128, 256], F32), ("sinc", [128, 256], F32),
                        ("RT", [128, 128], BF), ("ident", [128, 128], BF), ("ones", [128, 128], BF), ("msel", [128, 2, 64], BF),
                        ("cmpmask", [128, 2, NT], BF), ("fb", [128, 16, 64], F32), ("expand", [64, 32, 128], BF),
                        ("cmaskS", [128, 2, 128], BF), ("wmask", [128, 6, 128], BF), ("flags", [128, 2], F32)):
        Cd[nm] = din("c_" + nm, shp, dt)
    outT = nc.dram_tensor("outT", [16, 128, NT], F32, kind="ExternalOutput").ap()

    T = {}
    T["qT"] = scr("qT", [16, 128, NT]).ap()
    T["fm"] = scr("fm", [80, 128, NT]).ap()
    for nm, shp in (("gk_in", [2048, NT]), ("gk", [4096, NT]), ("gv_in", [2048, 1024]), ("gv", [4096, 1024]),
                    ("gt_in", [2048, 32]), ("gt", [4096, 32])):
        t = scr(nm, shp)
        T[nm + "_t"] = t
        T[nm] = t.ap()
    T["oT"] = scr("oT", [16, 128, NT]).ap()
    T["maT"] = scr("maT", [16, 128, NT]).ap()
    T["mT"] = scr("mT", [16, 128, NT]).ap()
    T["xres"] = scr("xres", [16, 128, NT], F32).ap()
    if "dbg_k" in dbg:
        for nm, shp, dt in (("dbg_k", [128, S], BF), ("dbg_kw", [128, S], BF), ("dbg_v", [128, 32 * 130], BF),
                            ("dbg_vw", [128, 32 * 130], BF), ("dbg_bt", [64, 128], BF), ("dbg_imp", [128, 64], F32),
                            ("dbg_ds", [128, 12], F32), ("dbg_coef", [128, 12], F32), ("dbg_obf", [128, 512], BF),
                            ("dbg_m8", [128, 16], F32), ("dbg_kct", [128, 1024], BF), ("dbg_vca", [128, 4 * 2 * 194], BF),
                            ("dbg_gates", [128, 16 * 48], F32), ("dbg_osl", [128, 1024], F32), ("dbg_ow", [128, 1024], F32),
                            ("dbg_oacc", [128, 512], F32), ("dbg_tmpo", [128, 512], F32)):
            T[nm] = scr(nm, shp, dt).ap()

    with contextlib.ExitStack() as es:
        del SEM_POOL[:]
        for i in range(24):
            SEM_POOL.append([es.enter_context(nc.semaphore(f"sem{i}")), 0])
        actT = es.enter_context(nc.sbuf_tensor("actT", [128, 16, NT], BF))
        gates = es.enter_context(nc.sbuf_tensor("gates", [128, 16, 48], F32))
        KCT = es.enter_context(nc.sbuf_tensor("KCT", [128, 4, 256], BF))
        VCA = es.enter_context(nc.sbuf_tensor("VCA", [128, 4, 2, 194], BF))
        RT = es.enter_context(nc.sbuf_tensor("RTs", [128, 128], BF))
        ident = es.enter_context(nc.sbuf_tensor("idents", [128, 128], BF))
        ones = es.enter_context(nc.sbuf_tensor("oness", [128, 128], BF))
        T["gates"] = gates
        C = dict(Cd)
        C["RT"], C["ident"], C["ones"] = RT, ident, ones
        ph = Phase(nc, "init")
        s = ph.sem("s")
        ph.dma("sp", RT[:], Cd["RT"], [], s)
        ph.dma("sp", ident[:], Cd["ident"], [], s)
        ph.dma("sp", ones[:], Cd["ones"], [], s)
        ph.wait("sp", [(s, 48)])
        ph.run()
        xcur = xin
        for L in range(DEPTH):
            if on(L, "n1"):
                phase_norm(nc, f"n1_{L}", xcur, W["norm1_gT"][L], ones, dst_sb=actT)
            if on(L, "win"):
                phase_win(nc, f"win{L}", actT, W["w_in"][L], C, T)
            if on(L, "ag"):
                phase_gather(nc, f"ag{L}", T)
            if on(L, "cmp"):
                phase_compress(nc, f"cmp{L}", L, W, C, T, KCT, VCA)
            if on(L, "att"):
                phase_attn(nc, f"att{L}", C, T, KCT, VCA, gates)
            if on(L, "ap"):
                phase_proj(nc, f"ap{L}", actT, W["w_attn_proj"][L], "attn", T, C, src_act=T["oT"])
            if on(L, "cv"):
                phase_conv(nc, f"cv{L}", actT, W, L, C, T)
            if on(L, "co"):
                phase_proj(nc, f"co{L}", actT, W["w_conv_out"][L], "conv", T, C)
            if on(L, "wo"):
                phase_proj(nc, f"wo{L}", actT, W["w_o"][L], "resid", T, C, src_act=T["mT"], xsrc=xcur)
            xcur = T["xres"]
            if on(L, "n2"):
                phase_norm(nc, f"n2_{L}", xcur, W["norm2_gT"][L], ones, dst_sb=actT)
            if on(L, "ffn"):
                phase_ffn(nc, f"ffn{L}", actT, W, L, T)
        if on(DEPTH, "nf"):
            phase_norm(nc, "nf", xcur, W["final_gT"], ones, dst_dram=outT)
    return nc


def _bf(a):
    return np.ascontiguousarray(a.astype(ml_dtypes.bfloat16))


def _consts(p):
    f32 = np.float32
    tl = np.arange(NT)
    pos = (128 * (2 * (tl // 128) + p) + tl % 128)
    half = 64
    inv_freq = np.exp(-math.log(10000.0) * np.arange(half, dtype=f32) / half).astype(f32)
    ang = pos.astype(f32)[None, :] * inv_freq[:, None]
    c = {}
    c["cos"] = np.concatenate([np.cos(ang), np.cos(ang)], 0).astype(f32)
    c["sin"] = np.concatenate([np.sin(ang), np.sin(ang)], 0).astype(f32)
    cpos = (np.arange(256) * 16 + 31).astype(f32)
    angc = cpos[None, :] * inv_freq[:, None]
    c["cosc"] = np.concatenate([np.cos(angc), np.cos(angc)], 0).astype(f32)
    c["sinc"] = np.concatenate([np.sin(angc), np.sin(angc)], 0).astype(f32)
    RT = np.zeros((128, 128), f32)
    for d in range(64):
        RT[d + 64, d] = -1.0
        RT[d, d + 64] = 1.0
    c["RT"] = _bf(RT)
    c["ident"] = _bf(np.eye(128, dtype=f32))
    c["ones"] = _bf(np.ones((128, 128), f32))
    n = np.arange(256)
    cs = n * 16
    ss = np.arange(64) * 64
    ov = np.minimum(cs[:, None] + 32, ss[None, :] + 64) - np.maximum(cs[:, None], ss[None, :])
    msel = (np.clip(ov, 0, None) / 32.0).astype(f32)
    msel[255] = 0.0
    c["msel"] = _bf(msel.reshape(2, 128, 64).transpose(1, 0, 2))
    cm = np.where((n[:, None] * 16 + 31 <= pos[None, :]) & (n[:, None] < 255), 0.0, NEG).astype(f32)
    c["cmpmask"] = _bf(cm.reshape(2, 128, NT).transpose(1, 0, 2))
    tb = pos // 64
    m = np.arange(64)
    valid = m[None, :] <= tb[:, None]
    forced = (m[None, :] == 0) | (m[None, :] == tb[:, None]) | (m[None, :] == tb[:, None] - 1)
    fb = np.where(valid, np.where(forced, 1e4, 0.0), -1e4).astype(f32)
    c["fb"] = np.ascontiguousarray(fb.reshape(16, 128, 64).transpose(1, 0, 2))
    ex = np.zeros((64, 32, 128), f32)
    for kt in range(32):
        ex[2 * kt, kt, 0:64] = 1.0
        ex[2 * kt + 1, kt, 64:128] = 1.0
    c["expand"] = _bf(ex)
    k = np.arange(128)
    causal = np.where(k[:, None] <= k[None, :], 0.0, NEG).astype(f32)
    allm = np.full((128, 128), NEG, f32)
    zero = np.zeros((128, 128), f32)
    anti = np.where(k[:, None] > k[None, :], 0.0, NEG).astype(f32)
    if p == 0:
        cms = [causal, allm]
        wm = [anti, zero, zero, zero, causal, allm]
    else:
        cms = [zero, causal]
        wm = [allm, anti, zero, zero, zero, causal]
    c["cmaskS"] = _bf(np.stack(cms, 1))
    c["wmask"] = _bf(np.stack(wm, 1))
    fl = np.zeros((128, 2), f32)
    fl[:, p] = 1.0
    c["flags"] = fl
    return c


def kernel(**inputs):
    f32 = np.float32
    x = np.asarray(inputs["x"], f32)

    def gT(a):
        a = np.asarray(a, f32)
        return np.ascontiguousarray(np.swapaxes(a.reshape(a.shape[:-1] + (16, 128)), -1, -2))

    shared = {
        "w_in": np.asarray(inputs["w_in"], f32),
        "cmp_w1_k": np.asarray(inputs["cmp_w1_k"], f32), "cmp_w2_k": np.asarray(inputs["cmp_w2_k"], f32),
        "cmp_pos_kT": np.ascontiguousarray(np.swapaxes(np.asarray(inputs["cmp_pos_k"], f32), 1, 2)),
        "cmp_w1_v": np.asarray(inputs["cmp_w1_v"], f32), "cmp_w2_v": np.asarray(inputs["cmp_w2_v"], f32),
        "cmp_pos_vT": np.ascontiguousarray(np.swapaxes(np.asarray(inputs["cmp_pos_v"], f32), 1, 2)),
        "conv_wT": np.ascontiguousarray(np.asarray(inputs["conv_w"], f32).reshape(DEPTH, 3, 16, 128).transpose(0, 3, 2, 1)),
        "w_attn_proj": np.asarray(inputs["w_attn_proj"], f32), "w_conv_out": np.asarray(inputs["w_conv_out"], f32),
        "w_o": np.asarray(inputs["w_o"], f32), "w_up": np.asarray(inputs["w_up"], f32), "w_down": np.asarray(inputs["w_down"], f32),
        "norm1_gT": gT(inputs["norm1_g"]), "norm2_gT": gT(inputs["norm2_g"]), "final_gT": gT(inputs["final_g"]),
    }
    consts = [_consts(0), _consts(1)]
    in_maps = []
    for c in range(8):
        b, p = c // 2, c % 2
        xo = x[b].reshape(16, 2, 128, D)[:, p].reshape(NT, D)
        m = dict(shared)
        m["xT"] = np.ascontiguousarray(xo.T.reshape(16, 128, NT))
        for k, v in consts[p].items():
            m["c_" + k] = v
        in_maps.append(m)
    nc = build()
    res = run_bass_kernel_spmd(nc, in_maps, core_ids=list(range(8)))
    out = np.empty((4, S, D), f32)
    for c in range(8):
        b, p = c // 2, c % 2
        o = np.asarray(res.results[c]["outT"], f32).reshape(D, NT).T
        out[b].reshape(16, 2, 128, D)[:, p] = o.reshape(16, 128, D)
    return out
```
